# Optimizing a Trainium2 kernel written in Bass

```python
import math
import jax, jax.numpy as jnp
from jax import lax
import numpy as np

D_MODEL = 2048
BATCH = 4
SEQ = 4096
DEPTH = 4

GRID_W = 64
N_MIXERS = 2
BLOCK_Q = 128
CHUNK = 128
ROPE_THETA = 10000.0
RMS_EPS = 1e-6
LN_EPS = 1e-5

A_HEADS = 16
A_KV_HEADS = 4
A_HEAD_DIM = D_MODEL // A_HEADS
A_GROUP = A_HEADS // A_KV_HEADS
A_WIDTH = A_HEADS * A_HEAD_DIM
A_KV_WIDTH = A_KV_HEADS * A_HEAD_DIM
A_IN = 2 * A_WIDTH + 2 * A_KV_WIDTH

R_HEADS = 8
R_QK_DIM = D_MODEL // R_HEADS
R_V_DIM = 2 * R_QK_DIM
R_QK_WIDTH = R_HEADS * R_QK_DIM
R_V_WIDTH = R_HEADS * R_V_DIM
R_IN = 2 * R_QK_WIDTH + 2 * R_V_WIDTH
R_DECAY_BASE = 5.0

N_A_LAYERS = (DEPTH + 1) // N_MIXERS
N_R_LAYERS = DEPTH // N_MIXERS
ALPHA = (2.0 * DEPTH) ** 0.25
BETA = (8.0 * DEPTH) ** -0.25

kernel_name = "hybrid_gqa_retention_deepnorm_encoder"


def axial_rope_tables(seq_len, head_dim):
    rows_n = seq_len // GRID_W
    row = jnp.repeat(jnp.arange(rows_n, dtype=jnp.float32), GRID_W)
    col = jnp.tile(jnp.arange(GRID_W, dtype=jnp.float32), rows_n)
    half = head_dim // 2
    inv_freq = ROPE_THETA ** (-jnp.arange(0, half, 2, dtype=jnp.float32) / half)
    ang_r = row[:, None] * inv_freq
    ang_c = col[:, None] * inv_freq
    ang = jnp.concatenate([ang_r, ang_r, ang_c, ang_c], axis=-1)
    return jnp.cos(ang), jnp.sin(ang)


def apply_rope(x, cos, sin):
    x32 = x.astype(jnp.float32)
    a, b, c, d = jnp.split(x32, 4, axis=-1)
    rot = jnp.concatenate([-b, a, -d, c], axis=-1)
    out = x32 * cos[None, :, None, :] + rot * sin[None, :, None, :]
    return out.astype(x.dtype)


def rms_norm(x, gain):
    x32 = x.astype(jnp.float32)
    y = x32 * lax.rsqrt(jnp.mean(x32 * x32, axis=-1, keepdims=True) + RMS_EPS)
    return (y * gain.astype(jnp.float32)).astype(x.dtype)


def layer_norm(x, gain, bias):
    x32 = x.astype(jnp.float32)
    mu = jnp.mean(x32, axis=-1, keepdims=True)
    var = jnp.mean(jnp.square(x32 - mu), axis=-1, keepdims=True)
    y = (x32 - mu) * lax.rsqrt(var + LN_EPS)
    return (y * gain.astype(jnp.float32) + bias.astype(jnp.float32)).astype(x.dtype)


def head_group_norm(x):
    x32 = x.astype(jnp.float32)
    mu = jnp.mean(x32, axis=-1, keepdims=True)
    var = jnp.mean(jnp.square(x32 - mu), axis=-1, keepdims=True)
    return ((x32 - mu) * lax.rsqrt(var + LN_EPS)).astype(x.dtype)


def attention_mixer(h, w_in, q_gain, k_gain, w_out, cos, sin):
    B, S, _ = h.shape
    proj = h @ w_in
    q, k, v, g = jnp.split(proj, [A_WIDTH, A_WIDTH + A_KV_WIDTH, A_WIDTH + 2 * A_KV_WIDTH], axis=-1)
    q = q.reshape(B, S, A_HEADS, A_HEAD_DIM)
    k = k.reshape(B, S, A_KV_HEADS, A_HEAD_DIM)
    v = v.reshape(B, S, A_KV_HEADS, A_HEAD_DIM)
    q = apply_rope(rms_norm(q, q_gain), cos, sin) * (A_HEAD_DIM ** -0.5)
    k = apply_rope(rms_norm(k, k_gain), cos, sin)
    n_blocks = S // BLOCK_Q
    qb_all = q.reshape(B, n_blocks, BLOCK_Q, A_KV_HEADS, A_GROUP, A_HEAD_DIM).transpose(1, 0, 3, 4, 2, 5)

    def block(qb):
        s = jnp.einsum('bkgqd,bskd->bkgqs', qb, k).astype(jnp.float32)
        p = jax.nn.softmax(s, axis=-1).astype(v.dtype)
        return jnp.einsum('bkgqs,bskd->bkgqd', p, v)

    o = lax.map(block, qb_all)
    o = o.transpose(1, 0, 4, 2, 3, 5).reshape(B, S, A_WIDTH)
    return (o * jax.nn.silu(g)) @ w_out


def chunk_retention(q, k, v, log_g, strict):
    B, S, H, dk = q.shape
    dv = v.shape[-1]
    nc = S // CHUNK

    def to_chunks(t):
        return t.reshape(B, nc, CHUNK, H, t.shape[-1]).transpose(1, 0, 3, 2, 4)

    qc, kc, vc = to_chunks(q), to_chunks(k), to_chunks(v)
    log_g = log_g.astype(jnp.float32)
    pos = jnp.arange(CHUNK, dtype=jnp.float32)
    diff = pos[:, None] - pos[None, :]
    mask = (diff > 0) if strict else (diff >= 0)
    intra_decay = jnp.where(mask[None], jnp.exp(log_g[:, None, None] * jnp.maximum(diff, 0.0)[None]), 0.0)
    q_decay = jnp.exp(log_g[:, None] * (pos + 1.0))[..., None]
    k_decay = jnp.exp(log_g[:, None] * (CHUNK - 1.0 - pos))[..., None]
    chunk_decay = jnp.exp(log_g * CHUNK)[:, None, None]

    def step(state, xs):
        qb, kb, vb = xs
        q32, k32, v32 = qb.astype(jnp.float32), kb.astype(jnp.float32), vb.astype(jnp.float32)
        s = jnp.einsum('bhqd,bhkd->bhqk', q32, k32) * intra_decay
        inner = jnp.einsum('bhqk,bhkv->bhqv', s, v32)
        cross = jnp.einsum('bhqd,bhdv->bhqv', q32, state) * q_decay
        new_state = state * chunk_decay + jnp.einsum('bhkd,bhkv->bhdv', k32 * k_decay, v32)
        return new_state, (inner + cross).astype(vb.dtype)

    state0 = jnp.zeros((B, H, dk, dv), jnp.float32)
    _, out = lax.scan(step, state0, (qc, kc, vc))
    return out.transpose(1, 0, 3, 2, 4).reshape(B, S, H, dv)


def retention_mixer(h, w_in, decay_exp, w_out, cos, sin):
    B, S, _ = h.shape
    proj = h @ w_in
    q, k, v, g = jnp.split(proj, [R_QK_WIDTH, 2 * R_QK_WIDTH, 2 * R_QK_WIDTH + R_V_WIDTH], axis=-1)
    q = apply_rope(q.reshape(B, S, R_HEADS, R_QK_DIM), cos, sin)
    k = apply_rope(k.reshape(B, S, R_HEADS, R_QK_DIM), cos, sin) * (R_QK_DIM ** -0.5)
    v = v.reshape(B, S, R_HEADS, R_V_DIM)
    log_g = jnp.log1p(-jnp.exp2(-decay_exp.astype(jnp.float32)))
    fwd = chunk_retention(q, k, v, log_g[0], strict=False)
    bwd = jnp.flip(chunk_retention(jnp.flip(q, 1), jnp.flip(k, 1), jnp.flip(v, 1), log_g[1], strict=True), 1)
    o = head_group_norm(fwd + bwd).reshape(B, S, R_V_WIDTH)
    return (o * jax.nn.silu(g)) @ w_out


def setup_inputs(seed: int = 0) -> dict:
    key = jax.random.key(seed)
    ks = jax.random.split(key, 12)
    f32 = jnp.float32
    x = jax.random.normal(ks[0], (BATCH, SEQ, D_MODEL), f32)

    a_col_scale = jnp.concatenate([jnp.ones((A_WIDTH + A_KV_WIDTH,), f32),
                                   jnp.full((A_KV_WIDTH,), BETA, f32),
                                   jnp.ones((A_WIDTH,), f32)])
    attn_w_in = jax.random.normal(ks[1], (N_A_LAYERS, D_MODEL, A_IN), f32) * (D_MODEL ** -0.5) * a_col_scale
    attn_q_gain = 1.0 + 0.02 * jax.random.normal(ks[2], (N_A_LAYERS, A_HEAD_DIM), f32)
    attn_k_gain = 1.0 + 0.02 * jax.random.normal(ks[3], (N_A_LAYERS, A_HEAD_DIM), f32)
    attn_w_out = jax.random.normal(ks[4], (N_A_LAYERS, A_WIDTH, D_MODEL), f32) * (A_WIDTH ** -0.5) * BETA

    r_col_scale = jnp.concatenate([jnp.ones((2 * R_QK_WIDTH,), f32),
                                   jnp.full((R_V_WIDTH,), BETA, f32),
                                   jnp.ones((R_V_WIDTH,), f32)])
    ret_w_in = jax.random.normal(ks[5], (N_R_LAYERS, D_MODEL, R_IN), f32) * (D_MODEL ** -0.5) * r_col_scale
    base = R_DECAY_BASE + jnp.arange(R_HEADS, dtype=f32)
    base2 = jnp.stack([base, base[::-1]], axis=0)
    ret_decay_exp = base2[None] + 0.1 * jax.random.normal(ks[6], (N_R_LAYERS, 2, R_HEADS), f32)
    ret_w_out = jax.random.normal(ks[7], (N_R_LAYERS, R_V_WIDTH, D_MODEL), f32) * (R_V_WIDTH ** -0.5) * BETA

    ln_gain = 1.0 + 0.02 * jax.random.normal(ks[8], (DEPTH, D_MODEL), f32)
    ln_bias = 0.02 * jax.random.normal(ks[9], (DEPTH, D_MODEL), f32)
    return {"x": x, "attn_w_in": attn_w_in, "attn_q_gain": attn_q_gain, "attn_k_gain": attn_k_gain,
            "attn_w_out": attn_w_out, "ret_w_in": ret_w_in, "ret_decay_exp": ret_decay_exp,
            "ret_w_out": ret_w_out, "ln_gain": ln_gain, "ln_bias": ln_bias}


def reference(x, attn_w_in, attn_q_gain, attn_k_gain, attn_w_out, ret_w_in, ret_decay_exp,
              ret_w_out, ln_gain, ln_bias):
    S = x.shape[1]
    cos_a, sin_a = axial_rope_tables(S, A_HEAD_DIM)
    cos_r, sin_r = axial_rope_tables(S, R_QK_DIM)
    for i in range(DEPTH):
        j = i // N_MIXERS
        if i % N_MIXERS == 0:
            y = attention_mixer(x, attn_w_in[j], attn_q_gain[j], attn_k_gain[j], attn_w_out[j], cos_a, sin_a)
        else:
            y = retention_mixer(x, ret_w_in[j], ret_decay_exp[j], ret_w_out[j], cos_r, sin_r)
        x = layer_norm(ALPHA * x + y, ln_gain[i], ln_bias[i])
    return x
```

```python
import math
from contextlib import ExitStack
import numpy as np
import concourse.bass as bass
import concourse.mybir as mybir
from concourse.bass_utils import run_bass_kernel_spmd

F32 = mybir.dt.float32
BF16 = mybir.dt.bfloat16
AF = mybir.ActivationFunctionType
ALU = mybir.AluOpType
AX = mybir.AxisListType

D = 2048
GRID_W = 64
ROPE_THETA = 10000.0
RMS_EPS = 1e-6
LN_EPS = 1e-5
DEPTH = 4
ALPHA = (2.0 * DEPTH) ** 0.25
A_IN = 5120
R_IN = 12288
NCORES = 4


class Buf:
    __slots__ = ("name", "w", "r")

    def __init__(self, name):
        self.name = name
        self.w = None
        self.r = []


class Sched:
    ENG = ("pe", "act", "dve", "pool", "sp")
    NSLOT = 6

    def __init__(self):
        self.streams = {e: [] for e in self.ENG}
        self.cnt = {e: 0 for e in self.ENG}
        self.waited = {e: {} for e in self.ENG}
        self.dma_n = {e: 0 for e in self.ENG}
        self.dma_tok = {e: [None] * self.NSLOT for e in self.ENG}
        self.dma_cnt = {e: [0] * self.NSLOT for e in self.ENG}
        self.nops = 0

    def _need(self, eng, tok, waits):
        if tok is None:
            return
        key, val = tok
        if key == eng and eng in ("pe", "sp"):
            return
        if self.waited[eng].get(key, 0) >= val:
            return
        self.waited[eng][key] = val
        waits.append((key, val))

    def op(self, eng, emit, reads=(), writes=(), dma=False):
        waits = []
        for b in reads:
            self._need(eng, b.w, waits)
        for b in writes:
            self._need(eng, b.w, waits)
            for t in b.r:
                if t[0] == eng and not dma:
                    continue
                self._need(eng, t, waits)
        if dma:
            n = self.dma_n[eng]
            self.dma_n[eng] = n + 1
            slot = n % self.NSLOT
            self._need(eng, self.dma_tok[eng][slot], waits)
            self.dma_cnt[eng][slot] += 16
            tok = (("dma", eng, slot), self.dma_cnt[eng][slot])
            self.dma_tok[eng][slot] = tok
            sig = (tok[0], 16)
        else:
            self.cnt[eng] += 1
            tok = (eng, self.cnt[eng])
            sig = (eng, 1)
        for b in reads:
            b.r.append(tok)
        for b in writes:
            b.w = tok
            b.r = []
        self.streams[eng].append((waits, emit, sig))
        self.nops += 1
        return tok

    def fence(self):
        toks = [(e, self.cnt[e]) for e in ("pe", "act", "dve", "pool") if self.cnt[e] > 0]
        for e in self.ENG:
            for slot in range(self.NSLOT):
                if self.dma_tok[e][slot] is not None:
                    toks.append(self.dma_tok[e][slot])
        for e in self.ENG:
            waits = []
            for t in toks:
                self._need(e, t, waits)
            if waits:
                self.streams[e].append((waits, None, None))

    def finish(self):
        for eng in self.ENG:
            waits = []
            for slot in range(self.NSLOT):
                self._need(eng, self.dma_tok[eng][slot], waits)
            if waits:
                self.streams[eng].append((waits, None, None))

    def emit_all(self, nc, stack):
        sems = {}

        def sem(key):
            if key not in sems:
                nm = "s_" + ("_".join(str(k) for k in key) if isinstance(key, tuple) else key)
                sems[key] = stack.enter_context(nc.semaphore(nm))
            return sems[key]

        for e in self.ENG:
            sem(e)
            for s in range(self.NSLOT):
                if self.dma_cnt[e][s]:
                    sem(("dma", e, s))
        block = stack.enter_context(nc.Block())
        sect = {"pe": block.tensor, "act": block.scalar, "dve": block.vector, "pool": block.gpsimd, "sp": block.sync}

        def mk(stream):
            def body(engine):
                for waits, emit, sig in stream:
                    for key, val in waits:
                        engine.wait_ge(sems[key], val)
                    if emit is not None:
                        ins = emit(engine)
                        ins.then_inc(sems[sig[0]], sig[1])
            return body

        for e in self.ENG:
            if self.streams[e]:
                sect[e](mk(self.streams[e]))


def rope_tables(S, hd):
    rows_n = S // GRID_W
    row = np.repeat(np.arange(rows_n, dtype=np.float32), GRID_W)
    col = np.tile(np.arange(GRID_W, dtype=np.float32), rows_n)
    half = hd // 2
    inv_freq = (np.float32(ROPE_THETA) ** (-np.arange(0, half, 2, dtype=np.float32) / np.float32(half))).astype(np.float32)
    ang_r = row[:, None] * inv_freq
    ang_c = col[:, None] * inv_freq
    ang = np.concatenate([ang_r, ang_r, ang_c, ang_c], axis=-1).astype(np.float32)
    cos = np.cos(ang).astype(np.float32)
    sin = np.sin(ang).astype(np.float32)
    q4 = hd // 4
    sgn = np.concatenate([-np.ones(q4), np.ones(q4), -np.ones(q4), np.ones(q4)]).astype(np.float32)
    return cos, (sin * sgn[None, :]).astype(np.float32)


def build_program(S, layers):
    NT = S // 128
    n_a = sum(1 for l in layers if l == "a")
    n_r = sum(1 for l in layers if l == "r")
    nL = len(layers)
    nc = bass.Bass("TRN2", target_bir_lowering=False)
    sc = Sched()

    def din(name, shape, dt=F32):
        return nc.dram_tensor(name, list(shape), dt, kind="ExternalInput").ap()

    def dint(name, shape, dt):
        return nc.dram_tensor(name, list(shape), dt, kind="Internal").ap()

    x_in = din("x", [S, D])
    y_out = nc.dram_tensor("y", [S, D], F32, kind="ExternalOutput").ap()
    a_win = din("a_win", [max(n_a, 1), D, A_IN])
    a_wout = din("a_wout", [max(n_a, 1), D, D])
    a_qg = din("a_qg", [max(n_a, 1), 128])
    a_kg = din("a_kg", [max(n_a, 1), 128])
    r_win = din("r_win", [max(n_r, 1), D, R_IN])
    r_wout = din("r_wout", [max(n_r, 1), 2 * D, D])
    r_dec = din("r_dec", [max(n_r, 1), 16])
    ln_g = din("ln_g", [nL, D])
    ln_b = din("ln_b", [nL, D])
    cosa_d = din("cosa", [S, 128])
    sina_d = din("sina", [S, 128])
    cosr_d = din("cosr", [S, 256])
    sinr_d = din("sinr", [S, 256])
    identf_d = din("identf_d", [128, 128])
    p1_d = din("p1_d", [128, 128])
    p2_d = din("p2_d", [128, 128])
    posv_d = din("posv_d", [128, 4])
    prow_d = din("prow_d", [128, 256])

    a_win_b = dint("a_win_b", [max(n_a, 1), D, A_IN], BF16)
    a_wout_b = dint("a_wout_b", [max(n_a, 1), D, D], BF16)
    r_win_b = dint("r_win_b", [max(n_r, 1), D, R_IN], BF16)
    r_wout_b = dint("r_wout_b", [max(n_r, 1), 2 * D, D], BF16)
    xs = [dint("xs0", [S, D], F32), dint("xs1", [S, D], F32)]
    if n_r:
        r_qT = dint("r_qT", [NT, 128, 2 * 8 * 128], BF16)
        r_k = dint("r_k", [S, 2048], BF16)
        r_v = dint("r_v", [S, 4096], BF16)
        r_g = dint("r_g", [S, 4096], BF16)
        r_o1 = dint("r_o1", [S, 4096], F32)

    stack = ExitStack()
    DB = {}

    def dbuf(ap):
        k = ap.name if hasattr(ap, "name") else id(ap)
        if k not in DB:
            DB[k] = Buf(str(k))
        return DB[k]

    NA = 29696
    arena = stack.enter_context(nc.sbuf_tensor("arena", [128, NA], F32))
    aoff = [0]

    class T:
        def __init__(self, name, shape, dt, psum=False, persist=False, view=None, buf=None):
            if view is not None:
                self.t = view
                self.b = buf
                return
            shape = list(shape)
            if psum:
                self.t = stack.enter_context(nc.psum_tensor(name, shape, dt))
            elif persist:
                self.t = stack.enter_context(nc.sbuf_tensor(name, shape, dt))
            else:
                n = 1
                for d_ in shape[1:]:
                    n *= d_
                nw = n if dt == F32 else (n + 1) // 2
                nw = (nw + 7) // 8 * 8
                assert aoff[0] + nw <= NA, ("arena overflow", name, aoff[0], nw)
                v = arena[:, aoff[0]:aoff[0] + nw]
                aoff[0] += nw
                if dt != F32:
                    v = v.bitcast(dt)
                v = v[:, 0:n]
                if len(shape) == 3:
                    v = v.rearrange("p (a b) -> p a b", b=shape[2])
                elif len(shape) == 4:
                    v = v.rearrange("p (a b c) -> p a b c", b=shape[2], c=shape[3])
                self.t = v
            self.b = Buf(name)

        def __getitem__(self, k):
            return self.t[k]

    def new_phase():
        sc.fence()
        aoff[0] = 0

    def op(eng, fn, reads=(), writes=(), dma=False):
        rb = [x.b if isinstance(x, T) else x for x in reads]
        wb = [x.b if isinstance(x, T) else x for x in writes]
        return sc.op(eng, fn, rb, wb, dma)

    def dma(q, out, in_, reads, writes):
        op(q, lambda e: e.dma_start(out=out, in_=in_), reads, writes, dma=True)

    identf = T("identf", [128, 128], F32, persist=True)
    identb = T("identb", [128, 128], BF16, persist=True)
    onesb = T("onesb", [128, 128], BF16, persist=True)
    dma("sp", identf[:], identf_d[:, :], [], [identf])
    op("dve", lambda e: e.tensor_copy(out=identb[:], in_=identf[:]), [identf], [identb])
    op("dve", lambda e: e.memset(onesb[:], 1.0), [], [onesb])

    def cast_w(src, dst, rows, cols):
        for r0 in range(0, rows, 128):
            s = src[r0:r0 + 128, :].rearrange("r (a b) -> r a b", b=1024)
            d_ = dst[r0:r0 + 128, :].rearrange("r (a b) -> r a b", b=1024)
            dma("pool", d_, s, [], [dbuf(dst)])

    psA = [T("psA0", [128, 512], F32, True), T("psA1", [128, 512], F32, True)]
    psB = [T("psB0", [128, 512], F32, True), T("psB1", [128, 512], F32, True)]
    psT = [T("psT0", [128, 1024], BF16, True), T("psT1", [128, 1024], BF16, True)]
    psTf = [T(None, None, None, view=psT[i].t[:].bitcast(F32), buf=psT[i].b) for i in range(2)]
    psO = T("psO", [128, 512], F32, True)
    psL = T("psL", [128, 512], F32, True)

    wbuf = [T("wbuf0", [128, 16, 512], BF16, persist=True), T("wbuf1", [128, 16, 512], BF16, persist=True)]
    wctr = [0]
    xstg = [T("xstg0", [128, 512], F32, persist=True), T("xstg1", [128, 512], F32, persist=True)]
    xctr = [0]
    lng = [T("lng0", [128, 512], F32, persist=True), T("lng1", [128, 512], F32, persist=True)]
    lnb = [T("lnb0", [128, 512], F32, persist=True), T("lnb1", [128, 512], F32, persist=True)]
    small = {}

    def sm(name, shape, dt=F32):
        if name not in small:
            small[name] = T(name, shape, dt, persist=True)
        return small[name]

    evac_ctr = [0]

    def evac(out, in_, reads, writes):
        evac_ctr[0] += 1
        if evac_ctr[0] % 2:
            op("act", lambda e: e.activation(out=out, in_=in_, func=AF.Copy), reads, writes)
        else:
            op("dve", lambda e: e.tensor_copy(out=out, in_=in_), reads, writes)

    def load_w(src2d, c0, ncol, nk):
        w = wbuf[wctr[0] % 2]
        wctr[0] += 1
        assert nk * ncol <= 16 * 512
        view = w.t[:].rearrange("p a b -> p (a b)")[:, 0:nk * ncol].rearrange("p (a b) -> p a b", b=ncol)
        dma("sp", view, src2d[0:nk * 128, c0:c0 + ncol].rearrange("(kc p) c -> p kc c", p=128), [dbuf(src2d)], [w])

        return T(None, None, None, view=view, buf=w.b)

    def load_xT(xsrc, t0, ntile, xT):
        for j in range(ntile):
            tok0 = (t0 + j) * 128
            for c4 in range(4):
                st = xstg[xctr[0] % 2]
                xctr[0] += 1
                dma("sp", st[:], xsrc[tok0:tok0 + 128, c4 * 512:(c4 + 1) * 512], [dbuf(xsrc)], [st])
                ps = psB[c4 % 2]

                def f(e, ps=ps, st=st):
                    for i in range(4):
                        ins = e.transpose(out=ps[:, i * 128:(i + 1) * 128], in_=st[:, i * 128:(i + 1) * 128], identity=identf[:])
                    return ins
                op("pe", f, [st, identf], [ps])
                evac(xT[:, c4 * 4:(c4 + 1) * 4, j * 128:(j + 1) * 128],
                     ps[:].rearrange("p (a b) -> p a b", b=128), [ps], [xT])

    def proj_tok(ps, xT, j, w, nk=16, ncol=512):
        def f(e):
            for kc in range(nk):
                ins = e.matmul(ps[:, 0:ncol], lhsT=xT[:, kc, j * 128:(j + 1) * 128], rhs=w[:, kc, 0:ncol],
                               start=(kc == 0), stop=(kc == nk - 1))
            return ins
        op("pe", f, [xT, w], [ps])

    def layer_norm_out(li, rbuf, j, tok0, xdst):
        stats = sm("ln_stats", [128, 4, 6])
        mv = sm("ln_mv", [128, 2])
        rstd = sm("ln_rstd", [128, 1])
        nmr = sm("ln_nmr", [128, 1])
        for c in range(4):
            op("dve", lambda e, c=c: e.bn_stats(out=stats[:, c, :], in_=rbuf[:, j, c * 512:(c + 1) * 512]), [rbuf], [stats])
        op("dve", lambda e: e.bn_aggr(out=mv[:], in_=stats[:].rearrange("p a b -> p (a b)")), [stats], [mv])
        op("dve", lambda e: e.tensor_scalar(out=rstd[:], in0=mv[:, 1:2], scalar1=LN_EPS, scalar2=None, op0=ALU.add), [mv], [rstd])
        op("act", lambda e: e.activation(out=rstd[:], in_=rstd[:], func=AF.Sqrt), [rstd], [rstd])
        op("dve", lambda e: e.reciprocal(out=rstd[:], in_=rstd[:]), [rstd], [rstd])
        op("dve", lambda e: e.scalar_tensor_tensor(out=nmr[:], in0=mv[:, 0:1], scalar=-1.0, in1=rstd[:], op0=ALU.mult, op1=ALU.mult),
           [mv, rstd], [nmr])
        op("act", lambda e: e.activation(out=rbuf[:, j, :], in_=rbuf[:, j, :], func=AF.Identity, bias=nmr[:, 0:1], scale=rstd[:, 0:1]),
           [rbuf, nmr, rstd], [rbuf])
        for c in range(4):
            g_ = lng[c % 2]
            b_ = lnb[c % 2]
            dma("sp", g_[:], ln_g[li:li + 1, c * 512:(c + 1) * 512].partition_broadcast(128), [], [g_])
            dma("sp", b_[:], ln_b[li:li + 1, c * 512:(c + 1) * 512].partition_broadcast(128), [], [b_])
            op("pool", lambda e, c=c, g_=g_: e.tensor_tensor(out=rbuf[:, j, c * 512:(c + 1) * 512], in0=rbuf[:, j, c * 512:(c + 1) * 512],
                                                             in1=g_[:], op=ALU.mult), [rbuf, g_], [rbuf])
            op("dve", lambda e, c=c, b_=b_: e.tensor_tensor(out=rbuf[:, j, c * 512:(c + 1) * 512], in0=rbuf[:, j, c * 512:(c + 1) * 512],
                                                            in1=b_[:], op=ALU.add), [rbuf, b_], [rbuf])
        dma("pool", xdst[tok0:tok0 + 128, :], rbuf[:, j, :], [rbuf], [dbuf(xdst)])

    def attention_layer(li, ja, xsrc, xdst):
        new_phase()
        SBT = 2
        NSB = NT // SBT
        win = a_win_b[ja]
        wout = a_wout_b[ja]
        KT = T("KT", [128, 4, S], BF16)
        V = T("V", [128, NT, 512], BF16)
        xT = T("xT", [128, 16, SBT * 128], BF16)
        qT = T("qT", [128, 16, SBT * 128], BF16)
        gT = T("gT", [128, 16, SBT * 128], BF16)
        rbuf = T("rbuf", [128, SBT, D], F32)
        PT = [T("PT%d" % i, [128, 512], BF16) for i in range(3)]
        cosb = [sm("cosb0", [128, 128]), sm("cosb1", [128, 128])]
        sinb = [sm("sinb0", [128, 128]), sm("sinb1", [128, 128])]
        gq = sm("gq_rep", [128, 128])
        gk = sm("gk_rep", [128, 128])
        negb = sm("negb", [128, 1])
        mq = sm("mq", [128, 1])
        mk_ = sm("mk", [128, 1])
        sq = sm("sq", [128, 512])
        ss = sm("ss", [128, 4])
        rs = sm("rs", [128, 4])
        qg = sm("qg", [128, 512])
        t1 = sm("t1", [128, 512])
        t2 = sm("t2", [128, 512])
        qb = sm("qb", [128, 512], BF16)
        rl = sm("rl", [128, 512])
        tmpo = sm("tmpo", [128, 512])

        dma("sp", gq[:], a_qg[ja:ja + 1, :].partition_broadcast(128), [], [gq])
        dma("sp", gk[:], a_kg[ja:ja + 1, :].partition_broadcast(128), [], [gk])
        op("dve", lambda e: e.tensor_reduce(out=mq[:], in_=gq[:], axis=AX.X, op=ALU.max, apply_absolute_value=True), [gq], [mq])
        op("dve", lambda e: e.tensor_reduce(out=mk_[:], in_=gk[:], axis=AX.X, op=ALU.max, apply_absolute_value=True), [gk], [mk_])
        op("dve", lambda e: e.scalar_tensor_tensor(out=negb[:], in0=mq[:], scalar=-math.sqrt(128.0), in1=mk_[:], op0=ALU.mult, op1=ALU.mult),
           [mq, mk_], [negb])

        def qk_post(ps, gain, scale, tok0, cs):
            cb_, sb_ = cosb[cs % 2], sinb[cs % 2]
            op("act", lambda e: e.activation(out=sq[:], in_=ps[:], func=AF.Square), [ps], [sq])
            op("dve", lambda e: e.tensor_reduce(out=ss[:], in_=sq[:].rearrange("p (h d) -> p h d", d=128), axis=AX.X, op=ALU.add), [sq], [ss])
            op("dve", lambda e: e.tensor_scalar(out=rs[:], in0=ss[:], scalar1=1.0 / (128.0 * scale * scale), scalar2=RMS_EPS / (scale * scale),
                                                op0=ALU.mult, op1=ALU.add), [ss], [rs])
            op("act", lambda e: e.activation(out=rs[:], in_=rs[:], func=AF.Sqrt), [rs], [rs])
            op("dve", lambda e: e.reciprocal(out=rs[:], in_=rs[:]), [rs], [rs])
            op("dve", lambda e: e.tensor_tensor(out=qg[:].rearrange("p (h d) -> p h d", d=128), in0=ps[:].rearrange("p (h d) -> p h d", d=128),
                                                in1=gain[:].unsqueeze(1).to_broadcast([128, 4, 128]), op=ALU.mult), [ps, gain], [qg])
            op("dve", lambda e: e.tensor_tensor(out=t1[:].rearrange("p (h d) -> p h d", d=128), in0=qg[:].rearrange("p (h d) -> p h d", d=128),
                                                in1=cb_[:].unsqueeze(1).to_broadcast([128, 4, 128]), op=ALU.mult), [qg, cb_], [t1])
            qv = qg[:].rearrange("p (g s d) -> p g s d", s=2, d=32)
            tv = t2[:].rearrange("p (g s d) -> p g s d", s=2, d=32)
            sv = sb_[:].rearrange("p (g s d) -> p g s d", s=2, d=32)
            qv5 = qg[:].rearrange("p (h g s d) -> p h g s d", h=4, g=2, s=2, d=32)
            tv5 = t2[:].rearrange("p (h g s d) -> p h g s d", h=4, g=2, s=2, d=32)
            for s_ in range(2):
                op("dve", lambda e, s_=s_: e.tensor_tensor(out=tv5[:, :, :, s_, :], in0=qv5[:, :, :, 1 - s_, :],
                                                            in1=sv[:, :, s_, :].unsqueeze(1).to_broadcast([128, 4, 2, 32]), op=ALU.mult), [qg, sb_], [t2])
            op("dve", lambda e: e.tensor_tensor(out=t1[:], in0=t1[:], in1=t2[:], op=ALU.add), [t1, t2], [t1])
            op("dve", lambda e: e.tensor_tensor(out=qb[:].rearrange("p (h d) -> p h d", d=128), in0=t1[:].rearrange("p (h d) -> p h d", d=128),
                                                in1=rs[:].unsqueeze(2).to_broadcast([128, 4, 128]), op=ALU.mult), [t1, rs], [qb])

        def load_tables(tok0, cs):
            dma("sp", cosb[cs % 2][:], cosa_d[tok0:tok0 + 128, :], [], [cosb[cs % 2]])
            dma("sp", sinb[cs % 2][:], sina_d[tok0:tok0 + 128, :], [], [sinb[cs % 2]])

        def transpose4(dst3, pst):
            def f(e):
                for h in range(4):
                    ins = e.transpose(out=pst[:, h * 128:(h + 1) * 128], in_=qb[:, h * 128:(h + 1) * 128], identity=identb[:])
                return ins
            op("pe", f, [qb, identb], [pst])

        cs = 0
        for sb in range(NSB):
            load_xT(xsrc, sb * SBT, SBT, xT)
            wk = load_w(win, 2048, 512, 16)
            wv = load_w(win, 2560, 512, 16)
            for j in range(SBT):
                t = sb * SBT + j
                tok0 = t * 128
                load_tables(tok0, cs)
                ps = psA[0]
                proj_tok(ps, xT, j, wk)
                qk_post(ps, gk, 1.0, tok0, cs)
                cs += 1
                pst = psT[t % 2]
                transpose4(None, pst)
                evac(KT[:, :, tok0:tok0 + 128], pst[:, 0:512].rearrange("p (h t) -> p h t", t=128), [pst], [KT])
                ps = psA[1]
                proj_tok(ps, xT, j, wv)
                evac(V[:, t, :], ps[:], [ps], [V])

        for sb in range(NSB):
            load_xT(xsrc, sb * SBT, SBT, xT)
            for cb in range(4):
                w = load_w(win, cb * 512, 512, 16)
                for j in range(SBT):
                    tok0 = (sb * SBT + j) * 128
                    load_tables(tok0, cs)
                    ps = psA[j % 2]
                    proj_tok(ps, xT, j, w)
                    qk_post(ps, gq, 128.0 ** -0.5, tok0, cs)
                    cs += 1
                    pst = psT[j % 2]
                    transpose4(None, pst)
                    evac(qT[:, cb * 4:(cb + 1) * 4, j * 128:(j + 1) * 128], pst[:, 0:512].rearrange("p (h t) -> p h t", t=128), [pst], [qT])
            for cb in range(4):
                w = load_w(win, 3072 + cb * 512, 512, 16)
                for h in range(4):
                    ps = psA[h % 2]

                    def f(e, ps=ps, w=w, h=h):
                        for kc in range(16):
                            ins = e.matmul(ps[:, 0:SBT * 128], lhsT=w[:, kc, h * 128:(h + 1) * 128], rhs=xT[:, kc, :],
                                           start=(kc == 0), stop=(kc == 15))
                        return ins
                    op("pe", f, [xT, w], [ps])
                    op("act", lambda e, ps=ps, cb=cb, h=h: e.activation(out=gT[:, cb * 4 + h, :], in_=ps[:, 0:SBT * 128], func=AF.Silu), [ps], [gT])
            for j in range(SBT):
                for g in range(4):
                    rhs_q = qT[:, g * 4:(g + 1) * 4, j * 128:(j + 1) * 128]

                    def s_mm(kt, g=g, rhs_q=rhs_q):
                        ps = psB[kt % 2]
                        op("pe", lambda e: e.matmul(ps[:], lhsT=KT[:, g, kt * 128:(kt + 1) * 128], rhs=rhs_q, start=True, stop=True), [KT, qT], [ps])
                        pt = PT[kt % 3]
                        op("act", lambda e: e.activation(out=pt[:], in_=ps[:], func=AF.Exp, bias=negb[:, 0:1], scale=1.0), [ps, negb], [pt])

                    def pv_mm(kt, g=g):
                        pt = PT[kt % 3]

                        def f(e):
                            e.matmul(psO[:], lhsT=V[:, kt, g * 128:(g + 1) * 128], rhs=pt[:], start=(kt == 0), stop=(kt == NT - 1))
                            return e.matmul(psL[:], lhsT=onesb[:], rhs=pt[:], start=(kt == 0), stop=(kt == NT - 1))
                        op("pe", f, [V, pt, onesb], [psO, psL])

                    for kt in range(NT):
                        s_mm(kt)
                        if kt >= 1:
                            pv_mm(kt - 1)
                    pv_mm(NT - 1)
                    op("dve", lambda e: e.reciprocal(out=rl[:], in_=psL[:]), [psL], [rl])
                    op("dve", lambda e: e.tensor_tensor(out=tmpo[:], in0=psO[:], in1=rl[:], op=ALU.mult), [psO, rl], [tmpo])
                    gview = gT[:, g * 4:(g + 1) * 4, j * 128:(j + 1) * 128]
                    op("dve", lambda e, gview=gview: e.tensor_tensor(out=gview, in0=gview, in1=tmpo[:].rearrange("p (h t) -> p h t", t=128), op=ALU.mult),
                       [gT, tmpo], [gT])
            for cb in range(4):
                w = load_w(wout, cb * 512, 512, 16)
                for j in range(SBT):
                    tok0 = (sb * SBT + j) * 128
                    ps = psA[j % 2]

                    def f(e, ps=ps, w=w, j=j):
                        for h in range(16):
                            ins = e.matmul(ps[:], lhsT=gT[:, h, j * 128:(j + 1) * 128], rhs=w[:, h, :], start=(h == 0), stop=(h == 15))
                        return ins
                    op("pe", f, [gT, w], [ps])
                    st = xstg[xctr[0] % 2]
                    xctr[0] += 1
                    dma("sp", st[:], xsrc[tok0:tok0 + 128, cb * 512:(cb + 1) * 512], [dbuf(xsrc)], [st])
                    op("dve", lambda e, ps=ps, st=st, j=j, cb=cb: e.scalar_tensor_tensor(out=rbuf[:, j, cb * 512:(cb + 1) * 512], in0=st[:], scalar=ALPHA,
                                                                                          in1=ps[:], op0=ALU.mult, op1=ALU.add), [st, ps], [rbuf])
            for j in range(SBT):
                layer_norm_out(li, rbuf, j, (sb * SBT + j) * 128, xdst)

    def retention_layer(li, jr, xsrc, xdst):
        SBT = 2
        NSB = NT // SBT
        win = r_win_b[jr]
        wout = r_wout_b[jr]
        LN2 = math.log(2.0)

        posv = sm("posv", [128, 4])
        p1 = sm("p1", [128, 128])
        p2 = sm("p2", [128, 128])
        dec = sm("dec_rep", [128, 16])
        u = sm("dec_u", [128, 16])
        tt = sm("dec_t", [128, 16])
        lg = sm("dec_lg", [128, 16])
        dqf = sm("dqf", [128, 8])
        dqb = sm("dqb", [128, 8])
        kdf = sm("kdf", [128, 8])
        kdb = sm("kdb", [128, 8])
        gc = sm("gc", [128, 16])
        arg = sm("dc_arg", [128, 128])
        dma("sp", posv[:], posv_d[:, :], [], [posv])
        dma("sp", p1[:], p1_d[:, :], [], [p1])
        dma("sp", p2[:], p2_d[:, :], [], [p2])
        dma("sp", dec[:], r_dec[jr:jr + 1, :].partition_broadcast(128), [], [dec])
        op("act", lambda e: e.activation(out=u[:], in_=dec[:], func=AF.Exp, scale=-LN2), [dec], [u])
        op("dve", lambda e: e.tensor_scalar(out=tt[:], in0=u[:], scalar1=0.25, scalar2=1.0 / 3.0, op0=ALU.mult, op1=ALU.add), [u], [tt])
        op("dve", lambda e: e.tensor_tensor(out=tt[:], in0=tt[:], in1=u[:], op=ALU.mult), [tt, u], [tt])
        op("dve", lambda e: e.tensor_scalar(out=tt[:], in0=tt[:], scalar1=0.5, scalar2=None, op0=ALU.add), [tt], [tt])
        op("dve", lambda e: e.tensor_tensor(out=tt[:], in0=tt[:], in1=u[:], op=ALU.mult), [tt, u], [tt])
        op("dve", lambda e: e.tensor_scalar(out=tt[:], in0=tt[:], scalar1=1.0, scalar2=None, op0=ALU.add), [tt], [tt])
        op("dve", lambda e: e.scalar_tensor_tensor(out=lg[:], in0=tt[:], scalar=-1.0, in1=u[:], op0=ALU.mult, op1=ALU.mult), [tt, u], [lg])
        op("act", lambda e: e.activation(out=dqf[:], in_=lg[:, 0:8], func=AF.Exp, scale=posv[:, 0:1]), [lg, posv], [dqf])
        op("act", lambda e: e.activation(out=dqb[:], in_=lg[:, 8:16], func=AF.Exp, scale=posv[:, 1:2]), [lg, posv], [dqb])
        op("act", lambda e: e.activation(out=kdf[:], in_=lg[:, 0:8], func=AF.Exp, scale=posv[:, 2:3]), [lg, posv], [kdf])
        op("act", lambda e: e.activation(out=kdb[:], in_=lg[:, 8:16], func=AF.Exp, scale=posv[:, 3:4]), [lg, posv], [kdb])
        op("act", lambda e: e.activation(out=gc[:], in_=lg[:], func=AF.Exp, scale=128.0), [lg], [gc])

        cosr = [sm("cosr0", [128, 256]), sm("cosr1", [128, 256])]
        sinr = [sm("sinr0", [128, 256]), sm("sinr1", [128, 256])]
        t1 = sm("t1", [128, 512])
        t2 = sm("t2", [128, 512])
        cs = [0]

        def rope256(ps, out_ap, outT, scale, tok0):
            c_ = cosr[cs[0] % 2]
            s_ = sinr[cs[0] % 2]
            cs[0] += 1
            dma("sp", c_[:], cosr_d[tok0:tok0 + 128, :], [], [c_])
            dma("sp", s_[:], sinr_d[tok0:tok0 + 128, :], [], [s_])
            op("dve", lambda e: e.tensor_tensor(out=t1[:].rearrange("p (h d) -> p h d", d=256), in0=ps[:].rearrange("p (h d) -> p h d", d=256),
                                                in1=c_[:].unsqueeze(1).to_broadcast([128, 2, 256]), op=ALU.mult), [ps, c_], [t1])
            pv5 = ps[:].rearrange("p (h g s d) -> p h g s d", h=2, g=2, s=2, d=64)
            tv5 = t2[:].rearrange("p (h g s d) -> p h g s d", h=2, g=2, s=2, d=64)
            sv = s_[:].rearrange("p (g s d) -> p g s d", s=2, d=64)
            for k_ in range(2):
                op("dve", lambda e, k_=k_: e.tensor_tensor(out=tv5[:, :, :, k_, :], in0=pv5[:, :, :, 1 - k_, :],
                                                           in1=sv[:, :, k_, :].unsqueeze(1).to_broadcast([128, 2, 2, 64]), op=ALU.mult), [ps, s_], [t2])
            if scale == 1.0:
                op("dve", lambda e: e.tensor_tensor(out=out_ap, in0=t1[:], in1=t2[:], op=ALU.add), [t1, t2], [outT])
            else:
                op("dve", lambda e: e.tensor_tensor(out=t1[:], in0=t1[:], in1=t2[:], op=ALU.add), [t1, t2], [t1])
                op("act", lambda e: e.activation(out=out_ap, in_=t1[:], func=AF.Copy, scale=scale), [t1], [outT])

        new_phase()
        prow = sm("prow", [128, 256])
        dma("sp", prow[:], prow_d[:, :], [], [prow])
        St32 = [T("St32_%d" % h, [128, 2, 512], F32) for h in range(8)]
        Stb = [T("Stb_%d" % h, [128, 2, 512], BF16) for h in range(8)]
        DcT = [T("DcT_%d" % h, [128, 128], F32) for h in range(8)]
        DqT = T("DqfT", [128, 8, 128], F32)
        xT = T("xT", [128, 16, SBT * 128], BF16)
        qTc = [T("qTc%d" % j, [128, 2, 8, 128], BF16) for j in range(SBT)]
        kTc = [T("kTc%d" % j, [128, 2, 8, 128], BF16) for j in range(SBT)]
        kc = [T("kc%d" % j, [128, 2048], BF16) for j in range(SBT)]
        vall = [T("vall%d" % j, [128, 4096], BF16) for j in range(SBT)]
        AT = [T("AT%d" % i, [128, 128], BF16) for i in range(2)]
        gh = [T("gh%d" % i, [128, 512], BF16) for i in range(2)]
        qb = gh[0]
        o1buf = [T("o1buf%d" % i, [128, 512], F32) for i in range(2)]
        arg2 = sm("dc_arg2", [128, 128])

        for h in range(8):
            op("dve", lambda e, h=h: e.tensor_scalar(out=arg[:], in0=p1[:], scalar1=lg[:, h:h + 1], scalar2=None, op0=ALU.mult), [p1, lg], [arg])
            op("dve", lambda e, h=h: e.scalar_tensor_tensor(out=arg[:], in0=p2[:], scalar=lg[:, 8 + h:9 + h], in1=arg[:], op0=ALU.mult, op1=ALU.add),
               [p2, lg, arg], [arg])
            op("dve", lambda e, h=h: e.tensor_scalar(out=arg2[:], in0=prow[:, 0:128], scalar1=lg[:, h:h + 1], scalar2=None, op0=ALU.mult), [prow, lg], [arg2])
            op("dve", lambda e: e.tensor_tensor(out=arg[:], in0=arg[:], in1=arg2[:], op=ALU.subtract), [arg, arg2], [arg])
            op("act", lambda e, h=h: e.activation(out=DcT[h][:], in_=arg[:], func=AF.Exp), [arg], [DcT[h]])
            op("act", lambda e, h=h: e.activation(out=DqT[:, h, :], in_=prow[:, 0:128], func=AF.Exp, scale=lg[:, h:h + 1]), [prow, lg], [DqT])
            op("pool", lambda e, h=h: e.memset(St32[h][:], 0.0), [], [St32[h]])
            op("pool", lambda e, h=h: e.memset(Stb[h][:], 0.0), [], [Stb[h]])

        def state_update(St32, Stb, psU, h, kall, vht, gcol):
            S32h = St32[h]
            Sbh = Stb[h]
            for dkc in range(2):
                pu = psU[dkc]
                op("pe", lambda e, dkc=dkc, pu=pu: e.matmul(pu[:], lhsT=kall[:, h * 256 + dkc * 128:h * 256 + (dkc + 1) * 128], rhs=vht, start=True, stop=True),
                   [kall, vht_T[0]], [pu])
                op("dve", lambda e, dkc=dkc, pu=pu: e.scalar_tensor_tensor(out=S32h[:, dkc, :], in0=S32h[:, dkc, :], scalar=gc[:, gcol:gcol + 1],
                                                                          in1=pu[:], op0=ALU.mult, op1=ALU.add), [S32h, gc, pu], [S32h])
            op("act", lambda e: e.activation(out=Sbh[:], in_=S32h[:], func=AF.Copy), [S32h], [Sbh])

        vht_T = [None]
        psX = [psO, psL]
        sctr = [0]
        for sb in range(NSB):
            load_xT(xsrc, sb * SBT, SBT, xT)
            for kind in range(2):
                for cb in range(4):
                    w = load_w(win, kind * 2048 + cb * 512, 512, 16)
                    for j in range(SBT):
                        tok0 = (sb * SBT + j) * 128
                        ps = psA[j % 2]
                        proj_tok(ps, xT, j, w)
                        if kind == 0:
                            src_t, src_ap = qb, qb[:]
                            rope256(ps, src_ap, qb, 1.0, tok0)
                            dstT = qTc[j]
                        else:
                            src_t, src_ap = kc[j], kc[j][:, cb * 512:(cb + 1) * 512]
                            rope256(ps, src_ap, kc[j], 1.0 / 16.0, tok0)
                            dstT = kTc[j]
                        pst = psT[j % 2]

                        def f(e, src_ap=src_ap, pst=pst):
                            for dkc in range(2):
                                for hl in range(2):
                                    i = dkc * 2 + hl
                                    ins = e.transpose(out=pst[:, i * 128:(i + 1) * 128], in_=src_ap[:, hl * 256 + dkc * 128:hl * 256 + (dkc + 1) * 128],
                                                      identity=identb[:])
                            return ins
                        op("pe", f, [src_t, identb], [pst])
                        evac(dstT[:, :, 2 * cb:2 * cb + 2, :], pst[:, 0:512].rearrange("p (a b t) -> p a b t", a=2, b=2), [pst], [dstT])
            for j in range(SBT):
                c = sb * SBT + j
                dma("pool", r_qT[c], qTc[j][:].rearrange("p a b t -> p (a b t)"), [qTc[j]], [dbuf(r_qT)])
                dma("pool", r_k[c * 128:(c + 1) * 128, :], kc[j][:], [kc[j]], [dbuf(r_k)])
                op("dve", lambda e, j=j: e.tensor_tensor(out=qTc[j][:], in0=qTc[j][:], in1=DqT[:].unsqueeze(1).to_broadcast([128, 2, 8, 128]), op=ALU.mult),
                   [qTc[j], DqT], [qTc[j]])
                op("dve", lambda e, j=j: e.tensor_tensor(out=kc[j][:].rearrange("p (h d) -> p h d", d=256), in0=kc[j][:].rearrange("p (h d) -> p h d", d=256),
                                                         in1=kdf[:].unsqueeze(2).to_broadcast([128, 8, 256]), op=ALU.mult), [kc[j], kdf], [kc[j]])
            for h in range(8):
                w = load_w(win, 4096 + h * 512, 512, 16)
                for j in range(SBT):
                    ps = psA[j % 2]
                    proj_tok(ps, xT, j, w)
                    evac(vall[j][:, h * 512:(h + 1) * 512], ps[:], [ps], [vall[j]])
            for j in range(SBT):
                tok0 = (sb * SBT + j) * 128
                dma("pool", r_v[tok0:tok0 + 128, :], vall[j][:], [vall[j]], [dbuf(r_v)])
            for j in range(SBT):
                tok0 = (sb * SBT + j) * 128
                for h in range(8):
                    sctr[0] += 1
                    k_ = sctr[0]
                    psS = psB[k_ % 2]
                    vht = vall[j][:, h * 512:(h + 1) * 512]
                    vht_T[0] = vall[j]

                    def f(e, j=j, h=h, psS=psS):
                        for dkc in range(2):
                            ins = e.matmul(psS[:, 0:128], lhsT=kTc[j][:, dkc, h, :], rhs=qTc[j][:, dkc, h, :], start=(dkc == 0), stop=(dkc == 1))
                        return ins
                    op("pe", f, [kTc[j], qTc[j]], [psS])
                    at = AT[k_ % 2]
                    op("dve", lambda e, at=at, psS=psS, h=h: e.tensor_tensor(out=at[:], in0=psS[:, 0:128], in1=DcT[h][:], op=ALU.mult), [psS, DcT[h]], [at])
                    acc = psX[k_ % 2]

                    def f2(e, j=j, h=h, at=at, acc=acc, vht=vht, sb_=Stb[h]):
                        e.matmul(acc[:], lhsT=at[:], rhs=vht, start=True, stop=False)
                        e.matmul(acc[:], lhsT=qTc[j][:, 0, h, :], rhs=sb_[:, 0, :], start=False, stop=False)
                        return e.matmul(acc[:], lhsT=qTc[j][:, 1, h, :], rhs=sb_[:, 1, :], start=False, stop=True)
                    op("pe", f2, [at, vall[j], qTc[j], Stb[h]], [acc])
                    ob = o1buf[k_ % 2]
                    evac(ob[:], acc[:], [acc], [ob])
                    dma("pool", r_o1[tok0:tok0 + 128, h * 512:(h + 1) * 512], ob[:], [ob], [dbuf(r_o1)])
                    psU = psA if k_ % 2 else psTf
                    state_update(St32, Stb, psU, h, kc[j], vht, h)
            for h in range(8):
                w = load_w(win, 8192 + h * 512, 512, 16)
                for j in range(SBT):
                    tok0 = (sb * SBT + j) * 128
                    ps = psA[j % 2]
                    proj_tok(ps, xT, j, w)
                    g_ = gh[j % 2]
                    op("act", lambda e, g_=g_, ps=ps: e.activation(out=g_[:], in_=ps[:], func=AF.Silu), [ps], [g_])
                    dma("pool", r_g[tok0:tok0 + 128, h * 512:(h + 1) * 512], g_[:], [g_], [dbuf(r_g)])

        new_phase()
        St32b = [T("St32_%d" % h, [128, 2, 512], F32) for h in range(8)]
        Stbb = [T("Stb_%d" % h, [128, 2, 512], BF16) for h in range(8)]
        DqbT = T("DqbT", [128, 8, 128], F32)
        qT1 = T("qT1", [128, 2, 8, 128], BF16)
        kc1 = T("kc1", [128, 2048], BF16)
        vh2 = [T("vh%d" % i, [128, 512], BF16) for i in range(2)]
        gh2 = [T("gh%d" % i, [128, 512], BF16) for i in range(2)]
        og = T("og", [128, 4096], BF16)
        ogT = T("ogT", [128, 32, SBT * 128], BF16)
        o1b = [T("o1b%d" % i, [128, 512], F32) for i in range(2)]
        ot = [T("ot%d" % i, [128, 512], F32) for i in range(2)]
        rbuf = T("rbuf", [128, SBT, D], F32)
        gst = [sm("gn_stats%d" % i, [128, 6]) for i in range(2)]
        gmv = [sm("gn_mv%d" % i, [128, 2]) for i in range(2)]
        grs = [sm("gn_rstd%d" % i, [128, 1]) for i in range(2)]
        gnm = [sm("gn_nmr%d" % i, [128, 1]) for i in range(2)]
        for h in range(8):
            op("act", lambda e, h=h: e.activation(out=DqbT[:, h, :], in_=prow[:, 128:256], func=AF.Exp, scale=lg[:, 8 + h:9 + h]), [prow, lg], [DqbT])
            op("pool", lambda e, h=h: e.memset(St32b[h][:], 0.0), [], [St32b[h]])
            op("pool", lambda e, h=h: e.memset(Stbb[h][:], 0.0), [], [Stbb[h]])
        k2 = 0
        for sb in reversed(range(NSB)):
            for j in reversed(range(SBT)):
                c = sb * SBT + j
                tok0 = c * 128
                dma("sp", qT1[:].rearrange("p a b t -> p (a b t)"), r_qT[c], [dbuf(r_qT)], [qT1])
                dma("sp", kc1[:], r_k[tok0:tok0 + 128, :], [dbuf(r_k)], [kc1])
                op("dve", lambda e: e.tensor_tensor(out=qT1[:], in0=qT1[:], in1=DqbT[:].unsqueeze(1).to_broadcast([128, 2, 8, 128]), op=ALU.mult),
                   [qT1, DqbT], [qT1])
                op("dve", lambda e: e.tensor_tensor(out=kc1[:].rearrange("p (h d) -> p h d", d=256), in0=kc1[:].rearrange("p (h d) -> p h d", d=256),
                                                    in1=kdb[:].unsqueeze(2).to_broadcast([128, 8, 256]), op=ALU.mult), [kc1, kdb], [kc1])
                for h in range(8):
                    k2 += 1
                    v_ = vh2[k2 % 2]
                    g_ = gh2[k2 % 2]
                    o_ = o1b[k2 % 2]
                    t_ = ot[k2 % 2]
                    gst_, gmv_, grs_, gnm_ = gst[k2 % 2], gmv[k2 % 2], grs[k2 % 2], gnm[k2 % 2]
                    dma("sp", v_[:], r_v[tok0:tok0 + 128, h * 512:(h + 1) * 512], [dbuf(r_v)], [v_])
                    dma("sp", g_[:], r_g[tok0:tok0 + 128, h * 512:(h + 1) * 512], [dbuf(r_g)], [g_])
                    dma("sp", o_[:], r_o1[tok0:tok0 + 128, h * 512:(h + 1) * 512], [dbuf(r_o1)], [o_])
                    acc = psX[k2 % 2]

                    def f2(e, h=h, sb_=Stbb[h], acc=acc):
                        for dkc in range(2):
                            ins = e.matmul(acc[:], lhsT=qT1[:, dkc, h, :], rhs=sb_[:, dkc, :], start=(dkc == 0), stop=(dkc == 1))
                        return ins
                    op("pe", f2, [qT1, Stbb[h]], [acc])
                    op("dve", lambda e, t_=t_, o_=o_, acc=acc: e.tensor_tensor(out=t_[:], in0=acc[:], in1=o_[:], op=ALU.add), [acc, o_], [t_])
                    op("dve", lambda e, t_=t_, gst_=gst_: e.bn_stats(out=gst_[:], in_=t_[:]), [t_], [gst_])
                    op("dve", lambda e, gst_=gst_, gmv_=gmv_: e.bn_aggr(out=gmv_[:], in_=gst_[:]), [gst_], [gmv_])
                    op("dve", lambda e, gmv_=gmv_, grs_=grs_: e.tensor_scalar(out=grs_[:], in0=gmv_[:, 1:2], scalar1=LN_EPS, scalar2=None, op0=ALU.add), [gmv_], [grs_])
                    op("act", lambda e, grs_=grs_: e.activation(out=grs_[:], in_=grs_[:], func=AF.Sqrt), [grs_], [grs_])
                    op("dve", lambda e, grs_=grs_: e.reciprocal(out=grs_[:], in_=grs_[:]), [grs_], [grs_])
                    op("dve", lambda e, gmv_=gmv_, grs_=grs_, gnm_=gnm_: e.scalar_tensor_tensor(out=gnm_[:], in0=gmv_[:, 0:1], scalar=-1.0, in1=grs_[:], op0=ALU.mult, op1=ALU.mult),
                       [gmv_, grs_], [gnm_])
                    op("act", lambda e, t_=t_, gnm_=gnm_, grs_=grs_: e.activation(out=t_[:], in_=t_[:], func=AF.Identity, bias=gnm_[:, 0:1], scale=grs_[:, 0:1]),
                       [t_, gnm_, grs_], [t_])
                    op("pool", lambda e, t_=t_, g_=g_, h=h: e.tensor_tensor(out=og[:, h * 512:(h + 1) * 512], in0=t_[:], in1=g_[:], op=ALU.mult), [t_, g_], [og])
                    vht_T[0] = v_
                    psU = psA if k2 % 2 else psB
                    state_update(St32b, Stbb, psU, h, kc1, v_[:], 8 + h)
                for q8 in range(4):
                    pst = psT[q8 % 2]

                    def f(e, q8=q8, pst=pst):
                        for i in range(8):
                            kk = q8 * 8 + i
                            ins = e.transpose(out=pst[:, i * 128:(i + 1) * 128], in_=og[:, kk * 128:(kk + 1) * 128], identity=identb[:])
                        return ins
                    op("pe", f, [og, identb], [pst])
                    evac(ogT[:, q8 * 8:(q8 + 1) * 8, j * 128:(j + 1) * 128], pst[:].rearrange("p (a t) -> p a t", t=128), [pst], [ogT])
            for cb in range(8):
                w = load_w(wout, cb * 256, 256, 32)
                for j in range(SBT):
                    tok0 = (sb * SBT + j) * 128
                    ps = psA[j % 2]

                    def f(e, ps=ps, w=w, j=j):
                        for kk in range(32):
                            ins = e.matmul(ps[:, 0:256], lhsT=ogT[:, kk, j * 128:(j + 1) * 128], rhs=w[:, kk, :], start=(kk == 0), stop=(kk == 31))
                        return ins
                    op("pe", f, [ogT, w], [ps])
                    st = xstg[xctr[0] % 2]
                    xctr[0] += 1
                    dma("sp", st[:, 0:256], xsrc[tok0:tok0 + 128, cb * 256:(cb + 1) * 256], [dbuf(xsrc)], [st])
                    op("dve", lambda e, ps=ps, st=st, j=j, cb=cb: e.scalar_tensor_tensor(out=rbuf[:, j, cb * 256:(cb + 1) * 256], in0=st[:, 0:256], scalar=ALPHA,
                                                                                          in1=ps[:, 0:256], op0=ALU.mult, op1=ALU.add), [st, ps], [rbuf])
            for j in range(SBT):
                layer_norm_out(li, rbuf, j, (sb * SBT + j) * 128, xdst)

    ja = jr = 0
    for l in layers:
        if l == "a":
            cast_w(a_win[ja], a_win_b[ja], D, A_IN)
            cast_w(a_wout[ja], a_wout_b[ja], D, D)
            ja += 1
        else:
            cast_w(r_win[jr], r_win_b[jr], D, R_IN)
            cast_w(r_wout[jr], r_wout_b[jr], 2 * D, D)
            jr += 1
    ja = jr = 0
    cur = x_in
    for li, l in enumerate(layers):
        dst = y_out if li == nL - 1 else xs[li % 2]
        if l == "a":
            attention_layer(li, ja, cur, dst)
            ja += 1
        else:
            retention_layer(li, jr, cur, dst)
            jr += 1
        cur = dst

    sc.finish()
    sc.emit_all(nc, stack)
    stack.close()
    return nc, sc


def make_consts(S):
    cosa, sina = rope_tables(S, 128)
    cosr, sinr = rope_tables(S, 256)
    n = np.arange(128, dtype=np.float32)
    diff = n[None, :] - n[:, None]
    p1 = np.maximum(diff, 0).astype(np.float32)
    p2 = np.maximum(-diff, 0).astype(np.float32)
    posv = np.stack([n + 1, 128 - n, 127 - n, n], axis=1).astype(np.float32)
    prow = np.concatenate([np.tile((n + 1)[None, :], (128, 1)), np.tile((128 - n)[None, :], (128, 1))], axis=1).astype(np.float32)
    return dict(cosa=cosa, sina=sina, cosr=cosr, sinr=sinr, identf_d=np.eye(128, dtype=np.float32), p1_d=p1, p2_d=p2, posv_d=posv, prow_d=prow)


def run(x, attn_w_in, attn_q_gain, attn_k_gain, attn_w_out, ret_w_in, ret_decay_exp, ret_w_out, ln_gain, ln_bias, layers):
    B, S, _ = x.shape
    nc, sc = build_program(S, layers)
    consts = make_consts(S)
    n_a = max(1, sum(1 for l in layers if l == "a"))
    n_r = max(1, sum(1 for l in layers if l == "r"))
    nL = len(layers)
    shared = dict(a_win=np.ascontiguousarray(attn_w_in[:n_a]), a_wout=np.ascontiguousarray(attn_w_out[:n_a]),
                  a_qg=np.ascontiguousarray(attn_q_gain[:n_a]), a_kg=np.ascontiguousarray(attn_k_gain[:n_a]),
                  r_win=np.ascontiguousarray(ret_w_in[:n_r]), r_wout=np.ascontiguousarray(ret_w_out[:n_r]),
                  r_dec=np.ascontiguousarray(ret_decay_exp[:n_r].reshape(n_r, 16)),
                  ln_g=np.ascontiguousarray(ln_gain[:nL]), ln_b=np.ascontiguousarray(ln_bias[:nL]), **consts)
    in_maps = [dict(x=np.ascontiguousarray(x[b]), **shared) for b in range(B)]
    res = run_bass_kernel_spmd(nc, in_maps, core_ids=list(range(B)))
    return np.stack([res.results[b]["y"] for b in range(B)], axis=0)


def kernel(x, attn_w_in, attn_q_gain, attn_k_gain, attn_w_out, ret_w_in, ret_decay_exp, ret_w_out, ln_gain, ln_bias):
    out = run(np.asarray(x), np.asarray(attn_w_in), np.asarray(attn_q_gain), np.asarray(attn_k_gain), np.asarray(attn_w_out),
              np.asarray(ret_w_in), np.asarray(ret_decay_exp), np.asarray(ret_w_out), np.asarray(ln_gain), np.asarray(ln_bias),
              layers=["a", "r", "a", "r"])
    return out.astype(np.float32)
```

```python
import math
from contextlib import ExitStack
import numpy as np
import concourse.bass as bass
import concourse.mybir as mybir
from concourse.bass_utils import run_bass_kernel_spmd

F32 = mybir.dt.float32
BF16 = mybir.dt.bfloat16
AF = mybir.ActivationFunctionType
ALU = mybir.AluOpType
AX = mybir.AxisListType

D = 2048
GRID_W = 64
ROPE_THETA = 10000.0
RMS_EPS = 1e-6
LN_EPS = 1e-5
DEPTH = 4
ALPHA = (2.0 * DEPTH) ** 0.25
A_IN = 5120
R_IN = 12288
NCORES = 4


class Buf:
    __slots__ = ("name", "w", "r")

    def __init__(self, name):
        self.name = name
        self.w = None
        self.r = []


class Sched:
    ENG = ("pe", "act", "dve", "pool", "sp")
    NSLOT = 6

    def __init__(self):
        self.streams = {e: [] for e in self.ENG}
        self.cnt = {e: 0 for e in self.ENG}
        self.waited = {e: {} for e in self.ENG}
        self.dma_n = {e: 0 for e in self.ENG}
        self.dma_tok = {e: [None] * self.NSLOT for e in self.ENG}
        self.dma_cnt = {e: [0] * self.NSLOT for e in self.ENG}
        self.nops = 0

    def _need(self, eng, tok, waits):
        if tok is None:
            return
        key, val = tok
        if key == eng and eng in ("pe", "sp"):
            return
        if self.waited[eng].get(key, 0) >= val:
            return
        self.waited[eng][key] = val
        waits.append((key, val))

    def op(self, eng, emit, reads=(), writes=(), dma=False):
        waits = []
        for b in reads:
            self._need(eng, b.w, waits)
        for b in writes:
            self._need(eng, b.w, waits)
            for t in b.r:
                if t[0] == eng and not dma:
                    continue
                self._need(eng, t, waits)
        if dma:
            n = self.dma_n[eng]
            self.dma_n[eng] = n + 1
            slot = n % self.NSLOT
            self._need(eng, self.dma_tok[eng][slot], waits)
            self.dma_cnt[eng][slot] += 16
            tok = (("dma", eng, slot), self.dma_cnt[eng][slot])
            self.dma_tok[eng][slot] = tok
            sig = (tok[0], 16)
        else:
            self.cnt[eng] += 1
            tok = (eng, self.cnt[eng])
            sig = (eng, 1)
        for b in reads:
            b.r.append(tok)
        for b in writes:
            b.w = tok
            b.r = []
        self.streams[eng].append((waits, emit, sig))
        self.nops += 1
        return tok

    def fence(self):
        toks = [(e, self.cnt[e]) for e in ("pe", "act", "dve", "pool") if self.cnt[e] > 0]
        for e in self.ENG:
            for slot in range(self.NSLOT):
                if self.dma_tok[e][slot] is not None:
                    toks.append(self.dma_tok[e][slot])
        for e in self.ENG:
            waits = []
            for t in toks:
                self._need(e, t, waits)
            if waits:
                self.streams[e].append((waits, None, None))

    def finish(self):
        for eng in self.ENG:
            waits = []
            for slot in range(self.NSLOT):
                self._need(eng, self.dma_tok[eng][slot], waits)
            if waits:
                self.streams[eng].append((waits, None, None))

    def emit_all(self, nc, stack):
        sems = {}

        def sem(key):
            if key not in sems:
                nm = "s_" + ("_".join(str(k) for k in key) if isinstance(key, tuple) else key)
                sems[key] = stack.enter_context(nc.semaphore(nm))
            return sems[key]

        for e in self.ENG:
            sem(e)
            for s in range(self.NSLOT):
                if self.dma_cnt[e][s]:
                    sem(("dma", e, s))
        block = stack.enter_context(nc.Block())
        sect = {"pe": block.tensor, "act": block.scalar, "dve": block.vector, "pool": block.gpsimd, "sp": block.sync}

        def mk(stream):
            def body(engine):
                for waits, emit, sig in stream:
                    for key, val in waits:
                        engine.wait_ge(sems[key], val)
                    if emit is not None:
                        ins = emit(engine)
                        ins.then_inc(sems[sig[0]], sig[1])
            return body

        for e in self.ENG:
            if self.streams[e]:
                sect[e](mk(self.streams[e]))


def rope_tables(S, hd):
    rows_n = S // GRID_W
    row = np.repeat(np.arange(rows_n, dtype=np.float32), GRID_W)
    col = np.tile(np.arange(GRID_W, dtype=np.float32), rows_n)
    half = hd // 2
    inv_freq = (np.float32(ROPE_THETA) ** (-np.arange(0, half, 2, dtype=np.float32) / np.float32(half))).astype(np.float32)
    ang_r = row[:, None] * inv_freq
    ang_c = col[:, None] * inv_freq
    ang = np.concatenate([ang_r, ang_r, ang_c, ang_c], axis=-1).astype(np.float32)
    cos = np.cos(ang).astype(np.float32)
    sin = np.sin(ang).astype(np.float32)
    q4 = hd // 4
    sgn = np.concatenate([-np.ones(q4), np.ones(q4), -np.ones(q4), np.ones(q4)]).astype(np.float32)
    return cos, (sin * sgn[None, :]).astype(np.float32)


def build_program(S, layers):
    NT = S // 128
    n_a = sum(1 for l in layers if l == "a")
    n_r = sum(1 for l in layers if l == "r")
    nL = len(layers)
    nc = bass.Bass("TRN2", target_bir_lowering=False)
    sc = Sched()

    def din(name, shape, dt=F32):
        return nc.dram_tensor(name, list(shape), dt, kind="ExternalInput").ap()

    def dint(name, shape, dt):
        return nc.dram_tensor(name, list(shape), dt, kind="Internal").ap()

    x_in = din("x", [S, D])
    y_out = nc.dram_tensor("y", [S, D], F32, kind="ExternalOutput").ap()
    a_win = din("a_win", [max(n_a, 1), D, A_IN])
    a_wout = din("a_wout", [max(n_a, 1), D, D])
    a_qg = din("a_qg", [max(n_a, 1), 128])
    a_kg = din("a_kg", [max(n_a, 1), 128])
    r_win = din("r_win", [max(n_r, 1), D, R_IN])
    r_wout = din("r_wout", [max(n_r, 1), 2 * D, D])
    r_dec = din("r_dec", [max(n_r, 1), 16])
    ln_g = din("ln_g", [nL, D])
    ln_b = din("ln_b", [nL, D])
    cosa_d = din("cosa", [S, 128])
    sina_d = din("sina", [S, 128])
    cosr_d = din("cosr", [S, 256])
    sinr_d = din("sinr", [S, 256])
    identf_d = din("identf_d", [128, 128])
    p1_d = din("p1_d", [128, 128])
    p2_d = din("p2_d", [128, 128])
    posv_d = din("posv_d", [128, 4])
    prow_d = din("prow_d", [128, 256])

    a_win_b = dint("a_win_b", [max(n_a, 1), D, A_IN], BF16)
    a_wout_b = dint("a_wout_b", [max(n_a, 1), D, D], BF16)
    r_win_b = dint("r_win_b", [max(n_r, 1), D, R_IN], BF16)
    r_wout_b = dint("r_wout_b", [max(n_r, 1), 2 * D, D], BF16)
    xs = [dint("xs0", [S, D], F32), dint("xs1", [S, D], F32)]
    if n_r:
        r_qT = dint("r_qT", [NT, 128, 2 * 8 * 128], BF16)
        r_k = dint("r_k", [S, 2048], BF16)
        r_v = dint("r_v", [S, 4096], BF16)
        r_g = dint("r_g", [S, 4096], BF16)
        r_o1 = dint("r_o1", [S, 4096], F32)

    stack = ExitStack()
    DB = {}

    def dbuf(ap):
        k = ap.name if hasattr(ap, "name") else id(ap)
        if k not in DB:
            DB[k] = Buf(str(k))
        return DB[k]

    NA = 29696
    arena = stack.enter_context(nc.sbuf_tensor("arena", [128, NA], F32))
    aoff = [0]

    class T:
        def __init__(self, name, shape, dt, psum=False, persist=False, view=None, buf=None):
            if view is not None:
                self.t = view
                self.b = buf
                return
            shape = list(shape)
            if psum:
                self.t = stack.enter_context(nc.psum_tensor(name, shape, dt))
            elif persist:
                self.t = stack.enter_context(nc.sbuf_tensor(name, shape, dt))
            else:
                n = 1
                for d_ in shape[1:]:
                    n *= d_
                nw = n if dt == F32 else (n + 1) // 2
                nw = (nw + 7) // 8 * 8
                assert aoff[0] + nw <= NA, ("arena overflow", name, aoff[0], nw)
                v = arena[:, aoff[0]:aoff[0] + nw]
                aoff[0] += nw
                if dt != F32:
                    v = v.bitcast(dt)
                v = v[:, 0:n]
                if len(shape) == 3:
                    v = v.rearrange("p (a b) -> p a b", b=shape[2])
                elif len(shape) == 4:
                    v = v.rearrange("p (a b c) -> p a b c", b=shape[2], c=shape[3])
                self.t = v
            self.b = Buf(name)

        def __getitem__(self, k):
            return self.t[k]

    def new_phase():
        sc.fence()
        aoff[0] = 0

    def op(eng, fn, reads=(), writes=(), dma=False):
        rb = [x.b if isinstance(x, T) else x for x in reads]
        wb = [x.b if isinstance(x, T) else x for x in writes]
        return sc.op(eng, fn, rb, wb, dma)

    def dma(q, out, in_, reads, writes):
        op(q, lambda e: e.dma_start(out=out, in_=in_), reads, writes, dma=True)

    identf = T("identf", [128, 128], F32, persist=True)
    identb = T("identb", [128, 128], BF16, persist=True)
    onesb = T("onesb", [128, 128], BF16, persist=True)
    dma("sp", identf[:], identf_d[:, :], [], [identf])
    op("dve", lambda e: e.tensor_copy(out=identb[:], in_=identf[:]), [identf], [identb])
    op("dve", lambda e: e.memset(onesb[:], 1.0), [], [onesb])

    def cast_w(src, dst, rows, cols):
        for r0 in range(0, rows, 128):
            s = src[r0:r0 + 128, :].rearrange("r (a b) -> r a b", b=1024)
            d_ = dst[r0:r0 + 128, :].rearrange("r (a b) -> r a b", b=1024)
            dma("pool", d_, s, [], [dbuf(dst)])

    psA = [T("psA0", [128, 512], F32, True), T("psA1", [128, 512], F32, True)]
    psB = [T("psB0", [128, 512], F32, True), T("psB1", [128, 512], F32, True)]
    psT = [T("psT0", [128, 1024], BF16, True), T("psT1", [128, 1024], BF16, True)]
    psTf = [T(None, None, None, view=psT[i].t[:].bitcast(F32), buf=psT[i].b) for i in range(2)]
    psO = T("psO", [128, 512], F32, True)
    psL = T("psL", [128, 512], F32, True)

    wbuf = [T("wbuf0", [128, 16, 512], BF16, persist=True), T("wbuf1", [128, 16, 512], BF16, persist=True)]
    wctr = [0]
    xstg = [T("xstg0", [128, 512], F32, persist=True), T("xstg1", [128, 512], F32, persist=True)]
    xctr = [0]
    lng = [T("lng0", [128, 512], F32, persist=True), T("lng1", [128, 512], F32, persist=True)]
    lnb = [T("lnb0", [128, 512], F32, persist=True), T("lnb1", [128, 512], F32, persist=True)]
    small = {}

    def sm(name, shape, dt=F32):
        if name not in small:
            small[name] = T(name, shape, dt, persist=True)
        return small[name]

    evac_ctr = [0]

    def evac(out, in_, reads, writes):
        evac_ctr[0] += 1
        if evac_ctr[0] % 2:
            op("act", lambda e: e.activation(out=out, in_=in_, func=AF.Copy), reads, writes)
        else:
            op("dve", lambda e: e.tensor_copy(out=out, in_=in_), reads, writes)

    def load_w(src2d, c0, ncol, nk):
        w = wbuf[wctr[0] % 2]
        wctr[0] += 1
        assert nk * ncol <= 16 * 512
        view = w.t[:].rearrange("p a b -> p (a b)")[:, 0:nk * ncol].rearrange("p (a b) -> p a b", b=ncol)
        dma("sp", view, src2d[0:nk * 128, c0:c0 + ncol].rearrange("(kc p) c -> p kc c", p=128), [dbuf(src2d)], [w])

        return T(None, None, None, view=view, buf=w.b)

    def load_xT(xsrc, t0, ntile, xT):
        for j in range(ntile):
            tok0 = (t0 + j) * 128
            for c4 in range(4):
                st = xstg[xctr[0] % 2]
                xctr[0] += 1
                dma("sp", st[:], xsrc[tok0:tok0 + 128, c4 * 512:(c4 + 1) * 512], [dbuf(xsrc)], [st])
                ps = psB[c4 % 2]

                def f(e, ps=ps, st=st):
                    for i in range(4):
                        ins = e.transpose(out=ps[:, i * 128:(i + 1) * 128], in_=st[:, i * 128:(i + 1) * 128], identity=identf[:])
                    return ins
                op("pe", f, [st, identf], [ps])
                evac(xT[:, c4 * 4:(c4 + 1) * 4, j * 128:(j + 1) * 128],
                     ps[:].rearrange("p (a b) -> p a b", b=128), [ps], [xT])

    def proj_tok(ps, xT, j, w, nk=16, ncol=512):
        def f(e):
            for kc in range(nk):
                ins = e.matmul(ps[:, 0:ncol], lhsT=xT[:, kc, j * 128:(j + 1) * 128], rhs=w[:, kc, 0:ncol],
                               start=(kc == 0), stop=(kc == nk - 1))
            return ins
        op("pe", f, [xT, w], [ps])

    def layer_norm_out(li, rbuf, j, tok0, xdst):
        stats = sm("ln_stats", [128, 4, 6])
        mv = sm("ln_mv", [128, 2])
        rstd = sm("ln_rstd", [128, 1])
        nmr = sm("ln_nmr", [128, 1])
        for c in range(4):
            op("dve", lambda e, c=c: e.bn_stats(out=stats[:, c, :], in_=rbuf[:, j, c * 512:(c + 1) * 512]), [rbuf], [stats])
        op("dve", lambda e: e.bn_aggr(out=mv[:], in_=stats[:].rearrange("p a b -> p (a b)")), [stats], [mv])
        op("dve", lambda e: e.tensor_scalar(out=rstd[:], in0=mv[:, 1:2], scalar1=LN_EPS, scalar2=None, op0=ALU.add), [mv], [rstd])
        op("act", lambda e: e.activation(out=rstd[:], in_=rstd[:], func=AF.Sqrt), [rstd], [rstd])
        op("dve", lambda e: e.reciprocal(out=rstd[:], in_=rstd[:]), [rstd], [rstd])
        op("dve", lambda e: e.scalar_tensor_tensor(out=nmr[:], in0=mv[:, 0:1], scalar=-1.0, in1=rstd[:], op0=ALU.mult, op1=ALU.mult),
           [mv, rstd], [nmr])
        op("act", lambda e: e.activation(out=rbuf[:, j, :], in_=rbuf[:, j, :], func=AF.Identity, bias=nmr[:, 0:1], scale=rstd[:, 0:1]),
           [rbuf, nmr, rstd], [rbuf])
        for c in range(4):
            g_ = lng[c % 2]
            b_ = lnb[c % 2]
            dma("sp", g_[:], ln_g[li:li + 1, c * 512:(c + 1) * 512].partition_broadcast(128), [], [g_])
            dma("sp", b_[:], ln_b[li:li + 1, c * 512:(c + 1) * 512].partition_broadcast(128), [], [b_])
            op("pool", lambda e, c=c, g_=g_: e.tensor_tensor(out=rbuf[:, j, c * 512:(c + 1) * 512], in0=rbuf[:, j, c * 512:(c + 1) * 512],
                                                             in1=g_[:], op=ALU.mult), [rbuf, g_], [rbuf])
            op("dve", lambda e, c=c, b_=b_: e.tensor_tensor(out=rbuf[:, j, c * 512:(c + 1) * 512], in0=rbuf[:, j, c * 512:(c + 1) * 512],
                                                            in1=b_[:], op=ALU.add), [rbuf, b_], [rbuf])
        dma("pool", xdst[tok0:tok0 + 128, :], rbuf[:, j, :], [rbuf], [dbuf(xdst)])

    def attention_layer(li, ja, xsrc, xdst):
        new_phase()
        SBT = 2
        NSB = NT // SBT
        win = a_win_b[ja]
        wout = a_wout_b[ja]
        KT = T("KT", [128, 4, S], BF16)
        V = T("V", [128, NT, 512], BF16)
        xT = T("xT", [128, 16, SBT * 128], BF16)
        qT = T("qT", [128, 16, SBT * 128], BF16)
        gT = T("gT", [128, 16, SBT * 128], BF16)
        rbuf = T("rbuf", [128, SBT, D], F32)
        PT = [T("PT%d" % i, [128, 512], BF16) for i in range(3)]
        cosb = [sm("cosb0", [128, 128]), sm("cosb1", [128, 128])]
        sinb = [sm("sinb0", [128, 128]), sm("sinb1", [128, 128])]
        gq = sm("gq_rep", [128, 128])
        gk = sm("gk_rep", [128, 128])
        negb = sm("negb", [128, 1])
        mq = sm("mq", [128, 1])
        mk_ = sm("mk", [128, 1])
        sq = sm("sq", [128, 512])
        ss = sm("ss", [128, 4])
        rs = sm("rs", [128, 4])
        qg = sm("qg", [128, 512])
        t1 = sm("t1", [128, 512])
        t2 = sm("t2", [128, 512])
        qbs = [sm("qb0", [128, 512], BF16), sm("qb1", [128, 512], BF16)]
        pend = []
        rl = sm("rl", [128, 512])
        tmpo = sm("tmpo", [128, 512])

        dma("sp", gq[:], a_qg[ja:ja + 1, :].partition_broadcast(128), [], [gq])
        dma("sp", gk[:], a_kg[ja:ja + 1, :].partition_broadcast(128), [], [gk])
        op("dve", lambda e: e.tensor_reduce(out=mq[:], in_=gq[:], axis=AX.X, op=ALU.max, apply_absolute_value=True), [gq], [mq])
        op("dve", lambda e: e.tensor_reduce(out=mk_[:], in_=gk[:], axis=AX.X, op=ALU.max, apply_absolute_value=True), [gk], [mk_])
        op("dve", lambda e: e.scalar_tensor_tensor(out=negb[:], in0=mq[:], scalar=-math.sqrt(128.0), in1=mk_[:], op0=ALU.mult, op1=ALU.mult),
           [mq, mk_], [negb])

        def qk_post(ps, gain, scale, tok0, cs, qb):
            cb_, sb_ = cosb[cs % 2], sinb[cs % 2]
            op("act", lambda e: e.activation(out=sq[:], in_=ps[:], func=AF.Square), [ps], [sq])
            op("dve", lambda e: e.tensor_reduce(out=ss[:], in_=sq[:].rearrange("p (h d) -> p h d", d=128), axis=AX.X, op=ALU.add), [sq], [ss])
            op("dve", lambda e: e.tensor_scalar(out=rs[:], in0=ss[:], scalar1=1.0 / (128.0 * scale * scale), scalar2=RMS_EPS / (scale * scale),
                                                op0=ALU.mult, op1=ALU.add), [ss], [rs])
            op("act", lambda e: e.activation(out=rs[:], in_=rs[:], func=AF.Sqrt), [rs], [rs])
            op("dve", lambda e: e.reciprocal(out=rs[:], in_=rs[:]), [rs], [rs])
            op("dve", lambda e: e.tensor_tensor(out=qg[:].rearrange("p (h d) -> p h d", d=128), in0=ps[:].rearrange("p (h d) -> p h d", d=128),
                                                in1=gain[:].unsqueeze(1).to_broadcast([128, 4, 128]), op=ALU.mult), [ps, gain], [qg])
            op("dve", lambda e: e.tensor_tensor(out=t1[:].rearrange("p (h d) -> p h d", d=128), in0=qg[:].rearrange("p (h d) -> p h d", d=128),
                                                in1=cb_[:].unsqueeze(1).to_broadcast([128, 4, 128]), op=ALU.mult), [qg, cb_], [t1])
            qv = qg[:].rearrange("p (g s d) -> p g s d", s=2, d=32)
            tv = t2[:].rearrange("p (g s d) -> p g s d", s=2, d=32)
            sv = sb_[:].rearrange("p (g s d) -> p g s d", s=2, d=32)
            qv5 = qg[:].rearrange("p (h g s d) -> p h g s d", h=4, g=2, s=2, d=32)
            tv5 = t2[:].rearrange("p (h g s d) -> p h g s d", h=4, g=2, s=2, d=32)
            for s_ in range(2):
                op("dve", lambda e, s_=s_: e.tensor_tensor(out=tv5[:, :, :, s_, :], in0=qv5[:, :, :, 1 - s_, :],
                                                            in1=sv[:, :, s_, :].unsqueeze(1).to_broadcast([128, 4, 2, 32]), op=ALU.mult), [qg, sb_], [t2])
            op("dve", lambda e: e.tensor_tensor(out=t1[:], in0=t1[:], in1=t2[:], op=ALU.add), [t1, t2], [t1])
            op("dve", lambda e: e.tensor_tensor(out=qb[:].rearrange("p (h d) -> p h d", d=128), in0=t1[:].rearrange("p (h d) -> p h d", d=128),
                                                in1=rs[:].unsqueeze(2).to_broadcast([128, 4, 128]), op=ALU.mult), [t1, rs], [qb])

        def load_tables(tok0, cs):
            dma("sp", cosb[cs % 2][:], cosa_d[tok0:tok0 + 128, :], [], [cosb[cs % 2]])
            dma("sp", sinb[cs % 2][:], sina_d[tok0:tok0 + 128, :], [], [sinb[cs % 2]])

        def transpose4(qb, pst):
            def f(e):
                for h in range(4):
                    ins = e.transpose(out=pst[:, h * 128:(h + 1) * 128], in_=qb[:, h * 128:(h + 1) * 128], identity=identb[:])
                return ins
            op("pe", f, [qb, identb], [pst])

        cs = 0
        for sb in range(NSB):
            load_xT(xsrc, sb * SBT, SBT, xT)
            wk = load_w(win, 2048, 512, 16)
            wv = load_w(win, 2560, 512, 16)
            for j in range(SBT):
                t = sb * SBT + j
                tok0 = t * 128
                load_tables(tok0, cs)
                ps = psA[t % 2]
                proj_tok(ps, xT, j, wk)
                qb = qbs[cs % 2]
                qk_post(ps, gk, 1.0, tok0, cs, qb)
                cs += 1
                ps = (psO, psL)[t % 2]
                proj_tok(ps, xT, j, wv)
                evac(V[:, t, :], ps[:], [ps], [V])
                while pend:
                    pend.pop(0)()

                def fin(qb=qb, t=t, tok0=tok0):
                    pst = psT[t % 2]
                    transpose4(qb, pst)
                    evac(KT[:, :, tok0:tok0 + 128], pst[:, 0:512].rearrange("p (h t) -> p h t", t=128), [pst], [KT])
                pend.append(fin)
        while pend:
            pend.pop(0)()

        for sb in range(NSB):
            load_xT(xsrc, sb * SBT, SBT, xT)
            for cb in range(4):
                w = load_w(win, cb * 512, 512, 16)
                for j in range(SBT):
                    tok0 = (sb * SBT + j) * 128
                    load_tables(tok0, cs)
                    ps = psA[cs % 2]
                    proj_tok(ps, xT, j, w)
                    qb = qbs[cs % 2]
                    qk_post(ps, gq, 128.0 ** -0.5, tok0, cs, qb)
                    while pend:
                        pend.pop(0)()

                    def fin(qb=qb, cs_=cs, cb=cb, j=j):
                        pst = psT[cs_ % 2]
                        transpose4(qb, pst)
                        evac(qT[:, cb * 4:(cb + 1) * 4, j * 128:(j + 1) * 128], pst[:, 0:512].rearrange("p (h t) -> p h t", t=128), [pst], [qT])
                    pend.append(fin)
                    cs += 1
            for cb in range(4):
                w = load_w(win, 3072 + cb * 512, 512, 16)
                for h in range(4):
                    ps = psA[h % 2]

                    def f(e, ps=ps, w=w, h=h):
                        for kc in range(16):
                            ins = e.matmul(ps[:, 0:SBT * 128], lhsT=w[:, kc, h * 128:(h + 1) * 128], rhs=xT[:, kc, :],
                                           start=(kc == 0), stop=(kc == 15))
                        return ins
                    op("pe", f, [xT, w], [ps])
                    op("act", lambda e, ps=ps, cb=cb, h=h: e.activation(out=gT[:, cb * 4 + h, :], in_=ps[:, 0:SBT * 128], func=AF.Silu), [ps], [gT])
            while pend:
                pend.pop(0)()
            for j in range(SBT):
                for g in range(4):
                    rhs_q = qT[:, g * 4:(g + 1) * 4, j * 128:(j + 1) * 128]

                    def s_mm(kt, g=g, rhs_q=rhs_q):
                        ps = psB[kt % 2]
                        op("pe", lambda e: e.matmul(ps[:], lhsT=KT[:, g, kt * 128:(kt + 1) * 128], rhs=rhs_q, start=True, stop=True), [KT, qT], [ps])
                        pt = PT[kt % 3]
                        op("act", lambda e: e.activation(out=pt[:], in_=ps[:], func=AF.Exp, bias=negb[:, 0:1], scale=1.0), [ps, negb], [pt])

                    def pv_mm(kt, g=g):
                        pt = PT[kt % 3]

                        def f(e):
                            e.matmul(psO[:], lhsT=V[:, kt, g * 128:(g + 1) * 128], rhs=pt[:], start=(kt == 0), stop=(kt == NT - 1))
                            return e.matmul(psL[:], lhsT=onesb[:], rhs=pt[:], start=(kt == 0), stop=(kt == NT - 1))
                        op("pe", f, [V, pt, onesb], [psO, psL])

                    for kt in range(NT):
                        s_mm(kt)
                        if kt >= 1:
                            pv_mm(kt - 1)
                    pv_mm(NT - 1)
                    op("dve", lambda e: e.reciprocal(out=rl[:], in_=psL[:]), [psL], [rl])
                    op("dve", lambda e: e.tensor_tensor(out=tmpo[:], in0=psO[:], in1=rl[:], op=ALU.mult), [psO, rl], [tmpo])
                    gview = gT[:, g * 4:(g + 1) * 4, j * 128:(j + 1) * 128]
                    op("dve", lambda e, gview=gview: e.tensor_tensor(out=gview, in0=gview, in1=tmpo[:].rearrange("p (h t) -> p h t", t=128), op=ALU.mult),
                       [gT, tmpo], [gT])
            for cb in range(4):
                w = load_w(wout, cb * 512, 512, 16)
                for j in range(SBT):
                    tok0 = (sb * SBT + j) * 128
                    ps = psA[j % 2]

                    def f(e, ps=ps, w=w, j=j):
                        for h in range(16):
                            ins = e.matmul(ps[:], lhsT=gT[:, h, j * 128:(j + 1) * 128], rhs=w[:, h, :], start=(h == 0), stop=(h == 15))
                        return ins
                    op("pe", f, [gT, w], [ps])
                    st = xstg[xctr[0] % 2]
                    xctr[0] += 1
                    dma("sp", st[:], xsrc[tok0:tok0 + 128, cb * 512:(cb + 1) * 512], [dbuf(xsrc)], [st])
                    op("dve", lambda e, ps=ps, st=st, j=j, cb=cb: e.scalar_tensor_tensor(out=rbuf[:, j, cb * 512:(cb + 1) * 512], in0=st[:], scalar=ALPHA,
                                                                                          in1=ps[:], op0=ALU.mult, op1=ALU.add), [st, ps], [rbuf])
            for j in range(SBT):
                layer_norm_out(li, rbuf, j, (sb * SBT + j) * 128, xdst)

    def retention_layer(li, jr, xsrc, xdst):
        SBT = 2
        NSB = NT // SBT
        win = r_win_b[jr]
        wout = r_wout_b[jr]
        LN2 = math.log(2.0)

        posv = sm("posv", [128, 4])
        p1 = sm("p1", [128, 128])
        p2 = sm("p2", [128, 128])
        dec = sm("dec_rep", [128, 16])
        u = sm("dec_u", [128, 16])
        tt = sm("dec_t", [128, 16])
        lg = sm("dec_lg", [128, 16])
        dqf = sm("dqf", [128, 8])
        dqb = sm("dqb", [128, 8])
        kdf = sm("kdf", [128, 8])
        kdb = sm("kdb", [128, 8])
        gc = sm("gc", [128, 16])
        arg = sm("dc_arg", [128, 128])
        dma("sp", posv[:], posv_d[:, :], [], [posv])
        dma("sp", p1[:], p1_d[:, :], [], [p1])
        dma("sp", p2[:], p2_d[:, :], [], [p2])
        dma("sp", dec[:], r_dec[jr:jr + 1, :].partition_broadcast(128), [], [dec])
        op("act", lambda e: e.activation(out=u[:], in_=dec[:], func=AF.Exp, scale=-LN2), [dec], [u])
        op("dve", lambda e: e.tensor_scalar(out=tt[:], in0=u[:], scalar1=0.25, scalar2=1.0 / 3.0, op0=ALU.mult, op1=ALU.add), [u], [tt])
        op("dve", lambda e: e.tensor_tensor(out=tt[:], in0=tt[:], in1=u[:], op=ALU.mult), [tt, u], [tt])
        op("dve", lambda e: e.tensor_scalar(out=tt[:], in0=tt[:], scalar1=0.5, scalar2=None, op0=ALU.add), [tt], [tt])
        op("dve", lambda e: e.tensor_tensor(out=tt[:], in0=tt[:], in1=u[:], op=ALU.mult), [tt, u], [tt])
        op("dve", lambda e: e.tensor_scalar(out=tt[:], in0=tt[:], scalar1=1.0, scalar2=None, op0=ALU.add), [tt], [tt])
        op("dve", lambda e: e.scalar_tensor_tensor(out=lg[:], in0=tt[:], scalar=-1.0, in1=u[:], op0=ALU.mult, op1=ALU.mult), [tt, u], [lg])
        op("act", lambda e: e.activation(out=dqf[:], in_=lg[:, 0:8], func=AF.Exp, scale=posv[:, 0:1]), [lg, posv], [dqf])
        op("act", lambda e: e.activation(out=dqb[:], in_=lg[:, 8:16], func=AF.Exp, scale=posv[:, 1:2]), [lg, posv], [dqb])
        op("act", lambda e: e.activation(out=kdf[:], in_=lg[:, 0:8], func=AF.Exp, scale=posv[:, 2:3]), [lg, posv], [kdf])
        op("act", lambda e: e.activation(out=kdb[:], in_=lg[:, 8:16], func=AF.Exp, scale=posv[:, 3:4]), [lg, posv], [kdb])
        op("act", lambda e: e.activation(out=gc[:], in_=lg[:], func=AF.Exp, scale=128.0), [lg], [gc])

        cosr = [sm("cosr0", [128, 256]), sm("cosr1", [128, 256])]
        sinr = [sm("sinr0", [128, 256]), sm("sinr1", [128, 256])]
        t1 = sm("t1", [128, 512])
        t2 = sm("t2", [128, 512])
        cs = [0]

        def rope256(ps, out_ap, outT, scale, tok0):
            c_ = cosr[cs[0] % 2]
            s_ = sinr[cs[0] % 2]
            cs[0] += 1
            dma("sp", c_[:], cosr_d[tok0:tok0 + 128, :], [], [c_])
            dma("sp", s_[:], sinr_d[tok0:tok0 + 128, :], [], [s_])
            op("dve", lambda e: e.tensor_tensor(out=t1[:].rearrange("p (h d) -> p h d", d=256), in0=ps[:].rearrange("p (h d) -> p h d", d=256),
                                                in1=c_[:].unsqueeze(1).to_broadcast([128, 2, 256]), op=ALU.mult), [ps, c_], [t1])
            pv5 = ps[:].rearrange("p (h g s d) -> p h g s d", h=2, g=2, s=2, d=64)
            tv5 = t2[:].rearrange("p (h g s d) -> p h g s d", h=2, g=2, s=2, d=64)
            sv = s_[:].rearrange("p (g s d) -> p g s d", s=2, d=64)
            for k_ in range(2):
                op("dve", lambda e, k_=k_: e.tensor_tensor(out=tv5[:, :, :, k_, :], in0=pv5[:, :, :, 1 - k_, :],
                                                           in1=sv[:, :, k_, :].unsqueeze(1).to_broadcast([128, 2, 2, 64]), op=ALU.mult), [ps, s_], [t2])
            if scale == 1.0:
                op("dve", lambda e: e.tensor_tensor(out=out_ap, in0=t1[:], in1=t2[:], op=ALU.add), [t1, t2], [outT])
            else:
                op("dve", lambda e: e.tensor_tensor(out=t1[:], in0=t1[:], in1=t2[:], op=ALU.add), [t1, t2], [t1])
                op("act", lambda e: e.activation(out=out_ap, in_=t1[:], func=AF.Copy, scale=scale), [t1], [outT])

        new_phase()
        prow = sm("prow", [128, 256])
        dma("sp", prow[:], prow_d[:, :], [], [prow])
        St32 = [T("St32_%d" % h, [128, 2, 512], F32) for h in range(8)]
        Stb = [T("Stb_%d" % h, [128, 2, 512], BF16) for h in range(8)]
        DcT = [T("DcT_%d" % h, [128, 128], F32) for h in range(8)]
        DqT = T("DqfT", [128, 8, 128], F32)
        xT = T("xT", [128, 16, SBT * 128], BF16)
        qTc = [T("qTc%d" % j, [128, 2, 8, 128], BF16) for j in range(SBT)]
        kTc = [T("kTc%d" % j, [128, 2, 8, 128], BF16) for j in range(SBT)]
        kc = [T("kc%d" % j, [128, 2048], BF16) for j in range(SBT)]
        vall = [T("vall%d" % j, [128, 4096], BF16) for j in range(SBT)]
        AT = [T("AT%d" % i, [128, 128], BF16) for i in range(2)]
        gh = [T("gh%d" % i, [128, 512], BF16) for i in range(2)]
        o1buf = [T("o1buf%d" % i, [128, 512], F32) for i in range(2)]
        arg2 = sm("dc_arg2", [128, 128])

        for h in range(8):
            op("dve", lambda e, h=h: e.tensor_scalar(out=arg[:], in0=p1[:], scalar1=lg[:, h:h + 1], scalar2=None, op0=ALU.mult), [p1, lg], [arg])
            op("dve", lambda e, h=h: e.scalar_tensor_tensor(out=arg[:], in0=p2[:], scalar=lg[:, 8 + h:9 + h], in1=arg[:], op0=ALU.mult, op1=ALU.add),
               [p2, lg, arg], [arg])
            op("dve", lambda e, h=h: e.tensor_scalar(out=arg2[:], in0=prow[:, 0:128], scalar1=lg[:, h:h + 1], scalar2=None, op0=ALU.mult), [prow, lg], [arg2])
            op("dve", lambda e: e.tensor_tensor(out=arg[:], in0=arg[:], in1=arg2[:], op=ALU.subtract), [arg, arg2], [arg])
            op("act", lambda e, h=h: e.activation(out=DcT[h][:], in_=arg[:], func=AF.Exp), [arg], [DcT[h]])
            op("act", lambda e, h=h: e.activation(out=DqT[:, h, :], in_=prow[:, 0:128], func=AF.Exp, scale=lg[:, h:h + 1]), [prow, lg], [DqT])
            op("pool", lambda e, h=h: e.memset(St32[h][:], 0.0), [], [St32[h]])
            op("pool", lambda e, h=h: e.memset(Stb[h][:], 0.0), [], [Stb[h]])

        def state_update(St32, Stb, psU, h, kall, vht, gcol):
            S32h = St32[h]
            Sbh = Stb[h]
            for dkc in range(2):
                pu = psU[dkc]
                op("pe", lambda e, dkc=dkc, pu=pu: e.matmul(pu[:], lhsT=kall[:, h * 256 + dkc * 128:h * 256 + (dkc + 1) * 128], rhs=vht, start=True, stop=True),
                   [kall, vht_T[0]], [pu])
                op("dve", lambda e, dkc=dkc, pu=pu: e.scalar_tensor_tensor(out=S32h[:, dkc, :], in0=S32h[:, dkc, :], scalar=gc[:, gcol:gcol + 1],
                                                                          in1=pu[:], op0=ALU.mult, op1=ALU.add), [S32h, gc, pu], [S32h])
            op("act", lambda e: e.activation(out=Sbh[:], in_=S32h[:], func=AF.Copy), [S32h], [Sbh])

        vht_T = [None]
        psX = [psO, psL]
        pend = []
        bctr = [0]
        sctr = [0]
        for sb in range(NSB):
            load_xT(xsrc, sb * SBT, SBT, xT)
            for kind in range(2):
                for cb in range(4):
                    w = load_w(win, kind * 2048 + cb * 512, 512, 16)
                    for j in range(SBT):
                        tok0 = (sb * SBT + j) * 128
                        bctr[0] += 1
                        bc = bctr[0]
                        ps = psA[bc % 2]
                        proj_tok(ps, xT, j, w)
                        if kind == 0:
                            src_t = gh[bc % 2]
                            src_ap = src_t[:]
                            rope256(ps, src_ap, src_t, 1.0, tok0)
                            dstT = qTc[j]
                        else:
                            src_t, src_ap = kc[j], kc[j][:, cb * 512:(cb + 1) * 512]
                            rope256(ps, src_ap, kc[j], 1.0 / 16.0, tok0)
                            dstT = kTc[j]
                        while pend:
                            pend.pop(0)()

                        def fin(src_t=src_t, src_ap=src_ap, dstT=dstT, bc=bc, cb=cb):
                            pst = psT[bc % 2]

                            def f(e):
                                for dkc in range(2):
                                    for hl in range(2):
                                        i = dkc * 2 + hl
                                        ins = e.transpose(out=pst[:, i * 128:(i + 1) * 128], in_=src_ap[:, hl * 256 + dkc * 128:hl * 256 + (dkc + 1) * 128],
                                                          identity=identb[:])
                                return ins
                            op("pe", f, [src_t, identb], [pst])
                            evac(dstT[:, :, 2 * cb:2 * cb + 2, :], pst[:, 0:512].rearrange("p (a b t) -> p a b t", a=2, b=2), [pst], [dstT])
                        pend.append(fin)
            while pend:
                pend.pop(0)()
            for j in range(SBT):
                c = sb * SBT + j
                dma("pool", r_qT[c], qTc[j][:].rearrange("p a b t -> p (a b t)"), [qTc[j]], [dbuf(r_qT)])
                dma("pool", r_k[c * 128:(c + 1) * 128, :], kc[j][:], [kc[j]], [dbuf(r_k)])
                op("dve", lambda e, j=j: e.tensor_tensor(out=qTc[j][:], in0=qTc[j][:], in1=DqT[:].unsqueeze(1).to_broadcast([128, 2, 8, 128]), op=ALU.mult),
                   [qTc[j], DqT], [qTc[j]])
                op("dve", lambda e, j=j: e.tensor_tensor(out=kc[j][:].rearrange("p (h d) -> p h d", d=256), in0=kc[j][:].rearrange("p (h d) -> p h d", d=256),
                                                         in1=kdf[:].unsqueeze(2).to_broadcast([128, 8, 256]), op=ALU.mult), [kc[j], kdf], [kc[j]])
            for h in range(8):
                w = load_w(win, 4096 + h * 512, 512, 16)
                for j in range(SBT):
                    ps = psA[j % 2]
                    proj_tok(ps, xT, j, w)
                    evac(vall[j][:, h * 512:(h + 1) * 512], ps[:], [ps], [vall[j]])
            for j in range(SBT):
                tok0 = (sb * SBT + j) * 128
                dma("pool", r_v[tok0:tok0 + 128, :], vall[j][:], [vall[j]], [dbuf(r_v)])
            steps = [(j, h) for j in range(SBT) for h in range(8)]

            def step_S(j, h, k_):
                psS = psB[k_ % 2]

                def f(e):
                    for dkc in range(2):
                        ins = e.matmul(psS[:, 0:128], lhsT=kTc[j][:, dkc, h, :], rhs=qTc[j][:, dkc, h, :], start=(dkc == 0), stop=(dkc == 1))
                    return ins
                op("pe", f, [kTc[j], qTc[j]], [psS])
                at = AT[k_ % 2]
                op("dve", lambda e: e.tensor_tensor(out=at[:], in0=psS[:, 0:128], in1=DcT[h][:], op=ALU.mult), [psS, DcT[h]], [at])

            def step_rest(j, h, k_):
                tok0 = (sb * SBT + j) * 128
                vht = vall[j][:, h * 512:(h + 1) * 512]
                vht_T[0] = vall[j]
                at = AT[k_ % 2]
                acc = psX[k_ % 2]
                sb_ = Stb[h]

                def f2(e):
                    e.matmul(acc[:], lhsT=at[:], rhs=vht, start=True, stop=False)
                    e.matmul(acc[:], lhsT=qTc[j][:, 0, h, :], rhs=sb_[:, 0, :], start=False, stop=False)
                    return e.matmul(acc[:], lhsT=qTc[j][:, 1, h, :], rhs=sb_[:, 1, :], start=False, stop=True)
                op("pe", f2, [at, vall[j], qTc[j], Stb[h]], [acc])
                ob = o1buf[k_ % 2]
                evac(ob[:], acc[:], [acc], [ob])
                dma("pool", r_o1[tok0:tok0 + 128, h * 512:(h + 1) * 512], ob[:], [ob], [dbuf(r_o1)])
                psU = psA if k_ % 2 else psTf
                state_update(St32, Stb, psU, h, kc[j], vht, h)

            k0 = sctr[0]
            step_S(steps[0][0], steps[0][1], k0)
            for i_, (j, h) in enumerate(steps):
                if i_ + 1 < len(steps):
                    step_S(steps[i_ + 1][0], steps[i_ + 1][1], k0 + i_ + 1)
                step_rest(j, h, k0 + i_)
            sctr[0] += len(steps)
            for h in range(8):
                w = load_w(win, 8192 + h * 512, 512, 16)
                for j in range(SBT):
                    tok0 = (sb * SBT + j) * 128
                    ps = psA[j % 2]
                    proj_tok(ps, xT, j, w)
                    g_ = gh[j % 2]
                    op("act", lambda e, g_=g_, ps=ps: e.activation(out=g_[:], in_=ps[:], func=AF.Silu), [ps], [g_])
                    dma("pool", r_g[tok0:tok0 + 128, h * 512:(h + 1) * 512], g_[:], [g_], [dbuf(r_g)])

        new_phase()
        St32b = [T("St32_%d" % h, [128, 2, 512], F32) for h in range(8)]
        Stbb = [T("Stb_%d" % h, [128, 2, 512], BF16) for h in range(8)]
        DqbT = T("DqbT", [128, 8, 128], F32)
        qT1 = T("qT1", [128, 2, 8, 128], BF16)
        kc1 = T("kc1", [128, 2048], BF16)
        vh2 = [T("vh%d" % i, [128, 512], BF16) for i in range(2)]
        gh2 = [T("gh%d" % i, [128, 512], BF16) for i in range(2)]
        og = T("og", [128, 4096], BF16)
        ogT = T("ogT", [128, 32, SBT * 128], BF16)
        o1b = [T("o1b%d" % i, [128, 512], F32) for i in range(2)]
        ot = [T("ot%d" % i, [128, 512], F32) for i in range(2)]
        rbuf = T("rbuf", [128, SBT, D], F32)
        gst = [sm("gn_stats%d" % i, [128, 6]) for i in range(2)]
        gmv = [sm("gn_mv%d" % i, [128, 2]) for i in range(2)]
        grs = [sm("gn_rstd%d" % i, [128, 1]) for i in range(2)]
        gnm = [sm("gn_nmr%d" % i, [128, 1]) for i in range(2)]
        for h in range(8):
            op("act", lambda e, h=h: e.activation(out=DqbT[:, h, :], in_=prow[:, 128:256], func=AF.Exp, scale=lg[:, 8 + h:9 + h]), [prow, lg], [DqbT])
            op("pool", lambda e, h=h: e.memset(St32b[h][:], 0.0), [], [St32b[h]])
            op("pool", lambda e, h=h: e.memset(Stbb[h][:], 0.0), [], [Stbb[h]])
        k2 = 0
        for sb in reversed(range(NSB)):
            for j in reversed(range(SBT)):
                c = sb * SBT + j
                tok0 = c * 128
                dma("sp", qT1[:].rearrange("p a b t -> p (a b t)"), r_qT[c], [dbuf(r_qT)], [qT1])
                dma("sp", kc1[:], r_k[tok0:tok0 + 128, :], [dbuf(r_k)], [kc1])
                op("dve", lambda e: e.tensor_tensor(out=qT1[:], in0=qT1[:], in1=DqbT[:].unsqueeze(1).to_broadcast([128, 2, 8, 128]), op=ALU.mult),
                   [qT1, DqbT], [qT1])
                op("dve", lambda e: e.tensor_tensor(out=kc1[:].rearrange("p (h d) -> p h d", d=256), in0=kc1[:].rearrange("p (h d) -> p h d", d=256),
                                                    in1=kdb[:].unsqueeze(2).to_broadcast([128, 8, 256]), op=ALU.mult), [kc1, kdb], [kc1])
                for h in range(8):
                    k2 += 1
                    v_ = vh2[k2 % 2]
                    g_ = gh2[k2 % 2]
                    o_ = o1b[k2 % 2]
                    t_ = ot[k2 % 2]
                    gst_, gmv_, grs_, gnm_ = gst[k2 % 2], gmv[k2 % 2], grs[k2 % 2], gnm[k2 % 2]
                    dma("sp", v_[:], r_v[tok0:tok0 + 128, h * 512:(h + 1) * 512], [dbuf(r_v)], [v_])
                    dma("sp", g_[:], r_g[tok0:tok0 + 128, h * 512:(h + 1) * 512], [dbuf(r_g)], [g_])
                    dma("sp", o_[:], r_o1[tok0:tok0 + 128, h * 512:(h + 1) * 512], [dbuf(r_o1)], [o_])
                    acc = psX[k2 % 2]

                    def f2(e, h=h, sb_=Stbb[h], acc=acc):
                        for dkc in range(2):
                            ins = e.matmul(acc[:], lhsT=qT1[:, dkc, h, :], rhs=sb_[:, dkc, :], start=(dkc == 0), stop=(dkc == 1))
                        return ins
                    op("pe", f2, [qT1, Stbb[h]], [acc])
                    op("dve", lambda e, t_=t_, o_=o_, acc=acc: e.tensor_tensor(out=t_[:], in0=acc[:], in1=o_[:], op=ALU.add), [acc, o_], [t_])
                    op("dve", lambda e, t_=t_, gst_=gst_: e.bn_stats(out=gst_[:], in_=t_[:]), [t_], [gst_])
                    op("dve", lambda e, gst_=gst_, gmv_=gmv_: e.bn_aggr(out=gmv_[:], in_=gst_[:]), [gst_], [gmv_])
                    op("dve", lambda e, gmv_=gmv_, grs_=grs_: e.tensor_scalar(out=grs_[:], in0=gmv_[:, 1:2], scalar1=LN_EPS, scalar2=None, op0=ALU.add), [gmv_], [grs_])
                    op("act", lambda e, grs_=grs_: e.activation(out=grs_[:], in_=grs_[:], func=AF.Sqrt), [grs_], [grs_])
                    op("dve", lambda e, grs_=grs_: e.reciprocal(out=grs_[:], in_=grs_[:]), [grs_], [grs_])
                    op("dve", lambda e, gmv_=gmv_, grs_=grs_, gnm_=gnm_: e.scalar_tensor_tensor(out=gnm_[:], in0=gmv_[:, 0:1], scalar=-1.0, in1=grs_[:], op0=ALU.mult, op1=ALU.mult),
                       [gmv_, grs_], [gnm_])
                    op("act", lambda e, t_=t_, gnm_=gnm_, grs_=grs_: e.activation(out=t_[:], in_=t_[:], func=AF.Identity, bias=gnm_[:, 0:1], scale=grs_[:, 0:1]),
                       [t_, gnm_, grs_], [t_])
                    op("pool", lambda e, t_=t_, g_=g_, h=h: e.tensor_tensor(out=og[:, h * 512:(h + 1) * 512], in0=t_[:], in1=g_[:], op=ALU.mult), [t_, g_], [og])
                    vht_T[0] = v_
                    psU = psA if k2 % 2 else psB
                    state_update(St32b, Stbb, psU, h, kc1, v_[:], 8 + h)
                for q8 in range(4):
                    pst = psT[q8 % 2]

                    def f(e, q8=q8, pst=pst):
                        for i in range(8):
                            kk = q8 * 8 + i
                            ins = e.transpose(out=pst[:, i * 128:(i + 1) * 128], in_=og[:, kk * 128:(kk + 1) * 128], identity=identb[:])
                        return ins
                    op("pe", f, [og, identb], [pst])
                    evac(ogT[:, q8 * 8:(q8 + 1) * 8, j * 128:(j + 1) * 128], pst[:].rearrange("p (a t) -> p a t", t=128), [pst], [ogT])
            for cb in range(8):
                w = load_w(wout, cb * 256, 256, 32)
                for j in range(SBT):
                    tok0 = (sb * SBT + j) * 128
                    ps = psA[j % 2]

                    def f(e, ps=ps, w=w, j=j):
                        for kk in range(32):
                            ins = e.matmul(ps[:, 0:256], lhsT=ogT[:, kk, j * 128:(j + 1) * 128], rhs=w[:, kk, :], start=(kk == 0), stop=(kk == 31))
                        return ins
                    op("pe", f, [ogT, w], [ps])
                    st = xstg[xctr[0] % 2]
                    xctr[0] += 1
                    dma("sp", st[:, 0:256], xsrc[tok0:tok0 + 128, cb * 256:(cb + 1) * 256], [dbuf(xsrc)], [st])
                    op("dve", lambda e, ps=ps, st=st, j=j, cb=cb: e.scalar_tensor_tensor(out=rbuf[:, j, cb * 256:(cb + 1) * 256], in0=st[:, 0:256], scalar=ALPHA,
                                                                                          in1=ps[:, 0:256], op0=ALU.mult, op1=ALU.add), [st, ps], [rbuf])
            for j in range(SBT):
                layer_norm_out(li, rbuf, j, (sb * SBT + j) * 128, xdst)

    ja = jr = 0
    for l in layers:
        if l == "a":
            cast_w(a_win[ja], a_win_b[ja], D, A_IN)
            cast_w(a_wout[ja], a_wout_b[ja], D, D)
            ja += 1
        else:
            cast_w(r_win[jr], r_win_b[jr], D, R_IN)
            cast_w(r_wout[jr], r_wout_b[jr], 2 * D, D)
            jr += 1
    ja = jr = 0
    cur = x_in
    for li, l in enumerate(layers):
        dst = y_out if li == nL - 1 else xs[li % 2]
        if l == "a":
            attention_layer(li, ja, cur, dst)
            ja += 1
        else:
            retention_layer(li, jr, cur, dst)
            jr += 1
        cur = dst

    sc.finish()
    sc.emit_all(nc, stack)
    stack.close()
    return nc, sc


def make_consts(S):
    cosa, sina = rope_tables(S, 128)
    cosr, sinr = rope_tables(S, 256)
    n = np.arange(128, dtype=np.float32)
    diff = n[None, :] - n[:, None]
    p1 = np.maximum(diff, 0).astype(np.float32)
    p2 = np.maximum(-diff, 0).astype(np.float32)
    posv = np.stack([n + 1, 128 - n, 127 - n, n], axis=1).astype(np.float32)
    prow = np.concatenate([np.tile((n + 1)[None, :], (128, 1)), np.tile((128 - n)[None, :], (128, 1))], axis=1).astype(np.float32)
    return dict(cosa=cosa, sina=sina, cosr=cosr, sinr=sinr, identf_d=np.eye(128, dtype=np.float32), p1_d=p1, p2_d=p2, posv_d=posv, prow_d=prow)


def run(x, attn_w_in, attn_q_gain, attn_k_gain, attn_w_out, ret_w_in, ret_decay_exp, ret_w_out, ln_gain, ln_bias, layers):
    B, S, _ = x.shape
    nc, sc = build_program(S, layers)
    consts = make_consts(S)
    n_a = max(1, sum(1 for l in layers if l == "a"))
    n_r = max(1, sum(1 for l in layers if l == "r"))
    nL = len(layers)
    shared = dict(a_win=np.ascontiguousarray(attn_w_in[:n_a]), a_wout=np.ascontiguousarray(attn_w_out[:n_a]),
                  a_qg=np.ascontiguousarray(attn_q_gain[:n_a]), a_kg=np.ascontiguousarray(attn_k_gain[:n_a]),
                  r_win=np.ascontiguousarray(ret_w_in[:n_r]), r_wout=np.ascontiguousarray(ret_w_out[:n_r]),
                  r_dec=np.ascontiguousarray(ret_decay_exp[:n_r].reshape(n_r, 16)),
                  ln_g=np.ascontiguousarray(ln_gain[:nL]), ln_b=np.ascontiguousarray(ln_bias[:nL]), **consts)
    in_maps = [dict(x=np.ascontiguousarray(x[b]), **shared) for b in range(B)]
    res = run_bass_kernel_spmd(nc, in_maps, core_ids=list(range(B)))
    return np.stack([res.results[b]["y"] for b in range(B)], axis=0)


def kernel(x, attn_w_in, attn_q_gain, attn_k_gain, attn_w_out, ret_w_in, ret_decay_exp, ret_w_out, ln_gain, ln_bias):
    out = run(np.asarray(x), np.asarray(attn_w_in), np.asarray(attn_q_gain), np.asarray(attn_k_gain), np.asarray(attn_w_out),
              np.asarray(ret_w_in), np.asarray(ret_decay_exp), np.asarray(ret_w_out), np.asarray(ln_gain), np.asarray(ln_bias),
              layers=["a", "r", "a", "r"])
    return out.astype(np.float32)
```

```python
import math
from contextlib import ExitStack
import numpy as np
import concourse.bass as bass
import concourse.mybir as mybir
from concourse.bass_utils import run_bass_kernel_spmd

F32 = mybir.dt.float32
BF16 = mybir.dt.bfloat16
AF = mybir.ActivationFunctionType
ALU = mybir.AluOpType
AX = mybir.AxisListType

D = 2048
GRID_W = 64
ROPE_THETA = 10000.0
RMS_EPS = 1e-6
LN_EPS = 1e-5
DEPTH = 4
ALPHA = (2.0 * DEPTH) ** 0.25
A_IN = 5120
R_IN = 12288
NCORES = 4


class Buf:
    __slots__ = ("name", "w", "r")

    def __init__(self, name):
        self.name = name
        self.w = None
        self.r = []


class Sched:
    ENG = ("pe", "act", "dve", "pool", "sp")
    NSLOT = 6

    def __init__(self):
        self.streams = {e: [] for e in self.ENG}
        self.cnt = {e: 0 for e in self.ENG}
        self.waited = {e: {} for e in self.ENG}
        self.dma_n = {e: 0 for e in self.ENG}
        self.dma_tok = {e: [None] * self.NSLOT for e in self.ENG}
        self.dma_cnt = {e: [0] * self.NSLOT for e in self.ENG}
        self.nops = 0

    def _need(self, eng, tok, waits):
        if tok is None:
            return
        key, val = tok
        if key == eng and eng in ("pe", "sp"):
            return
        if self.waited[eng].get(key, 0) >= val:
            return
        self.waited[eng][key] = val
        waits.append((key, val))

    def op(self, eng, emit, reads=(), writes=(), dma=False):
        waits = []
        for b in reads:
            self._need(eng, b.w, waits)
        for b in writes:
            self._need(eng, b.w, waits)
            for t in b.r:
                if t[0] == eng and not dma:
                    continue
                self._need(eng, t, waits)
        if dma:
            n = self.dma_n[eng]
            self.dma_n[eng] = n + 1
            slot = n % self.NSLOT
            self._need(eng, self.dma_tok[eng][slot], waits)
            self.dma_cnt[eng][slot] += 16
            tok = (("dma", eng, slot), self.dma_cnt[eng][slot])
            self.dma_tok[eng][slot] = tok
            sig = (tok[0], 16)
        else:
            self.cnt[eng] += 1
            tok = (eng, self.cnt[eng])
            sig = (eng, 1)
        for b in reads:
            b.r.append(tok)
        for b in writes:
            b.w = tok
            b.r = []
        self.streams[eng].append((waits, emit, sig))
        self.nops += 1
        return tok

    def fence(self):
        toks = [(e, self.cnt[e]) for e in ("pe", "act", "dve", "pool") if self.cnt[e] > 0]
        for e in self.ENG:
            for slot in range(self.NSLOT):
                if self.dma_tok[e][slot] is not None:
                    toks.append(self.dma_tok[e][slot])
        for e in self.ENG:
            waits = []
            for t in toks:
                self._need(e, t, waits)
            if waits:
                self.streams[e].append((waits, None, None))

    def finish(self):
        for eng in self.ENG:
            waits = []
            for slot in range(self.NSLOT):
                self._need(eng, self.dma_tok[eng][slot], waits)
            if waits:
                self.streams[eng].append((waits, None, None))

    def emit_all(self, nc, stack):
        sems = {}

        def sem(key):
            if key not in sems:
                nm = "s_" + ("_".join(str(k) for k in key) if isinstance(key, tuple) else key)
                sems[key] = stack.enter_context(nc.semaphore(nm))
            return sems[key]

        for e in self.ENG:
            sem(e)
            for s in range(self.NSLOT):
                if self.dma_cnt[e][s]:
                    sem(("dma", e, s))
        block = stack.enter_context(nc.Block())
        sect = {"pe": block.tensor, "act": block.scalar, "dve": block.vector, "pool": block.gpsimd, "sp": block.sync}

        def mk(stream):
            def body(engine):
                for waits, emit, sig in stream:
                    for key, val in waits:
                        engine.wait_ge(sems[key], val)
                    if emit is not None:
                        ins = emit(engine)
                        ins.then_inc(sems[sig[0]], sig[1])
            return body

        for e in self.ENG:
            if self.streams[e]:
                sect[e](mk(self.streams[e]))


def rope_tables(S, hd):
    rows_n = S // GRID_W
    row = np.repeat(np.arange(rows_n, dtype=np.float32), GRID_W)
    col = np.tile(np.arange(GRID_W, dtype=np.float32), rows_n)
    half = hd // 2
    inv_freq = (np.float32(ROPE_THETA) ** (-np.arange(0, half, 2, dtype=np.float32) / np.float32(half))).astype(np.float32)
    ang_r = row[:, None] * inv_freq
    ang_c = col[:, None] * inv_freq
    ang = np.concatenate([ang_r, ang_r, ang_c, ang_c], axis=-1).astype(np.float32)
    cos = np.cos(ang).astype(np.float32)
    sin = np.sin(ang).astype(np.float32)
    q4 = hd // 4
    sgn = np.concatenate([-np.ones(q4), np.ones(q4), -np.ones(q4), np.ones(q4)]).astype(np.float32)
    return cos, (sin * sgn[None, :]).astype(np.float32)


def build_program(S, layers):
    NT = S // 128
    n_a = sum(1 for l in layers if l == "a")
    n_r = sum(1 for l in layers if l == "r")
    nL = len(layers)
    nc = bass.Bass("TRN2", target_bir_lowering=False)
    sc = Sched()

    def din(name, shape, dt=F32):
        return nc.dram_tensor(name, list(shape), dt, kind="ExternalInput").ap()

    def dint(name, shape, dt):
        return nc.dram_tensor(name, list(shape), dt, kind="Internal").ap()

    x_in = din("x", [S, D])
    y_out = nc.dram_tensor("y", [S, D], F32, kind="ExternalOutput").ap()
    a_win = din("a_win", [max(n_a, 1), D, A_IN])
    a_wout = din("a_wout", [max(n_a, 1), D, D])
    a_qg = din("a_qg", [max(n_a, 1), 128])
    a_kg = din("a_kg", [max(n_a, 1), 128])
    r_win = din("r_win", [max(n_r, 1), D, R_IN])
    r_wout = din("r_wout", [max(n_r, 1), 2 * D, D])
    r_dec = din("r_dec", [max(n_r, 1), 16])
    ln_g = din("ln_g", [nL, D])
    ln_b = din("ln_b", [nL, D])
    cosa_d = din("cosa", [S, 128])
    sina_d = din("sina", [S, 128])
    cosr_d = din("cosr", [S, 256])
    sinr_d = din("sinr", [S, 256])
    identf_d = din("identf_d", [128, 128])
    p1_d = din("p1_d", [128, 128])
    p2_d = din("p2_d", [128, 128])
    posv_d = din("posv_d", [128, 4])
    prow_d = din("prow_d", [128, 256])

    a_win_b = dint("a_win_b", [max(n_a, 1), A_IN // 512, 128, 16, 512], BF16)
    a_wout_b = dint("a_wout_b", [max(n_a, 1), D // 512, 128, 16, 512], BF16)
    r_win_b = dint("r_win_b", [max(n_r, 1), R_IN // 512, 128, 16, 512], BF16)
    r_wout_b = dint("r_wout_b", [max(n_r, 1), D // 256, 128, 32, 256], BF16)
    xs = [dint("xs0", [S, D], F32), dint("xs1", [S, D], F32)]
    if n_r:
        r_qT = dint("r_qT", [NT, 128, 2 * 8 * 128], BF16)
        r_k = dint("r_k", [S, 2048], BF16)
        r_v = dint("r_v", [S, 4096], BF16)
        r_g = dint("r_g", [S, 4096], BF16)
        r_o1 = dint("r_o1", [S, 4096], F32)

    stack = ExitStack()
    DB = {}

    def dbuf(ap):
        k = ap.name if hasattr(ap, "name") else id(ap)
        if k not in DB:
            DB[k] = Buf(str(k))
        return DB[k]

    NA = 29696
    arena = stack.enter_context(nc.sbuf_tensor("arena", [128, NA], F32))
    aoff = [0]

    class T:
        def __init__(self, name, shape, dt, psum=False, persist=False, view=None, buf=None):
            if view is not None:
                self.t = view
                self.b = buf
                return
            shape = list(shape)
            if psum:
                self.t = stack.enter_context(nc.psum_tensor(name, shape, dt))
            elif persist:
                self.t = stack.enter_context(nc.sbuf_tensor(name, shape, dt))
            else:
                n = 1
                for d_ in shape[1:]:
                    n *= d_
                nw = n if dt == F32 else (n + 1) // 2
                nw = (nw + 7) // 8 * 8
                assert aoff[0] + nw <= NA, ("arena overflow", name, aoff[0], nw)
                v = arena[:, aoff[0]:aoff[0] + nw]
                aoff[0] += nw
                if dt != F32:
                    v = v.bitcast(dt)
                v = v[:, 0:n]
                if len(shape) == 3:
                    v = v.rearrange("p (a b) -> p a b", b=shape[2])
                elif len(shape) == 4:
                    v = v.rearrange("p (a b c) -> p a b c", b=shape[2], c=shape[3])
                self.t = v
            self.b = Buf(name)

        def __getitem__(self, k):
            return self.t[k]

    def new_phase():
        sc.fence()
        aoff[0] = 0

    def op(eng, fn, reads=(), writes=(), dma=False):
        rb = [x.b if isinstance(x, T) else x for x in reads]
        wb = [x.b if isinstance(x, T) else x for x in writes]
        return sc.op(eng, fn, rb, wb, dma)

    def dma(q, out, in_, reads, writes):
        op(q, lambda e: e.dma_start(out=out, in_=in_), reads, writes, dma=True)

    identf = T("identf", [128, 128], F32, persist=True)
    identb = T("identb", [128, 128], BF16, persist=True)
    onesb = T("onesb", [128, 128], BF16, persist=True)
    dma("sp", identf[:], identf_d[:, :], [], [identf])
    op("dve", lambda e: e.tensor_copy(out=identb[:], in_=identf[:]), [identf], [identb])
    op("dve", lambda e: e.memset(onesb[:], 1.0), [], [onesb])

    def cast_w(src, dst, rows, cols, cw):
        for kc in range(rows // 128):
            s_ = src[kc * 128:(kc + 1) * 128, :].rearrange("p (cb c) -> p cb c", c=cw)
            d_ = dst[:, :, kc, :].rearrange("cb p c -> p cb c")
            dma("pool", d_, s_, [], [dbuf(dst)])

    psA2 = stack.enter_context(nc.psum_tensor("psA2", [128, 1024], F32))
    psB2 = stack.enter_context(nc.psum_tensor("psB2", [128, 1024], F32))
    psA = [T(None, None, None, view=psA2[:, i * 512:(i + 1) * 512], buf=Buf("psA%d" % i)) for i in range(2)]
    psB = [T(None, None, None, view=psB2[:, i * 512:(i + 1) * 512], buf=Buf("psB%d" % i)) for i in range(2)]
    psW = [(psA2, psA), (psB2, psB)]
    psT = [T("psT0", [128, 1024], BF16, True), T("psT1", [128, 1024], BF16, True)]
    psTf = [T(None, None, None, view=psT[i].t[:].bitcast(F32), buf=psT[i].b) for i in range(2)]
    psO = T("psO", [128, 512], F32, True)
    psL = T("psL", [128, 512], F32, True)

    wbuf = [T("wbuf0", [128, 16, 512], BF16, persist=True), T("wbuf1", [128, 16, 512], BF16, persist=True)]
    wctr = [0]
    xstg = [T("xstg0", [128, 512], F32, persist=True), T("xstg1", [128, 512], F32, persist=True)]
    xctr = [0]
    lng = [T("lng0", [128, 512], F32, persist=True), T("lng1", [128, 512], F32, persist=True)]
    lnb = [T("lnb0", [128, 512], F32, persist=True), T("lnb1", [128, 512], F32, persist=True)]
    small = {}

    def sm(name, shape, dt=F32):
        if name not in small:
            small[name] = T(name, shape, dt, persist=True)
        return small[name]

    evac_ctr = [0]

    def evac(out, in_, reads, writes):
        evac_ctr[0] += 1
        if evac_ctr[0] % 2:
            op("act", lambda e: e.activation(out=out, in_=in_, func=AF.Copy), reads, writes)
        else:
            op("dve", lambda e: e.tensor_copy(out=out, in_=in_), reads, writes)

    def load_w(src2d, c0, ncol, nk):
        w = wbuf[wctr[0] % 2]
        wctr[0] += 1
        assert nk * ncol <= 16 * 512
        view = w.t[:].rearrange("p a b -> p (a b)")[:, 0:nk * ncol].rearrange("p (a b) -> p a b", b=ncol)
        dma("sp", view, src2d[c0 // ncol], [dbuf(src2d)], [w])

        return T(None, None, None, view=view, buf=w.b)

    def load_xT(xsrc, t0, ntile, xT):
        for j in range(ntile):
            tok0 = (t0 + j) * 128
            for c4 in range(4):
                st = xstg[xctr[0] % 2]
                xctr[0] += 1
                dma("sp", st[:], xsrc[tok0:tok0 + 128, c4 * 512:(c4 + 1) * 512], [dbuf(xsrc)], [st])
                ps = psB[c4 % 2]

                def f(e, ps=ps, st=st):
                    for i in range(4):
                        ins = e.transpose(out=ps[:, i * 128:(i + 1) * 128], in_=st[:, i * 128:(i + 1) * 128], identity=identf[:])
                    return ins
                op("pe", f, [st, identf], [ps])
                evac(xT[:, c4 * 4:(c4 + 1) * 4, j * 128:(j + 1) * 128],
                     ps[:].rearrange("p (a b) -> p a b", b=128), [ps], [xT])

    def proj_tok(ps, xT, j, w, nk=16, ncol=512):
        def f(e):
            for kc in range(nk):
                ins = e.matmul(ps[:, 0:ncol], lhsT=xT[:, kc, j * 128:(j + 1) * 128], rhs=w[:, kc, 0:ncol],
                               start=(kc == 0), stop=(kc == nk - 1))
            return ins
        op("pe", f, [xT, w], [ps])

    def layer_norm_out(li, rbuf, j, tok0, xdst):
        stats = sm("ln_stats", [128, 4, 6])
        mv = sm("ln_mv", [128, 2])
        rstd = sm("ln_rstd", [128, 1])
        nmr = sm("ln_nmr", [128, 1])
        for c in range(4):
            op("dve", lambda e, c=c: e.bn_stats(out=stats[:, c, :], in_=rbuf[:, j, c * 512:(c + 1) * 512]), [rbuf], [stats])
        op("dve", lambda e: e.bn_aggr(out=mv[:], in_=stats[:].rearrange("p a b -> p (a b)")), [stats], [mv])
        op("dve", lambda e: e.tensor_scalar(out=rstd[:], in0=mv[:, 1:2], scalar1=LN_EPS, scalar2=None, op0=ALU.add), [mv], [rstd])
        op("act", lambda e: e.activation(out=rstd[:], in_=rstd[:], func=AF.Sqrt), [rstd], [rstd])
        op("dve", lambda e: e.reciprocal(out=rstd[:], in_=rstd[:]), [rstd], [rstd])
        op("dve", lambda e: e.scalar_tensor_tensor(out=nmr[:], in0=mv[:, 0:1], scalar=-1.0, in1=rstd[:], op0=ALU.mult, op1=ALU.mult),
           [mv, rstd], [nmr])
        op("act", lambda e: e.activation(out=rbuf[:, j, :], in_=rbuf[:, j, :], func=AF.Identity, bias=nmr[:, 0:1], scale=rstd[:, 0:1]),
           [rbuf, nmr, rstd], [rbuf])
        for c in range(4):
            g_ = lng[c % 2]
            b_ = lnb[c % 2]
            dma("sp", g_[:], ln_g[li:li + 1, c * 512:(c + 1) * 512].partition_broadcast(128), [], [g_])
            dma("sp", b_[:], ln_b[li:li + 1, c * 512:(c + 1) * 512].partition_broadcast(128), [], [b_])
            op("pool", lambda e, c=c, g_=g_: e.tensor_tensor(out=rbuf[:, j, c * 512:(c + 1) * 512], in0=rbuf[:, j, c * 512:(c + 1) * 512],
                                                             in1=g_[:], op=ALU.mult), [rbuf, g_], [rbuf])
            op("dve", lambda e, c=c, b_=b_: e.tensor_tensor(out=rbuf[:, j, c * 512:(c + 1) * 512], in0=rbuf[:, j, c * 512:(c + 1) * 512],
                                                            in1=b_[:], op=ALU.add), [rbuf, b_], [rbuf])
        dma("pool", xdst[tok0:tok0 + 128, :], rbuf[:, j, :], [rbuf], [dbuf(xdst)])

    def attention_layer(li, ja, xsrc, xdst):
        new_phase()
        SBT = 2
        NSB = NT // SBT
        win = a_win_b[ja]
        wout = a_wout_b[ja]
        KT = T("KT", [128, 4, S], BF16)
        V = T("V", [128, NT, 512], BF16)
        xT = T("xT", [128, 16, SBT * 128], BF16)
        qT = T("qT", [128, 16, SBT * 128], BF16)
        gT = T("gT", [128, 16, SBT * 128], BF16)
        rbuf = T("rbuf", [128, SBT, D], F32)
        PT = [T("PT%d" % i, [128, 1024], BF16) for i in range(3)]
        cosb = [sm("cosb0", [128, 128]), sm("cosb1", [128, 128])]
        sinb = [sm("sinb0", [128, 128]), sm("sinb1", [128, 128])]
        gq = sm("gq_rep", [128, 128])
        gk = sm("gk_rep", [128, 128])
        negb = sm("negb", [128, 1])
        mq = sm("mq", [128, 1])
        mk_ = sm("mk", [128, 1])
        sq = sm("sq", [128, 512])
        ss = sm("ss", [128, 4])
        rs = sm("rs", [128, 4])
        qg = sm("qg", [128, 512])
        t1 = sm("t1", [128, 512])
        t2 = sm("t2", [128, 512])
        qbs = [sm("qb0", [128, 512], BF16), sm("qb1", [128, 512], BF16)]
        pend = []
        rl = sm("rl", [128, 512])
        tmpo = sm("tmpo", [128, 512])

        dma("sp", gq[:], a_qg[ja:ja + 1, :].partition_broadcast(128), [], [gq])
        dma("sp", gk[:], a_kg[ja:ja + 1, :].partition_broadcast(128), [], [gk])
        op("dve", lambda e: e.tensor_reduce(out=mq[:], in_=gq[:], axis=AX.X, op=ALU.max, apply_absolute_value=True), [gq], [mq])
        op("dve", lambda e: e.tensor_reduce(out=mk_[:], in_=gk[:], axis=AX.X, op=ALU.max, apply_absolute_value=True), [gk], [mk_])
        op("dve", lambda e: e.scalar_tensor_tensor(out=negb[:], in0=mq[:], scalar=-math.sqrt(128.0), in1=mk_[:], op0=ALU.mult, op1=ALU.mult),
           [mq, mk_], [negb])

        def qk_post(ps, gain, scale, tok0, cs, qb):
            cb_, sb_ = cosb[cs % 2], sinb[cs % 2]
            op("act", lambda e: e.activation(out=sq[:], in_=ps[:], func=AF.Square), [ps], [sq])
            op("dve", lambda e: e.tensor_reduce(out=ss[:], in_=sq[:].rearrange("p (h d) -> p h d", d=128), axis=AX.X, op=ALU.add), [sq], [ss])
            op("dve", lambda e: e.tensor_scalar(out=rs[:], in0=ss[:], scalar1=1.0 / (128.0 * scale * scale), scalar2=RMS_EPS / (scale * scale),
                                                op0=ALU.mult, op1=ALU.add), [ss], [rs])
            op("act", lambda e: e.activation(out=rs[:], in_=rs[:], func=AF.Sqrt), [rs], [rs])
            op("dve", lambda e: e.reciprocal(out=rs[:], in_=rs[:]), [rs], [rs])
            op("dve", lambda e: e.tensor_tensor(out=qg[:].rearrange("p (h d) -> p h d", d=128), in0=ps[:].rearrange("p (h d) -> p h d", d=128),
                                                in1=gain[:].unsqueeze(1).to_broadcast([128, 4, 128]), op=ALU.mult), [ps, gain], [qg])
            op("dve", lambda e: e.tensor_tensor(out=t1[:].rearrange("p (h d) -> p h d", d=128), in0=qg[:].rearrange("p (h d) -> p h d", d=128),
                                                in1=cb_[:].unsqueeze(1).to_broadcast([128, 4, 128]), op=ALU.mult), [qg, cb_], [t1])
            qv = qg[:].rearrange("p (g s d) -> p g s d", s=2, d=32)
            tv = t2[:].rearrange("p (g s d) -> p g s d", s=2, d=32)
            sv = sb_[:].rearrange("p (g s d) -> p g s d", s=2, d=32)
            qv5 = qg[:].rearrange("p (h g s d) -> p h g s d", h=4, g=2, s=2, d=32)
            tv5 = t2[:].rearrange("p (h g s d) -> p h g s d", h=4, g=2, s=2, d=32)
            for s_ in range(2):
                op("dve", lambda e, s_=s_: e.tensor_tensor(out=tv5[:, :, :, s_, :], in0=qv5[:, :, :, 1 - s_, :],
                                                            in1=sv[:, :, s_, :].unsqueeze(1).to_broadcast([128, 4, 2, 32]), op=ALU.mult), [qg, sb_], [t2])
            op("dve", lambda e: e.tensor_tensor(out=t1[:], in0=t1[:], in1=t2[:], op=ALU.add), [t1, t2], [t1])
            op("dve", lambda e: e.tensor_tensor(out=qb[:].rearrange("p (h d) -> p h d", d=128), in0=t1[:].rearrange("p (h d) -> p h d", d=128),
                                                in1=rs[:].unsqueeze(2).to_broadcast([128, 4, 128]), op=ALU.mult), [t1, rs], [qb])

        def load_tables(tok0, cs):
            dma("sp", cosb[cs % 2][:], cosa_d[tok0:tok0 + 128, :], [], [cosb[cs % 2]])
            dma("sp", sinb[cs % 2][:], sina_d[tok0:tok0 + 128, :], [], [sinb[cs % 2]])

        def transpose4(qb, pst):
            def f(e):
                for h in range(4):
                    ins = e.transpose(out=pst[:, h * 128:(h + 1) * 128], in_=qb[:, h * 128:(h + 1) * 128], identity=identb[:])
                return ins
            op("pe", f, [qb, identb], [pst])

        cs = 0
        for sb in range(NSB):
            load_xT(xsrc, sb * SBT, SBT, xT)
            wk = load_w(win, 2048, 512, 16)
            wv = load_w(win, 2560, 512, 16)
            for j in range(SBT):
                t = sb * SBT + j
                tok0 = t * 128
                load_tables(tok0, cs)
                ps = psA[t % 2]
                proj_tok(ps, xT, j, wk)
                qb = qbs[cs % 2]
                qk_post(ps, gk, 1.0, tok0, cs, qb)
                cs += 1
                ps = (psO, psL)[t % 2]
                proj_tok(ps, xT, j, wv)
                evac(V[:, t, :], ps[:], [ps], [V])
                while pend:
                    pend.pop(0)()

                def fin(qb=qb, t=t, tok0=tok0):
                    pst = psT[t % 2]
                    transpose4(qb, pst)
                    evac(KT[:, :, tok0:tok0 + 128], pst[:, 0:512].rearrange("p (h t) -> p h t", t=128), [pst], [KT])
                pend.append(fin)
        while pend:
            pend.pop(0)()

        for sb in range(NSB):
            load_xT(xsrc, sb * SBT, SBT, xT)
            for cb in range(4):
                w = load_w(win, cb * 512, 512, 16)
                for j in range(SBT):
                    tok0 = (sb * SBT + j) * 128
                    load_tables(tok0, cs)
                    ps = psA[cs % 2]
                    proj_tok(ps, xT, j, w)
                    qb = qbs[cs % 2]
                    qk_post(ps, gq, 128.0 ** -0.5, tok0, cs, qb)
                    while pend:
                        pend.pop(0)()

                    def fin(qb=qb, cs_=cs, cb=cb, j=j):
                        pst = psT[cs_ % 2]
                        transpose4(qb, pst)
                        evac(qT[:, cb * 4:(cb + 1) * 4, j * 128:(j + 1) * 128], pst[:, 0:512].rearrange("p (h t) -> p h t", t=128), [pst], [qT])
                    pend.append(fin)
                    cs += 1
            for cb in range(4):
                w = load_w(win, 3072 + cb * 512, 512, 16)
                for h in range(4):
                    ps = psA[h % 2]

                    def f(e, ps=ps, w=w, h=h):
                        for kc in range(16):
                            ins = e.matmul(ps[:, 0:SBT * 128], lhsT=w[:, kc, h * 128:(h + 1) * 128], rhs=xT[:, kc, :],
                                           start=(kc == 0), stop=(kc == 15))
                        return ins
                    op("pe", f, [xT, w], [ps])
                    op("act", lambda e, ps=ps, cb=cb, h=h: e.activation(out=gT[:, cb * 4 + h, :], in_=ps[:, 0:SBT * 128], func=AF.Silu), [ps], [gT])
            while pend:
                pend.pop(0)()
            for j in range(SBT):
                for g in range(4):
                    rhs_q = qT[:, g * 4:(g + 1) * 4, j * 128:(j + 1) * 128]

                    def s_mm(i, g=g, rhs_q=rhs_q):
                        ps2, halves = psW[i % 2]

                        def f(e):
                            for u_ in range(2):
                                kt = 2 * i + u_
                                ins = e.matmul(halves[u_][:], lhsT=KT[:, g, kt * 128:(kt + 1) * 128], rhs=rhs_q, start=True, stop=True)
                            return ins
                        op("pe", f, [KT, qT], [halves[0], halves[1]])
                        pt = PT[i % 3]
                        op("act", lambda e: e.activation(out=pt[:], in_=ps2[:, :], func=AF.Exp, bias=negb[:, 0:1], scale=1.0), [halves[0], halves[1], negb], [pt])

                    def pv_mm(i, g=g):
                        pt = PT[i % 3]

                        def f(e):
                            for u_ in range(2):
                                kt = 2 * i + u_
                                e.matmul(psO[:], lhsT=V[:, kt, g * 128:(g + 1) * 128], rhs=pt[:, u_ * 512:(u_ + 1) * 512], start=(kt == 0), stop=(kt == NT - 1))
                                ins = e.matmul(psL[:], lhsT=onesb[:], rhs=pt[:, u_ * 512:(u_ + 1) * 512], start=(kt == 0), stop=(kt == NT - 1))
                            return ins
                        op("pe", f, [V, pt, onesb], [psO, psL])

                    NP = NT // 2
                    for i in range(NP):
                        s_mm(i)
                        if i >= 1:
                            pv_mm(i - 1)
                    pv_mm(NP - 1)
                    op("dve", lambda e: e.reciprocal(out=rl[:], in_=psL[:]), [psL], [rl])
                    op("dve", lambda e: e.tensor_tensor(out=tmpo[:], in0=psO[:], in1=rl[:], op=ALU.mult), [psO, rl], [tmpo])
                    gview = gT[:, g * 4:(g + 1) * 4, j * 128:(j + 1) * 128]
                    op("dve", lambda e, gview=gview: e.tensor_tensor(out=gview, in0=gview, in1=tmpo[:].rearrange("p (h t) -> p h t", t=128), op=ALU.mult),
                       [gT, tmpo], [gT])
            for cb in range(4):
                w = load_w(wout, cb * 512, 512, 16)
                for j in range(SBT):
                    tok0 = (sb * SBT + j) * 128
                    ps = psA[j % 2]

                    def f(e, ps=ps, w=w, j=j):
                        for h in range(16):
                            ins = e.matmul(ps[:], lhsT=gT[:, h, j * 128:(j + 1) * 128], rhs=w[:, h, :], start=(h == 0), stop=(h == 15))
                        return ins
                    op("pe", f, [gT, w], [ps])
                    st = xstg[xctr[0] % 2]
                    xctr[0] += 1
                    dma("sp", st[:], xsrc[tok0:tok0 + 128, cb * 512:(cb + 1) * 512], [dbuf(xsrc)], [st])
                    op("dve", lambda e, ps=ps, st=st, j=j, cb=cb: e.scalar_tensor_tensor(out=rbuf[:, j, cb * 512:(cb + 1) * 512], in0=st[:], scalar=ALPHA,
                                                                                          in1=ps[:], op0=ALU.mult, op1=ALU.add), [st, ps], [rbuf])
            for j in range(SBT):
                layer_norm_out(li, rbuf, j, (sb * SBT + j) * 128, xdst)

    def retention_layer(li, jr, xsrc, xdst):
        SBT = 2
        NSB = NT // SBT
        win = r_win_b[jr]
        wout = r_wout_b[jr]
        LN2 = math.log(2.0)

        posv = sm("posv", [128, 4])
        p1 = sm("p1", [128, 128])
        p2 = sm("p2", [128, 128])
        dec = sm("dec_rep", [128, 16])
        u = sm("dec_u", [128, 16])
        tt = sm("dec_t", [128, 16])
        lg = sm("dec_lg", [128, 16])
        dqf = sm("dqf", [128, 8])
        dqb = sm("dqb", [128, 8])
        kdf = sm("kdf", [128, 8])
        kdb = sm("kdb", [128, 8])
        gc = sm("gc", [128, 16])
        arg = sm("dc_arg", [128, 128])
        dma("sp", posv[:], posv_d[:, :], [], [posv])
        dma("sp", p1[:], p1_d[:, :], [], [p1])
        dma("sp", p2[:], p2_d[:, :], [], [p2])
        dma("sp", dec[:], r_dec[jr:jr + 1, :].partition_broadcast(128), [], [dec])
        op("act", lambda e: e.activation(out=u[:], in_=dec[:], func=AF.Exp, scale=-LN2), [dec], [u])
        op("dve", lambda e: e.tensor_scalar(out=tt[:], in0=u[:], scalar1=0.25, scalar2=1.0 / 3.0, op0=ALU.mult, op1=ALU.add), [u], [tt])
        op("dve", lambda e: e.tensor_tensor(out=tt[:], in0=tt[:], in1=u[:], op=ALU.mult), [tt, u], [tt])
        op("dve", lambda e: e.tensor_scalar(out=tt[:], in0=tt[:], scalar1=0.5, scalar2=None, op0=ALU.add), [tt], [tt])
        op("dve", lambda e: e.tensor_tensor(out=tt[:], in0=tt[:], in1=u[:], op=ALU.mult), [tt, u], [tt])
        op("dve", lambda e: e.tensor_scalar(out=tt[:], in0=tt[:], scalar1=1.0, scalar2=None, op0=ALU.add), [tt], [tt])
        op("dve", lambda e: e.scalar_tensor_tensor(out=lg[:], in0=tt[:], scalar=-1.0, in1=u[:], op0=ALU.mult, op1=ALU.mult), [tt, u], [lg])
        op("act", lambda e: e.activation(out=dqf[:], in_=lg[:, 0:8], func=AF.Exp, scale=posv[:, 0:1]), [lg, posv], [dqf])
        op("act", lambda e: e.activation(out=dqb[:], in_=lg[:, 8:16], func=AF.Exp, scale=posv[:, 1:2]), [lg, posv], [dqb])
        op("act", lambda e: e.activation(out=kdf[:], in_=lg[:, 0:8], func=AF.Exp, scale=posv[:, 2:3]), [lg, posv], [kdf])
        op("act", lambda e: e.activation(out=kdb[:], in_=lg[:, 8:16], func=AF.Exp, scale=posv[:, 3:4]), [lg, posv], [kdb])
        op("act", lambda e: e.activation(out=gc[:], in_=lg[:], func=AF.Exp, scale=128.0), [lg], [gc])

        cosr = [sm("cosr0", [128, 256]), sm("cosr1", [128, 256])]
        sinr = [sm("sinr0", [128, 256]), sm("sinr1", [128, 256])]
        t1 = sm("t1", [128, 512])
        t2 = sm("t2", [128, 512])
        cs = [0]

        def rope256(ps, out_ap, outT, scale, tok0):
            c_ = cosr[cs[0] % 2]
            s_ = sinr[cs[0] % 2]
            cs[0] += 1
            dma("sp", c_[:], cosr_d[tok0:tok0 + 128, :], [], [c_])
            dma("sp", s_[:], sinr_d[tok0:tok0 + 128, :], [], [s_])
            op("dve", lambda e: e.tensor_tensor(out=t1[:].rearrange("p (h d) -> p h d", d=256), in0=ps[:].rearrange("p (h d) -> p h d", d=256),
                                                in1=c_[:].unsqueeze(1).to_broadcast([128, 2, 256]), op=ALU.mult), [ps, c_], [t1])
            pv5 = ps[:].rearrange("p (h g s d) -> p h g s d", h=2, g=2, s=2, d=64)
            tv5 = t2[:].rearrange("p (h g s d) -> p h g s d", h=2, g=2, s=2, d=64)
            sv = s_[:].rearrange("p (g s d) -> p g s d", s=2, d=64)
            for k_ in range(2):
                op("dve", lambda e, k_=k_: e.tensor_tensor(out=tv5[:, :, :, k_, :], in0=pv5[:, :, :, 1 - k_, :],
                                                           in1=sv[:, :, k_, :].unsqueeze(1).to_broadcast([128, 2, 2, 64]), op=ALU.mult), [ps, s_], [t2])
            if scale == 1.0:
                op("dve", lambda e: e.tensor_tensor(out=out_ap, in0=t1[:], in1=t2[:], op=ALU.add), [t1, t2], [outT])
            else:
                op("dve", lambda e: e.tensor_tensor(out=t1[:], in0=t1[:], in1=t2[:], op=ALU.add), [t1, t2], [t1])
                op("act", lambda e: e.activation(out=out_ap, in_=t1[:], func=AF.Copy, scale=scale), [t1], [outT])

        new_phase()
        prow = sm("prow", [128, 256])
        dma("sp", prow[:], prow_d[:, :], [], [prow])
        St32 = [T("St32_%d" % h, [128, 2, 512], F32) for h in range(8)]
        Stb = [T("Stb_%d" % h, [128, 2, 512], BF16) for h in range(8)]
        DcT = [T("DcT_%d" % h, [128, 128], F32) for h in range(8)]
        DqT = T("DqfT", [128, 8, 128], F32)
        xT = T("xT", [128, 16, SBT * 128], BF16)
        qTc = [T("qTc%d" % j, [128, 2, 8, 128], BF16) for j in range(SBT)]
        kTc = [T("kTc%d" % j, [128, 2, 8, 128], BF16) for j in range(SBT)]
        kc = [T("kc%d" % j, [128, 2048], BF16) for j in range(SBT)]
        vall = [T("vall%d" % j, [128, 4096], BF16) for j in range(SBT)]
        AT = [T("AT%d" % i, [128, 128], BF16) for i in range(2)]
        gh = [T("gh%d" % i, [128, 512], BF16) for i in range(2)]
        o1buf = [T("o1buf%d" % i, [128, 512], F32) for i in range(2)]
        arg2 = sm("dc_arg2", [128, 128])

        for h in range(8):
            op("dve", lambda e, h=h: e.tensor_scalar(out=arg[:], in0=p1[:], scalar1=lg[:, h:h + 1], scalar2=None, op0=ALU.mult), [p1, lg], [arg])
            op("dve", lambda e, h=h: e.scalar_tensor_tensor(out=arg[:], in0=p2[:], scalar=lg[:, 8 + h:9 + h], in1=arg[:], op0=ALU.mult, op1=ALU.add),
               [p2, lg, arg], [arg])
            op("dve", lambda e, h=h: e.tensor_scalar(out=arg2[:], in0=prow[:, 0:128], scalar1=lg[:, h:h + 1], scalar2=None, op0=ALU.mult), [prow, lg], [arg2])
            op("dve", lambda e: e.tensor_tensor(out=arg[:], in0=arg[:], in1=arg2[:], op=ALU.subtract), [arg, arg2], [arg])
            op("act", lambda e, h=h: e.activation(out=DcT[h][:], in_=arg[:], func=AF.Exp), [arg], [DcT[h]])
            op("act", lambda e, h=h: e.activation(out=DqT[:, h, :], in_=prow[:, 0:128], func=AF.Exp, scale=lg[:, h:h + 1]), [prow, lg], [DqT])
            op("pool", lambda e, h=h: e.memset(St32[h][:], 0.0), [], [St32[h]])
            op("pool", lambda e, h=h: e.memset(Stb[h][:], 0.0), [], [Stb[h]])

        def state_update(St32, Stb, psU, h, kall, vht, gcol):
            S32h = St32[h]
            Sbh = Stb[h]
            for dkc in range(2):
                pu = psU[dkc]
                op("pe", lambda e, dkc=dkc, pu=pu: e.matmul(pu[:], lhsT=kall[:, h * 256 + dkc * 128:h * 256 + (dkc + 1) * 128], rhs=vht, start=True, stop=True),
                   [kall, vht_T[0]], [pu])
                op("dve", lambda e, dkc=dkc, pu=pu: e.scalar_tensor_tensor(out=S32h[:, dkc, :], in0=S32h[:, dkc, :], scalar=gc[:, gcol:gcol + 1],
                                                                          in1=pu[:], op0=ALU.mult, op1=ALU.add), [S32h, gc, pu], [S32h])
            op("act", lambda e: e.activation(out=Sbh[:], in_=S32h[:], func=AF.Copy), [S32h], [Sbh])

        vht_T = [None]
        psX = [psO, psL]
        pend = []
        bctr = [0]
        sctr = [0]
        for sb in range(NSB):
            load_xT(xsrc, sb * SBT, SBT, xT)
            for kind in range(2):
                for cb in range(4):
                    w = load_w(win, kind * 2048 + cb * 512, 512, 16)
                    for j in range(SBT):
                        tok0 = (sb * SBT + j) * 128
                        bctr[0] += 1
                        bc = bctr[0]
                        ps = psA[bc % 2]
                        proj_tok(ps, xT, j, w)
                        if kind == 0:
                            src_t = gh[bc % 2]
                            src_ap = src_t[:]
                            rope256(ps, src_ap, src_t, 1.0, tok0)
                            dstT = qTc[j]
                        else:
                            src_t, src_ap = kc[j], kc[j][:, cb * 512:(cb + 1) * 512]
                            rope256(ps, src_ap, kc[j], 1.0 / 16.0, tok0)
                            dstT = kTc[j]
                        while pend:
                            pend.pop(0)()

                        def fin(src_t=src_t, src_ap=src_ap, dstT=dstT, bc=bc, cb=cb):
                            pst = psT[bc % 2]

                            def f(e):
                                for dkc in range(2):
                                    for hl in range(2):
                                        i = dkc * 2 + hl
                                        ins = e.transpose(out=pst[:, i * 128:(i + 1) * 128], in_=src_ap[:, hl * 256 + dkc * 128:hl * 256 + (dkc + 1) * 128],
                                                          identity=identb[:])
                                return ins
                            op("pe", f, [src_t, identb], [pst])
                            evac(dstT[:, :, 2 * cb:2 * cb + 2, :], pst[:, 0:512].rearrange("p (a b t) -> p a b t", a=2, b=2), [pst], [dstT])
                        pend.append(fin)
            while pend:
                pend.pop(0)()
            for j in range(SBT):
                c = sb * SBT + j
                dma("pool", r_qT[c], qTc[j][:].rearrange("p a b t -> p (a b t)"), [qTc[j]], [dbuf(r_qT)])
                dma("pool", r_k[c * 128:(c + 1) * 128, :], kc[j][:], [kc[j]], [dbuf(r_k)])
                op("dve", lambda e, j=j: e.tensor_tensor(out=qTc[j][:], in0=qTc[j][:], in1=DqT[:].unsqueeze(1).to_broadcast([128, 2, 8, 128]), op=ALU.mult),
                   [qTc[j], DqT], [qTc[j]])
                op("dve", lambda e, j=j: e.tensor_tensor(out=kc[j][:].rearrange("p (h d) -> p h d", d=256), in0=kc[j][:].rearrange("p (h d) -> p h d", d=256),
                                                         in1=kdf[:].unsqueeze(2).to_broadcast([128, 8, 256]), op=ALU.mult), [kc[j], kdf], [kc[j]])
            for h in range(8):
                w = load_w(win, 4096 + h * 512, 512, 16)
                for j in range(SBT):
                    ps = psA[j % 2]
                    proj_tok(ps, xT, j, w)
                    evac(vall[j][:, h * 512:(h + 1) * 512], ps[:], [ps], [vall[j]])
            for j in range(SBT):
                tok0 = (sb * SBT + j) * 128
                dma("pool", r_v[tok0:tok0 + 128, :], vall[j][:], [vall[j]], [dbuf(r_v)])
            steps = [(j, h) for j in range(SBT) for h in range(8)]

            def step_S(j, h, k_):
                psS = psB[k_ % 2]

                def f(e):
                    for dkc in range(2):
                        ins = e.matmul(psS[:, 0:128], lhsT=kTc[j][:, dkc, h, :], rhs=qTc[j][:, dkc, h, :], start=(dkc == 0), stop=(dkc == 1))
                    return ins
                op("pe", f, [kTc[j], qTc[j]], [psS])
                at = AT[k_ % 2]
                op("dve", lambda e: e.tensor_tensor(out=at[:], in0=psS[:, 0:128], in1=DcT[h][:], op=ALU.mult), [psS, DcT[h]], [at])

            def step_rest(j, h, k_):
                tok0 = (sb * SBT + j) * 128
                vht = vall[j][:, h * 512:(h + 1) * 512]
                vht_T[0] = vall[j]
                at = AT[k_ % 2]
                acc = psX[k_ % 2]
                sb_ = Stb[h]

                def f2(e):
                    e.matmul(acc[:], lhsT=at[:], rhs=vht, start=True, stop=False)
                    e.matmul(acc[:], lhsT=qTc[j][:, 0, h, :], rhs=sb_[:, 0, :], start=False, stop=False)
                    return e.matmul(acc[:], lhsT=qTc[j][:, 1, h, :], rhs=sb_[:, 1, :], start=False, stop=True)
                op("pe", f2, [at, vall[j], qTc[j], Stb[h]], [acc])
                ob = o1buf[k_ % 2]
                evac(ob[:], acc[:], [acc], [ob])
                dma("pool", r_o1[tok0:tok0 + 128, h * 512:(h + 1) * 512], ob[:], [ob], [dbuf(r_o1)])
                psU = psA if k_ % 2 else psTf
                state_update(St32, Stb, psU, h, kc[j], vht, h)

            k0 = sctr[0]
            step_S(steps[0][0], steps[0][1], k0)
            for i_, (j, h) in enumerate(steps):
                if i_ + 1 < len(steps):
                    step_S(steps[i_ + 1][0], steps[i_ + 1][1], k0 + i_ + 1)
                step_rest(j, h, k0 + i_)
            sctr[0] += len(steps)
            for h in range(8):
                w = load_w(win, 8192 + h * 512, 512, 16)
                for j in range(SBT):
                    tok0 = (sb * SBT + j) * 128
                    ps = psA[j % 2]
                    proj_tok(ps, xT, j, w)
                    g_ = gh[j % 2]
                    op("act", lambda e, g_=g_, ps=ps: e.activation(out=g_[:], in_=ps[:], func=AF.Silu), [ps], [g_])
                    dma("pool", r_g[tok0:tok0 + 128, h * 512:(h + 1) * 512], g_[:], [g_], [dbuf(r_g)])

        new_phase()
        St32b = [T("St32_%d" % h, [128, 2, 512], F32) for h in range(8)]
        Stbb = [T("Stb_%d" % h, [128, 2, 512], BF16) for h in range(8)]
        DqbT = T("DqbT", [128, 8, 128], F32)
        qT1 = T("qT1", [128, 2, 8, 128], BF16)
        kc1 = T("kc1", [128, 2048], BF16)
        vh2 = [T("vh%d" % i, [128, 512], BF16) for i in range(2)]
        gh2 = [T("gh%d" % i, [128, 512], BF16) for i in range(2)]
        og = T("og", [128, 4096], BF16)
        ogT = T("ogT", [128, 32, SBT * 128], BF16)
        o1b = [T("o1b%d" % i, [128, 512], F32) for i in range(2)]
        ot = [T("ot%d" % i, [128, 512], F32) for i in range(2)]
        rbuf = T("rbuf", [128, SBT, D], F32)
        gst = [sm("gn_stats%d" % i, [128, 6]) for i in range(2)]
        gmv = [sm("gn_mv%d" % i, [128, 2]) for i in range(2)]
        grs = [sm("gn_rstd%d" % i, [128, 1]) for i in range(2)]
        gnm = [sm("gn_nmr%d" % i, [128, 1]) for i in range(2)]
        for h in range(8):
            op("act", lambda e, h=h: e.activation(out=DqbT[:, h, :], in_=prow[:, 128:256], func=AF.Exp, scale=lg[:, 8 + h:9 + h]), [prow, lg], [DqbT])
            op("pool", lambda e, h=h: e.memset(St32b[h][:], 0.0), [], [St32b[h]])
            op("pool", lambda e, h=h: e.memset(Stbb[h][:], 0.0), [], [Stbb[h]])
        k2 = 0
        for sb in reversed(range(NSB)):
            for j in reversed(range(SBT)):
                c = sb * SBT + j
                tok0 = c * 128
                dma("sp", qT1[:].rearrange("p a b t -> p (a b t)"), r_qT[c], [dbuf(r_qT)], [qT1])
                dma("sp", kc1[:], r_k[tok0:tok0 + 128, :], [dbuf(r_k)], [kc1])
                op("dve", lambda e: e.tensor_tensor(out=qT1[:], in0=qT1[:], in1=DqbT[:].unsqueeze(1).to_broadcast([128, 2, 8, 128]), op=ALU.mult),
                   [qT1, DqbT], [qT1])
                op("dve", lambda e: e.tensor_tensor(out=kc1[:].rearrange("p (h d) -> p h d", d=256), in0=kc1[:].rearrange("p (h d) -> p h d", d=256),
                                                    in1=kdb[:].unsqueeze(2).to_broadcast([128, 8, 256]), op=ALU.mult), [kc1, kdb], [kc1])
                for h in range(8):
                    k2 += 1
                    v_ = vh2[k2 % 2]
                    g_ = gh2[k2 % 2]
                    o_ = o1b[k2 % 2]
                    t_ = ot[k2 % 2]
                    gst_, gmv_, grs_, gnm_ = gst[k2 % 2], gmv[k2 % 2], grs[k2 % 2], gnm[k2 % 2]
                    dma("sp", v_[:], r_v[tok0:tok0 + 128, h * 512:(h + 1) * 512], [dbuf(r_v)], [v_])
                    dma("sp", g_[:], r_g[tok0:tok0 + 128, h * 512:(h + 1) * 512], [dbuf(r_g)], [g_])
                    dma("sp", o_[:], r_o1[tok0:tok0 + 128, h * 512:(h + 1) * 512], [dbuf(r_o1)], [o_])
                    acc = psX[k2 % 2]

                    def f2(e, h=h, sb_=Stbb[h], acc=acc):
                        for dkc in range(2):
                            ins = e.matmul(acc[:], lhsT=qT1[:, dkc, h, :], rhs=sb_[:, dkc, :], start=(dkc == 0), stop=(dkc == 1))
                        return ins
                    op("pe", f2, [qT1, Stbb[h]], [acc])
                    op("dve", lambda e, t_=t_, o_=o_, acc=acc: e.tensor_tensor(out=t_[:], in0=acc[:], in1=o_[:], op=ALU.add), [acc, o_], [t_])
                    op("dve", lambda e, t_=t_, gst_=gst_: e.bn_stats(out=gst_[:], in_=t_[:]), [t_], [gst_])
                    op("dve", lambda e, gst_=gst_, gmv_=gmv_: e.bn_aggr(out=gmv_[:], in_=gst_[:]), [gst_], [gmv_])
                    op("dve", lambda e, gmv_=gmv_, grs_=grs_: e.tensor_scalar(out=grs_[:], in0=gmv_[:, 1:2], scalar1=LN_EPS, scalar2=None, op0=ALU.add), [gmv_], [grs_])
                    op("act", lambda e, grs_=grs_: e.activation(out=grs_[:], in_=grs_[:], func=AF.Sqrt), [grs_], [grs_])
                    op("dve", lambda e, grs_=grs_: e.reciprocal(out=grs_[:], in_=grs_[:]), [grs_], [grs_])
                    op("dve", lambda e, gmv_=gmv_, grs_=grs_, gnm_=gnm_: e.scalar_tensor_tensor(out=gnm_[:], in0=gmv_[:, 0:1], scalar=-1.0, in1=grs_[:], op0=ALU.mult, op1=ALU.mult),
                       [gmv_, grs_], [gnm_])
                    op("act", lambda e, t_=t_, gnm_=gnm_, grs_=grs_: e.activation(out=t_[:], in_=t_[:], func=AF.Identity, bias=gnm_[:, 0:1], scale=grs_[:, 0:1]),
                       [t_, gnm_, grs_], [t_])
                    op("pool", lambda e, t_=t_, g_=g_, h=h: e.tensor_tensor(out=og[:, h * 512:(h + 1) * 512], in0=t_[:], in1=g_[:], op=ALU.mult), [t_, g_], [og])
                    vht_T[0] = v_
                    psU = psA if k2 % 2 else psB
                    state_update(St32b, Stbb, psU, h, kc1, v_[:], 8 + h)
                for q8 in range(4):
                    pst = psT[q8 % 2]

                    def f(e, q8=q8, pst=pst):
                        for i in range(8):
                            kk = q8 * 8 + i
                            ins = e.transpose(out=pst[:, i * 128:(i + 1) * 128], in_=og[:, kk * 128:(kk + 1) * 128], identity=identb[:])
                        return ins
                    op("pe", f, [og, identb], [pst])
                    evac(ogT[:, q8 * 8:(q8 + 1) * 8, j * 128:(j + 1) * 128], pst[:].rearrange("p (a t) -> p a t", t=128), [pst], [ogT])
            for cb in range(8):
                w = load_w(wout, cb * 256, 256, 32)
                for j in range(SBT):
                    tok0 = (sb * SBT + j) * 128
                    ps = psA[j % 2]

                    def f(e, ps=ps, w=w, j=j):
                        for kk in range(32):
                            ins = e.matmul(ps[:, 0:256], lhsT=ogT[:, kk, j * 128:(j + 1) * 128], rhs=w[:, kk, :], start=(kk == 0), stop=(kk == 31))
                        return ins
                    op("pe", f, [ogT, w], [ps])
                    st = xstg[xctr[0] % 2]
                    xctr[0] += 1
                    dma("sp", st[:, 0:256], xsrc[tok0:tok0 + 128, cb * 256:(cb + 1) * 256], [dbuf(xsrc)], [st])
                    op("dve", lambda e, ps=ps, st=st, j=j, cb=cb: e.scalar_tensor_tensor(out=rbuf[:, j, cb * 256:(cb + 1) * 256], in0=st[:, 0:256], scalar=ALPHA,
                                                                                          in1=ps[:, 0:256], op0=ALU.mult, op1=ALU.add), [st, ps], [rbuf])
            for j in range(SBT):
                layer_norm_out(li, rbuf, j, (sb * SBT + j) * 128, xdst)

    ja = jr = 0
    for l in layers:
        if l == "a":
            cast_w(a_win[ja], a_win_b[ja], D, A_IN, 512)
            cast_w(a_wout[ja], a_wout_b[ja], D, D, 512)
            ja += 1
        else:
            cast_w(r_win[jr], r_win_b[jr], D, R_IN, 512)
            cast_w(r_wout[jr], r_wout_b[jr], 2 * D, D, 256)
            jr += 1
    ja = jr = 0
    cur = x_in
    for li, l in enumerate(layers):
        dst = y_out if li == nL - 1 else xs[li % 2]
        if l == "a":
            attention_layer(li, ja, cur, dst)
            ja += 1
        else:
            retention_layer(li, jr, cur, dst)
            jr += 1
        cur = dst

    sc.finish()
    sc.emit_all(nc, stack)
    stack.close()
    return nc, sc


def make_consts(S):
    cosa, sina = rope_tables(S, 128)
    cosr, sinr = rope_tables(S, 256)
    n = np.arange(128, dtype=np.float32)
    diff = n[None, :] - n[:, None]
    p1 = np.maximum(diff, 0).astype(np.float32)
    p2 = np.maximum(-diff, 0).astype(np.float32)
    posv = np.stack([n + 1, 128 - n, 127 - n, n], axis=1).astype(np.float32)
    prow = np.concatenate([np.tile((n + 1)[None, :], (128, 1)), np.tile((128 - n)[None, :], (128, 1))], axis=1).astype(np.float32)
    return dict(cosa=cosa, sina=sina, cosr=cosr, sinr=sinr, identf_d=np.eye(128, dtype=np.float32), p1_d=p1, p2_d=p2, posv_d=posv, prow_d=prow)


def run(x, attn_w_in, attn_q_gain, attn_k_gain, attn_w_out, ret_w_in, ret_decay_exp, ret_w_out, ln_gain, ln_bias, layers):
    B, S, _ = x.shape
    nc, sc = build_program(S, layers)
    consts = make_consts(S)
    n_a = max(1, sum(1 for l in layers if l == "a"))
    n_r = max(1, sum(1 for l in layers if l == "r"))
    nL = len(layers)
    shared = dict(a_win=np.ascontiguousarray(attn_w_in[:n_a]), a_wout=np.ascontiguousarray(attn_w_out[:n_a]),
                  a_qg=np.ascontiguousarray(attn_q_gain[:n_a]), a_kg=np.ascontiguousarray(attn_k_gain[:n_a]),
                  r_win=np.ascontiguousarray(ret_w_in[:n_r]), r_wout=np.ascontiguousarray(ret_w_out[:n_r]),
                  r_dec=np.ascontiguousarray(ret_decay_exp[:n_r].reshape(n_r, 16)),
                  ln_g=np.ascontiguousarray(ln_gain[:nL]), ln_b=np.ascontiguousarray(ln_bias[:nL]), **consts)
    in_maps = [dict(x=np.ascontiguousarray(x[b]), **shared) for b in range(B)]
    res = run_bass_kernel_spmd(nc, in_maps, core_ids=list(range(B)))
    return np.stack([res.results[b]["y"] for b in range(B)], axis=0)


def kernel(x, attn_w_in, attn_q_gain, attn_k_gain, attn_w_out, ret_w_in, ret_decay_exp, ret_w_out, ln_gain, ln_bias):
    out = run(np.asarray(x), np.asarray(attn_w_in), np.asarray(attn_q_gain), np.asarray(attn_k_gain), np.asarray(attn_w_out),
              np.asarray(ret_w_in), np.asarray(ret_decay_exp), np.asarray(ret_w_out), np.asarray(ln_gain), np.asarray(ln_bias),
              layers=["a", "r", "a", "r"])
    return out.astype(np.float32)
```

```python
import math
from contextlib import ExitStack
import numpy as np
import concourse.bass as bass
import concourse.mybir as mybir
from concourse.bass_utils import run_bass_kernel_spmd

F32 = mybir.dt.float32
BF16 = mybir.dt.bfloat16
AF = mybir.ActivationFunctionType
ALU = mybir.AluOpType
AX = mybir.AxisListType

D = 2048
GRID_W = 64
ROPE_THETA = 10000.0
RMS_EPS = 1e-6
LN_EPS = 1e-5
DEPTH = 4
ALPHA = (2.0 * DEPTH) ** 0.25
A_IN = 5120
R_IN = 12288
NCORES = 4


class Buf:
    __slots__ = ("name", "w", "r")

    def __init__(self, name):
        self.name = name
        self.w = None
        self.r = []


class Sched:
    ENG = ("pe", "act", "dve", "pool", "sp")
    NSLOT = 6

    def __init__(self):
        self.streams = {e: [] for e in self.ENG}
        self.cnt = {e: 0 for e in self.ENG}
        self.waited = {e: {} for e in self.ENG}
        self.dma_n = {e: 0 for e in self.ENG}
        self.dma_tok = {e: [None] * self.NSLOT for e in self.ENG}
        self.dma_cnt = {e: [0] * self.NSLOT for e in self.ENG}
        self.nops = 0

    def _need(self, eng, tok, waits):
        if tok is None:
            return
        key, val = tok
        if key == eng and eng in ("pe", "sp"):
            return
        if self.waited[eng].get(key, 0) >= val:
            return
        self.waited[eng][key] = val
        waits.append((key, val))

    def op(self, eng, emit, reads=(), writes=(), dma=False):
        waits = []
        for b in reads:
            self._need(eng, b.w, waits)
        for b in writes:
            self._need(eng, b.w, waits)
            for t in b.r:
                if t[0] == eng and not dma:
                    continue
                self._need(eng, t, waits)
        if dma:
            n = self.dma_n[eng]
            self.dma_n[eng] = n + 1
            slot = n % self.NSLOT
            self._need(eng, self.dma_tok[eng][slot], waits)
            self.dma_cnt[eng][slot] += 16
            tok = (("dma", eng, slot), self.dma_cnt[eng][slot])
            self.dma_tok[eng][slot] = tok
            sig = (tok[0], 16)
        else:
            self.cnt[eng] += 1
            tok = (eng, self.cnt[eng])
            sig = (eng, 1)
        for b in reads:
            b.r.append(tok)
        for b in writes:
            b.w = tok
            b.r = []
        self.streams[eng].append((waits, emit, sig))
        self.nops += 1
        return tok

    def fence(self):
        toks = [(e, self.cnt[e]) for e in ("pe", "act", "dve", "pool") if self.cnt[e] > 0]
        for e in self.ENG:
            for slot in range(self.NSLOT):
                if self.dma_tok[e][slot] is not None:
                    toks.append(self.dma_tok[e][slot])
        for e in self.ENG:
            waits = []
            for t in toks:
                self._need(e, t, waits)
            if waits:
                self.streams[e].append((waits, None, None))

    def finish(self):
        for eng in self.ENG:
            waits = []
            for slot in range(self.NSLOT):
                self._need(eng, self.dma_tok[eng][slot], waits)
            if waits:
                self.streams[eng].append((waits, None, None))

    def emit_all(self, nc, stack):
        sems = {}

        def sem(key):
            if key not in sems:
                nm = "s_" + ("_".join(str(k) for k in key) if isinstance(key, tuple) else key)
                sems[key] = stack.enter_context(nc.semaphore(nm))
            return sems[key]

        for e in self.ENG:
            sem(e)
            for s in range(self.NSLOT):
                if self.dma_cnt[e][s]:
                    sem(("dma", e, s))
        block = stack.enter_context(nc.Block())
        sect = {"pe": block.tensor, "act": block.scalar, "dve": block.vector, "pool": block.gpsimd, "sp": block.sync}

        def mk(stream):
            def body(engine):
                for waits, emit, sig in stream:
                    for key, val in waits:
                        engine.wait_ge(sems[key], val)
                    if emit is not None:
                        ins = emit(engine)
                        ins.then_inc(sems[sig[0]], sig[1])
            return body

        for e in self.ENG:
            if self.streams[e]:
                sect[e](mk(self.streams[e]))


def rope_tables(S, hd):
    rows_n = S // GRID_W
    row = np.repeat(np.arange(rows_n, dtype=np.float32), GRID_W)
    col = np.tile(np.arange(GRID_W, dtype=np.float32), rows_n)
    half = hd // 2
    inv_freq = (np.float32(ROPE_THETA) ** (-np.arange(0, half, 2, dtype=np.float32) / np.float32(half))).astype(np.float32)
    ang_r = row[:, None] * inv_freq
    ang_c = col[:, None] * inv_freq
    ang = np.concatenate([ang_r, ang_r, ang_c, ang_c], axis=-1).astype(np.float32)
    cos = np.cos(ang).astype(np.float32)
    sin = np.sin(ang).astype(np.float32)
    q4 = hd // 4
    sgn = np.concatenate([-np.ones(q4), np.ones(q4), -np.ones(q4), np.ones(q4)]).astype(np.float32)
    return cos, (sin * sgn[None, :]).astype(np.float32)


def build_program(S, layers):
    NT = S // 128
    n_a = sum(1 for l in layers if l == "a")
    n_r = sum(1 for l in layers if l == "r")
    nL = len(layers)
    nc = bass.Bass("TRN2", target_bir_lowering=False)
    sc = Sched()

    def din(name, shape, dt=F32):
        return nc.dram_tensor(name, list(shape), dt, kind="ExternalInput").ap()

    def dint(name, shape, dt):
        return nc.dram_tensor(name, list(shape), dt, kind="Internal").ap()

    x_in = din("x", [S, D])
    y_out = nc.dram_tensor("y", [S, D], F32, kind="ExternalOutput").ap()
    a_win = din("a_win", [max(n_a, 1), D, A_IN])
    a_wout = din("a_wout", [max(n_a, 1), D, D])
    a_qg = din("a_qg", [max(n_a, 1), 128])
    a_kg = din("a_kg", [max(n_a, 1), 128])
    r_win = din("r_win", [max(n_r, 1), D, R_IN])
    r_wout = din("r_wout", [max(n_r, 1), 2 * D, D])
    r_dec = din("r_dec", [max(n_r, 1), 16])
    ln_g = din("ln_g", [nL, D])
    ln_b = din("ln_b", [nL, D])
    cosa_d = din("cosa", [S, 128])
    sina_d = din("sina", [S, 128])
    cosr_d = din("cosr", [S, 256])
    sinr_d = din("sinr", [S, 256])
    identf_d = din("identf_d", [128, 128])
    p1_d = din("p1_d", [128, 128])
    p2_d = din("p2_d", [128, 128])
    posv_d = din("posv_d", [128, 4])
    prow_d = din("prow_d", [128, 256])

    a_win_b = dint("a_win_b", [max(n_a, 1), A_IN // 512, 128, 16, 512], BF16)
    a_wout_b = dint("a_wout_b", [max(n_a, 1), D // 512, 128, 16, 512], BF16)
    r_win_b = dint("r_win_b", [max(n_r, 1), R_IN // 512, 128, 16, 512], BF16)
    r_wout_b = dint("r_wout_b", [max(n_r, 1), D // 256, 128, 32, 256], BF16)
    xs = [dint("xs0", [S, D], F32), dint("xs1", [S, D], F32)]
    if n_r:
        r_qT = dint("r_qT", [NT, 128, 2 * 8 * 128], BF16)
        r_k = dint("r_k", [S, 2048], BF16)
        r_v = dint("r_v", [S, 4096], BF16)
        r_g = dint("r_g", [S, 4096], BF16)
        r_o1 = dint("r_o1", [S, 4096], F32)

    stack = ExitStack()
    DB = {}

    def dbuf(ap):
        k = ap.name if hasattr(ap, "name") else id(ap)
        if k not in DB:
            DB[k] = Buf(str(k))
        return DB[k]

    NA = 32768
    arena = stack.enter_context(nc.sbuf_tensor("arena", [128, NA], F32))
    aoff = [0]

    class T:
        def __init__(self, name, shape, dt, psum=False, persist=False, view=None, buf=None):
            if view is not None:
                self.t = view
                self.b = buf
                return
            shape = list(shape)
            if psum:
                self.t = stack.enter_context(nc.psum_tensor(name, shape, dt))
            elif persist:
                self.t = stack.enter_context(nc.sbuf_tensor(name, shape, dt))
            else:
                n = 1
                for d_ in shape[1:]:
                    n *= d_
                nw = n if dt == F32 else (n + 1) // 2
                nw = (nw + 7) // 8 * 8
                assert aoff[0] + nw <= NA, ("arena overflow", name, aoff[0], nw)
                v = arena[:, aoff[0]:aoff[0] + nw]
                aoff[0] += nw
                if dt != F32:
                    v = v.bitcast(dt)
                v = v[:, 0:n]
                if len(shape) == 3:
                    v = v.rearrange("p (a b) -> p a b", b=shape[2])
                elif len(shape) == 4:
                    v = v.rearrange("p (a b c) -> p a b c", b=shape[2], c=shape[3])
                self.t = v
            self.b = Buf(name)

        def __getitem__(self, k):
            return self.t[k]

    def new_phase():
        sc.fence()
        aoff[0] = 0

    def op(eng, fn, reads=(), writes=(), dma=False):
        rb = [x.b if isinstance(x, T) else x for x in reads]
        wb = [x.b if isinstance(x, T) else x for x in writes]
        return sc.op(eng, fn, rb, wb, dma)

    def dma(q, out, in_, reads, writes):
        op(q, lambda e: e.dma_start(out=out, in_=in_), reads, writes, dma=True)

    identf = T("identf", [128, 128], F32, persist=True)
    identb = T("identb", [128, 128], BF16, persist=True)
    onesb = T("onesb", [128, 128], BF16, persist=True)
    dma("sp", identf[:], identf_d[:, :], [], [identf])
    op("dve", lambda e: e.tensor_copy(out=identb[:], in_=identf[:]), [identf], [identb])
    op("dve", lambda e: e.memset(onesb[:], 1.0), [], [onesb])

    WB = {}
    cast_todo = {}

    def cast_w(li_, key, src, dst, rows, cols, cw):
        WB[key] = Buf(str(key))
        for kc in range(rows // 128):
            s_ = src[kc * 128:(kc + 1) * 128, :].rearrange("p (cb c) -> p cb c", c=cw)
            d_ = dst[:, :, kc, :].rearrange("cb p c -> p cb c")
            cast_todo.setdefault(li_, []).append(lambda d_=d_, s_=s_, key=key: dma("pool", d_, s_, [], [WB[key]]))

    def cast_some(li_, n):
        lst = cast_todo.get(li_, [])
        while lst and n > 0:
            lst.pop(0)()
            n -= 1

    psA2 = stack.enter_context(nc.psum_tensor("psA2", [128, 1024], F32))
    psB2 = stack.enter_context(nc.psum_tensor("psB2", [128, 1024], F32))
    psA = [T(None, None, None, view=psA2[:, i * 512:(i + 1) * 512], buf=Buf("psA%d" % i)) for i in range(2)]
    psB = [T(None, None, None, view=psB2[:, i * 512:(i + 1) * 512], buf=Buf("psB%d" % i)) for i in range(2)]
    psW = [(psA2, psA), (psB2, psB)]
    psT = [T("psT0", [128, 1024], BF16, True), T("psT1", [128, 1024], BF16, True)]
    psTf = [T(None, None, None, view=psT[i].t[:].bitcast(F32), buf=psT[i].b) for i in range(2)]
    psO = T("psO", [128, 512], F32, True)
    psL = T("psL", [128, 512], F32, True)

    wbuf = [T("wbuf0", [128, 16, 512], BF16, persist=True), T("wbuf1", [128, 16, 512], BF16, persist=True)]
    wctr = [0]
    xstg = [T("xstg0", [128, 512], F32, persist=True), T("xstg1", [128, 512], F32, persist=True)]
    xctr = [0]
    t1_g = T("t1", [128, 512], F32, persist=True)
    t2_g = T("t2", [128, 512], F32, persist=True)
    lng = [T("lng0", [128, 512], F32, persist=True), t1_g]
    lnb = [T("lnb0", [128, 512], F32, persist=True), t2_g]
    small = {"t1": t1_g, "t2": t2_g}

    def sm(name, shape, dt=F32):
        if name not in small:
            small[name] = T(name, shape, dt, persist=True)
        return small[name]

    evac_ctr = [0]
    cur_layer = [0]

    def trickle():
        cast_some(cur_layer[0] + 1, 2)

    def evac(out, in_, reads, writes):
        evac_ctr[0] += 1
        if evac_ctr[0] % 2:
            op("act", lambda e: e.activation(out=out, in_=in_, func=AF.Copy), reads, writes)
        else:
            op("dve", lambda e: e.tensor_copy(out=out, in_=in_), reads, writes)

    def load_w(src2d, c0, ncol, nk, wb=None):
        w = wbuf[wctr[0] % 2]
        wctr[0] += 1
        assert nk * ncol <= 16 * 512
        view = w.t[:].rearrange("p a b -> p (a b)")[:, 0:nk * ncol].rearrange("p (a b) -> p a b", b=ncol)
        dma("sp", view, src2d[c0 // ncol], [wb], [w])

        return T(None, None, None, view=view, buf=w.b)

    def load_xT(xsrc, t0, ntile, xT):
        for j in range(ntile):
            tok0 = (t0 + j) * 128
            for c4 in range(4):
                st = xstg[xctr[0] % 2]
                xctr[0] += 1
                dma("sp", st[:], xsrc[tok0:tok0 + 128, c4 * 512:(c4 + 1) * 512], [dbuf(xsrc)], [st])
                ps = psB[c4 % 2]

                def f(e, ps=ps, st=st):
                    for i in range(4):
                        ins = e.transpose(out=ps[:, i * 128:(i + 1) * 128], in_=st[:, i * 128:(i + 1) * 128], identity=identf[:])
                    return ins
                op("pe", f, [st, identf], [ps])
                evac(xT[:, c4 * 4:(c4 + 1) * 4, j * 128:(j + 1) * 128],
                     ps[:].rearrange("p (a b) -> p a b", b=128), [ps], [xT])

    def proj_tok(ps, xT, j, w, nk=16, ncol=512):
        def f(e):
            for kc in range(nk):
                ins = e.matmul(ps[:, 0:ncol], lhsT=xT[:, kc, j * 128:(j + 1) * 128], rhs=w[:, kc, 0:ncol],
                               start=(kc == 0), stop=(kc == nk - 1))
            return ins
        op("pe", f, [xT, w], [ps])

    def layer_norm_out(li, rbuf, j, tok0, xdst):
        stats = sm("ln_stats", [128, 4, 6])
        mv = sm("ln_mv", [128, 2])
        rstd = sm("ln_rstd", [128, 1])
        nmr = sm("ln_nmr", [128, 1])
        for c in range(4):
            op("dve", lambda e, c=c: e.bn_stats(out=stats[:, c, :], in_=rbuf[:, j, c * 512:(c + 1) * 512]), [rbuf], [stats])
        op("dve", lambda e: e.bn_aggr(out=mv[:], in_=stats[:].rearrange("p a b -> p (a b)")), [stats], [mv])
        op("dve", lambda e: e.tensor_scalar(out=rstd[:], in0=mv[:, 1:2], scalar1=LN_EPS, scalar2=None, op0=ALU.add), [mv], [rstd])
        op("act", lambda e: e.activation(out=rstd[:], in_=rstd[:], func=AF.Sqrt), [rstd], [rstd])
        op("dve", lambda e: e.reciprocal(out=rstd[:], in_=rstd[:]), [rstd], [rstd])
        op("dve", lambda e: e.scalar_tensor_tensor(out=nmr[:], in0=mv[:, 0:1], scalar=-1.0, in1=rstd[:], op0=ALU.mult, op1=ALU.mult),
           [mv, rstd], [nmr])
        op("act", lambda e: e.activation(out=rbuf[:, j, :], in_=rbuf[:, j, :], func=AF.Identity, bias=nmr[:, 0:1], scale=rstd[:, 0:1]),
           [rbuf, nmr, rstd], [rbuf])
        for c in range(4):
            g_ = lng[c % 2]
            b_ = lnb[c % 2]
            dma("pool", g_[:], ln_g[li:li + 1, c * 512:(c + 1) * 512].partition_broadcast(128), [], [g_])
            dma("pool", b_[:], ln_b[li:li + 1, c * 512:(c + 1) * 512].partition_broadcast(128), [], [b_])
            op("pool", lambda e, c=c, g_=g_: e.tensor_tensor(out=rbuf[:, j, c * 512:(c + 1) * 512], in0=rbuf[:, j, c * 512:(c + 1) * 512],
                                                             in1=g_[:], op=ALU.mult), [rbuf, g_], [rbuf])
            op("dve", lambda e, c=c, b_=b_: e.tensor_tensor(out=rbuf[:, j, c * 512:(c + 1) * 512], in0=rbuf[:, j, c * 512:(c + 1) * 512],
                                                            in1=b_[:], op=ALU.add), [rbuf, b_], [rbuf])
        dma("pool", xdst[tok0:tok0 + 128, :], rbuf[:, j, :], [rbuf], [dbuf(xdst)])

    def attention_layer(li, ja, xsrc, xdst):
        new_phase()
        SBT = 2
        NSB = NT // SBT
        win = a_win_b[ja]
        wout = a_wout_b[ja]
        KT = T("KT", [128, 4, S], BF16)
        V = T("V", [128, NT, 512], BF16)
        xT = T("xT", [128, 16, SBT * 128], BF16)
        qT = T("qT", [128, 16, SBT * 128], BF16)
        gT = T("gT", [128, 16, SBT * 128], BF16)
        rbuf = T("rbuf", [128, SBT, D], F32)
        PT = [T("PT%d" % i, [128, 1024], BF16) for i in range(3)]
        cosb = [sm("cosb0", [128, 128]), sm("cosb1", [128, 128])]
        sinb = [sm("sinb0", [128, 128]), sm("sinb1", [128, 128])]
        gq = sm("gq_rep", [128, 128])
        gk = sm("gk_rep", [128, 128])
        negb = sm("negb", [128, 1])
        mq = sm("mq", [128, 1])
        mk_ = sm("mk", [128, 1])
        sq = T("sq", [128, 512], F32)
        ss = sm("ss", [128, 4])
        rs = sm("rs", [128, 4])
        qg = T("qg", [128, 512], F32)
        t1 = sm("t1", [128, 512])
        t2 = sm("t2", [128, 512])
        qbs = [sm("qb0", [128, 512], BF16), sm("qb1", [128, 512], BF16)]
        pend = []
        rl = T("rl", [128, 512], F32)
        tmpo = T("tmpo", [128, 512], F32)

        dma("sp", gq[:], a_qg[ja:ja + 1, :].partition_broadcast(128), [], [gq])
        dma("sp", gk[:], a_kg[ja:ja + 1, :].partition_broadcast(128), [], [gk])
        op("dve", lambda e: e.tensor_reduce(out=mq[:], in_=gq[:], axis=AX.X, op=ALU.max, apply_absolute_value=True), [gq], [mq])
        op("dve", lambda e: e.tensor_reduce(out=mk_[:], in_=gk[:], axis=AX.X, op=ALU.max, apply_absolute_value=True), [gk], [mk_])
        op("dve", lambda e: e.scalar_tensor_tensor(out=negb[:], in0=mq[:], scalar=-math.sqrt(128.0), in1=mk_[:], op0=ALU.mult, op1=ALU.mult),
           [mq, mk_], [negb])

        def qk_post(ps, gain, scale, tok0, cs, qb):
            cb_, sb_ = cosb[cs % 2], sinb[cs % 2]
            op("act", lambda e: e.activation(out=sq[:], in_=ps[:], func=AF.Square), [ps], [sq])
            op("dve", lambda e: e.tensor_reduce(out=ss[:], in_=sq[:].rearrange("p (h d) -> p h d", d=128), axis=AX.X, op=ALU.add), [sq], [ss])
            op("dve", lambda e: e.tensor_scalar(out=rs[:], in0=ss[:], scalar1=1.0 / (128.0 * scale * scale), scalar2=RMS_EPS / (scale * scale),
                                                op0=ALU.mult, op1=ALU.add), [ss], [rs])
            op("act", lambda e: e.activation(out=rs[:], in_=rs[:], func=AF.Sqrt), [rs], [rs])
            op("dve", lambda e: e.reciprocal(out=rs[:], in_=rs[:]), [rs], [rs])
            op("dve", lambda e: e.tensor_tensor(out=qg[:].rearrange("p (h d) -> p h d", d=128), in0=ps[:].rearrange("p (h d) -> p h d", d=128),
                                                in1=gain[:].unsqueeze(1).to_broadcast([128, 4, 128]), op=ALU.mult), [ps, gain], [qg])
            op("dve", lambda e: e.tensor_tensor(out=t1[:].rearrange("p (h d) -> p h d", d=128), in0=qg[:].rearrange("p (h d) -> p h d", d=128),
                                                in1=cb_[:].unsqueeze(1).to_broadcast([128, 4, 128]), op=ALU.mult), [qg, cb_], [t1])
            qv = qg[:].rearrange("p (g s d) -> p g s d", s=2, d=32)
            tv = t2[:].rearrange("p (g s d) -> p g s d", s=2, d=32)
            sv = sb_[:].rearrange("p (g s d) -> p g s d", s=2, d=32)
            qv5 = qg[:].rearrange("p (h g s d) -> p h g s d", h=4, g=2, s=2, d=32)
            tv5 = t2[:].rearrange("p (h g s d) -> p h g s d", h=4, g=2, s=2, d=32)
            for s_ in range(2):
                op("dve", lambda e, s_=s_: e.tensor_tensor(out=tv5[:, :, :, s_, :], in0=qv5[:, :, :, 1 - s_, :],
                                                            in1=sv[:, :, s_, :].unsqueeze(1).to_broadcast([128, 4, 2, 32]), op=ALU.mult), [qg, sb_], [t2])
            op("dve", lambda e: e.tensor_tensor(out=t1[:], in0=t1[:], in1=t2[:], op=ALU.add), [t1, t2], [t1])
            op("dve", lambda e: e.tensor_tensor(out=qb[:].rearrange("p (h d) -> p h d", d=128), in0=t1[:].rearrange("p (h d) -> p h d", d=128),
                                                in1=rs[:].unsqueeze(2).to_broadcast([128, 4, 128]), op=ALU.mult), [t1, rs], [qb])

        def load_tables(tok0, cs):
            dma("sp", cosb[cs % 2][:], cosa_d[tok0:tok0 + 128, :], [], [cosb[cs % 2]])
            dma("sp", sinb[cs % 2][:], sina_d[tok0:tok0 + 128, :], [], [sinb[cs % 2]])

        def transpose4(qb, pst):
            def f(e):
                for h in range(4):
                    ins = e.transpose(out=pst[:, h * 128:(h + 1) * 128], in_=qb[:, h * 128:(h + 1) * 128], identity=identb[:])
                return ins
            op("pe", f, [qb, identb], [pst])

        cs = 0
        for sb in range(NSB):
            trickle()
            load_xT(xsrc, sb * SBT, SBT, xT)
            wk = load_w(win, 2048, 512, 16, WB[('ai', ja)])
            wv = load_w(win, 2560, 512, 16, WB[('ai', ja)])
            for j in range(SBT):
                t = sb * SBT + j
                tok0 = t * 128
                load_tables(tok0, cs)
                ps = psA[t % 2]
                proj_tok(ps, xT, j, wk)
                qb = qbs[cs % 2]
                qk_post(ps, gk, 1.0, tok0, cs, qb)
                cs += 1
                ps = (psO, psL)[t % 2]
                proj_tok(ps, xT, j, wv)
                evac(V[:, t, :], ps[:], [ps], [V])
                while pend:
                    pend.pop(0)()

                def fin(qb=qb, t=t, tok0=tok0):
                    pst = psT[t % 2]
                    transpose4(qb, pst)
                    evac(KT[:, :, tok0:tok0 + 128], pst[:, 0:512].rearrange("p (h t) -> p h t", t=128), [pst], [KT])
                pend.append(fin)
        while pend:
            pend.pop(0)()

        for sb in range(NSB):
            trickle()
            load_xT(xsrc, sb * SBT, SBT, xT)
            for cb in range(4):
                w = load_w(win, cb * 512, 512, 16, WB[('ai', ja)])
                for j in range(SBT):
                    tok0 = (sb * SBT + j) * 128
                    load_tables(tok0, cs)
                    ps = psA[cs % 2]
                    proj_tok(ps, xT, j, w)
                    qb = qbs[cs % 2]
                    qk_post(ps, gq, 128.0 ** -0.5, tok0, cs, qb)
                    while pend:
                        pend.pop(0)()

                    def fin(qb=qb, cs_=cs, cb=cb, j=j):
                        pst = psT[cs_ % 2]
                        transpose4(qb, pst)
                        evac(qT[:, cb * 4:(cb + 1) * 4, j * 128:(j + 1) * 128], pst[:, 0:512].rearrange("p (h t) -> p h t", t=128), [pst], [qT])
                    pend.append(fin)
                    cs += 1
            for cb in range(4):
                w = load_w(win, 3072 + cb * 512, 512, 16, WB[('ai', ja)])
                for h in range(4):
                    ps = psA[h % 2]

                    def f(e, ps=ps, w=w, h=h):
                        for kc in range(16):
                            ins = e.matmul(ps[:, 0:SBT * 128], lhsT=w[:, kc, h * 128:(h + 1) * 128], rhs=xT[:, kc, :],
                                           start=(kc == 0), stop=(kc == 15))
                        return ins
                    op("pe", f, [xT, w], [ps])
                    op("act", lambda e, ps=ps, cb=cb, h=h: e.activation(out=gT[:, cb * 4 + h, :], in_=ps[:, 0:SBT * 128], func=AF.Silu), [ps], [gT])
            while pend:
                pend.pop(0)()
            gctr = [0]
            for j in range(SBT):
                for g in range(4):
                    rhs_q = qT[:, g * 4:(g + 1) * 4, j * 128:(j + 1) * 128]

                    def s_mm(i, g=g, rhs_q=rhs_q):
                        ps2, halves = psW[i % 2]

                        def f(e):
                            for u_ in range(2):
                                kt = 2 * i + u_
                                ins = e.matmul(halves[u_][:], lhsT=KT[:, g, kt * 128:(kt + 1) * 128], rhs=rhs_q, start=True, stop=True)
                            return ins
                        op("pe", f, [KT, qT], [halves[0], halves[1]])
                        pt = PT[i % 3]
                        op("act", lambda e: e.activation(out=pt[:], in_=ps2[:, :], func=AF.Exp, bias=negb[:, 0:1], scale=1.0), [halves[0], halves[1], negb], [pt])

                    gctr[0] += 1
                    pO, pL = (psO, psL) if gctr[0] % 2 else (psTf[0], psTf[1])

                    def pv_mm(i, g=g, pO=pO, pL=pL):
                        pt = PT[i % 3]

                        def f(e):
                            for u_ in range(2):
                                kt = 2 * i + u_
                                e.matmul(pO[:], lhsT=V[:, kt, g * 128:(g + 1) * 128], rhs=pt[:, u_ * 512:(u_ + 1) * 512], start=(kt == 0), stop=(kt == NT - 1))
                                ins = e.matmul(pL[:], lhsT=onesb[:], rhs=pt[:, u_ * 512:(u_ + 1) * 512], start=(kt == 0), stop=(kt == NT - 1))
                            return ins
                        op("pe", f, [V, pt, onesb], [pO, pL])

                    NP = NT // 2
                    for i in range(NP):
                        s_mm(i)
                        if i >= 1:
                            pv_mm(i - 1)
                    pv_mm(NP - 1)
                    op("dve", lambda e, pL=pL: e.reciprocal(out=rl[:], in_=pL[:]), [pL], [rl])
                    op("dve", lambda e, pO=pO: e.tensor_tensor(out=tmpo[:], in0=pO[:], in1=rl[:], op=ALU.mult), [pO, rl], [tmpo])
                    gview = gT[:, g * 4:(g + 1) * 4, j * 128:(j + 1) * 128]
                    op("dve", lambda e, gview=gview: e.tensor_tensor(out=gview, in0=gview, in1=tmpo[:].rearrange("p (h t) -> p h t", t=128), op=ALU.mult),
                       [gT, tmpo], [gT])
            for cb in range(4):
                w = load_w(wout, cb * 512, 512, 16, WB[('ao', ja)])
                for j in range(SBT):
                    tok0 = (sb * SBT + j) * 128
                    ps = psA[j % 2]

                    def f(e, ps=ps, w=w, j=j):
                        for h in range(16):
                            ins = e.matmul(ps[:], lhsT=gT[:, h, j * 128:(j + 1) * 128], rhs=w[:, h, :], start=(h == 0), stop=(h == 15))
                        return ins
                    op("pe", f, [gT, w], [ps])
                    st = xstg[xctr[0] % 2]
                    xctr[0] += 1
                    dma("sp", st[:], xsrc[tok0:tok0 + 128, cb * 512:(cb + 1) * 512], [dbuf(xsrc)], [st])
                    op("dve", lambda e, ps=ps, st=st, j=j, cb=cb: e.scalar_tensor_tensor(out=rbuf[:, j, cb * 512:(cb + 1) * 512], in0=st[:], scalar=ALPHA,
                                                                                          in1=ps[:], op0=ALU.mult, op1=ALU.add), [st, ps], [rbuf])
            for j in range(SBT):
                layer_norm_out(li, rbuf, j, (sb * SBT + j) * 128, xdst)

    def retention_layer(li, jr, xsrc, xdst):
        SBT = 2
        NSB = NT // SBT
        win = r_win_b[jr]
        wout = r_wout_b[jr]
        LN2 = math.log(2.0)

        posv = sm("posv", [128, 4])
        p1 = sm("p1", [128, 128])
        p2 = sm("p2", [128, 128])
        dec = sm("dec_rep", [128, 16])
        u = sm("dec_u", [128, 16])
        tt = sm("dec_t", [128, 16])
        lg = sm("dec_lg", [128, 16])
        dqf = sm("dqf", [128, 8])
        dqb = sm("dqb", [128, 8])
        kdf = sm("kdf", [128, 8])
        kdb = sm("kdb", [128, 8])
        gc = sm("gc", [128, 16])
        arg = sm("dc_arg", [128, 128])
        dma("sp", posv[:], posv_d[:, :], [], [posv])
        dma("sp", p1[:], p1_d[:, :], [], [p1])
        dma("sp", p2[:], p2_d[:, :], [], [p2])
        dma("sp", dec[:], r_dec[jr:jr + 1, :].partition_broadcast(128), [], [dec])
        op("act", lambda e: e.activation(out=u[:], in_=dec[:], func=AF.Exp, scale=-LN2), [dec], [u])
        op("dve", lambda e: e.tensor_scalar(out=tt[:], in0=u[:], scalar1=0.25, scalar2=1.0 / 3.0, op0=ALU.mult, op1=ALU.add), [u], [tt])
        op("dve", lambda e: e.tensor_tensor(out=tt[:], in0=tt[:], in1=u[:], op=ALU.mult), [tt, u], [tt])
        op("dve", lambda e: e.tensor_scalar(out=tt[:], in0=tt[:], scalar1=0.5, scalar2=None, op0=ALU.add), [tt], [tt])
        op("dve", lambda e: e.tensor_tensor(out=tt[:], in0=tt[:], in1=u[:], op=ALU.mult), [tt, u], [tt])
        op("dve", lambda e: e.tensor_scalar(out=tt[:], in0=tt[:], scalar1=1.0, scalar2=None, op0=ALU.add), [tt], [tt])
        op("dve", lambda e: e.scalar_tensor_tensor(out=lg[:], in0=tt[:], scalar=-1.0, in1=u[:], op0=ALU.mult, op1=ALU.mult), [tt, u], [lg])
        op("act", lambda e: e.activation(out=dqf[:], in_=lg[:, 0:8], func=AF.Exp, scale=posv[:, 0:1]), [lg, posv], [dqf])
        op("act", lambda e: e.activation(out=dqb[:], in_=lg[:, 8:16], func=AF.Exp, scale=posv[:, 1:2]), [lg, posv], [dqb])
        op("act", lambda e: e.activation(out=kdf[:], in_=lg[:, 0:8], func=AF.Exp, scale=posv[:, 2:3]), [lg, posv], [kdf])
        op("act", lambda e: e.activation(out=kdb[:], in_=lg[:, 8:16], func=AF.Exp, scale=posv[:, 3:4]), [lg, posv], [kdb])
        op("act", lambda e: e.activation(out=gc[:], in_=lg[:], func=AF.Exp, scale=128.0), [lg], [gc])

        cosr = [sm("cosr0", [128, 256]), sm("cosr1", [128, 256])]
        sinr = [sm("sinr0", [128, 256]), sm("sinr1", [128, 256])]
        t1 = sm("t1", [128, 512])
        t2 = sm("t2", [128, 512])
        cs = [0]

        def rope256(ps, out_ap, outT, scale, tok0):
            c_ = cosr[cs[0] % 2]
            s_ = sinr[cs[0] % 2]
            cs[0] += 1
            dma("sp", c_[:], cosr_d[tok0:tok0 + 128, :], [], [c_])
            dma("sp", s_[:], sinr_d[tok0:tok0 + 128, :], [], [s_])
            op("dve", lambda e: e.tensor_tensor(out=t1[:].rearrange("p (h d) -> p h d", d=256), in0=ps[:].rearrange("p (h d) -> p h d", d=256),
                                                in1=c_[:].unsqueeze(1).to_broadcast([128, 2, 256]), op=ALU.mult), [ps, c_], [t1])
            pv5 = ps[:].rearrange("p (h g s d) -> p h g s d", h=2, g=2, s=2, d=64)
            tv5 = t2[:].rearrange("p (h g s d) -> p h g s d", h=2, g=2, s=2, d=64)
            sv = s_[:].rearrange("p (g s d) -> p g s d", s=2, d=64)
            for k_ in range(2):
                op("dve", lambda e, k_=k_: e.tensor_tensor(out=tv5[:, :, :, k_, :], in0=pv5[:, :, :, 1 - k_, :],
                                                           in1=sv[:, :, k_, :].unsqueeze(1).to_broadcast([128, 2, 2, 64]), op=ALU.mult), [ps, s_], [t2])
            if scale == 1.0:
                op("dve", lambda e: e.tensor_tensor(out=out_ap, in0=t1[:], in1=t2[:], op=ALU.add), [t1, t2], [outT])
            else:
                op("dve", lambda e: e.tensor_tensor(out=t1[:], in0=t1[:], in1=t2[:], op=ALU.add), [t1, t2], [t1])
                op("act", lambda e: e.activation(out=out_ap, in_=t1[:], func=AF.Copy, scale=scale), [t1], [outT])

        new_phase()
        prow = sm("prow", [128, 256])
        dma("sp", prow[:], prow_d[:, :], [], [prow])
        St32 = [T("St32_%d" % h, [128, 2, 512], F32) for h in range(8)]
        Stb = [T("Stb_%d" % h, [128, 2, 512], BF16) for h in range(8)]
        DcT = [T("DcT_%d" % h, [128, 128], F32) for h in range(8)]
        DqT = T("DqfT", [128, 8, 128], F32)
        xT = T("xT", [128, 16, SBT * 128], BF16)
        qTc = [T("qTc%d" % j, [128, 2, 8, 128], BF16) for j in range(SBT)]
        kTc = [T("kTc%d" % j, [128, 2, 8, 128], BF16) for j in range(SBT)]
        kc = [T("kc%d" % j, [128, 2048], BF16) for j in range(SBT)]
        vall = [T("vall%d" % j, [128, 4096], BF16) for j in range(SBT)]
        AT = [T("AT%d" % i, [128, 128], BF16) for i in range(2)]
        gh = [T("gh%d" % i, [128, 512], BF16) for i in range(2)]
        o1buf = [T("o1buf%d" % i, [128, 512], F32) for i in range(2)]
        arg2 = sm("dc_arg2", [128, 128])

        for h in range(8):
            op("dve", lambda e, h=h: e.tensor_scalar(out=arg[:], in0=p1[:], scalar1=lg[:, h:h + 1], scalar2=None, op0=ALU.mult), [p1, lg], [arg])
            op("dve", lambda e, h=h: e.scalar_tensor_tensor(out=arg[:], in0=p2[:], scalar=lg[:, 8 + h:9 + h], in1=arg[:], op0=ALU.mult, op1=ALU.add),
               [p2, lg, arg], [arg])
            op("dve", lambda e, h=h: e.tensor_scalar(out=arg2[:], in0=prow[:, 0:128], scalar1=lg[:, h:h + 1], scalar2=None, op0=ALU.mult), [prow, lg], [arg2])
            op("dve", lambda e: e.tensor_tensor(out=arg[:], in0=arg[:], in1=arg2[:], op=ALU.subtract), [arg, arg2], [arg])
            op("act", lambda e, h=h: e.activation(out=DcT[h][:], in_=arg[:], func=AF.Exp), [arg], [DcT[h]])
            op("act", lambda e, h=h: e.activation(out=DqT[:, h, :], in_=prow[:, 0:128], func=AF.Exp, scale=lg[:, h:h + 1]), [prow, lg], [DqT])
            op("pool", lambda e, h=h: e.memset(St32[h][:], 0.0), [], [St32[h]])
            op("pool", lambda e, h=h: e.memset(Stb[h][:], 0.0), [], [Stb[h]])

        def state_update(St32, Stb, psU, h, kall, vht, gcol):
            S32h = St32[h]
            Sbh = Stb[h]
            for dkc in range(2):
                pu = psU[dkc]
                op("pe", lambda e, dkc=dkc, pu=pu: e.matmul(pu[:], lhsT=kall[:, h * 256 + dkc * 128:h * 256 + (dkc + 1) * 128], rhs=vht, start=True, stop=True),
                   [kall, vht_T[0]], [pu])
                op("dve", lambda e, dkc=dkc, pu=pu: e.scalar_tensor_tensor(out=S32h[:, dkc, :], in0=S32h[:, dkc, :], scalar=gc[:, gcol:gcol + 1],
                                                                          in1=pu[:], op0=ALU.mult, op1=ALU.add), [S32h, gc, pu], [S32h])
            op("act", lambda e: e.activation(out=Sbh[:], in_=S32h[:], func=AF.Copy), [S32h], [Sbh])

        vht_T = [None]
        psX = [psO, psL]
        pend = []
        bctr = [0]
        sctr = [0]
        for sb in range(NSB):
            trickle()
            load_xT(xsrc, sb * SBT, SBT, xT)
            for kind in range(2):
                for cb in range(4):
                    w = load_w(win, kind * 2048 + cb * 512, 512, 16, WB[('ri', jr)])
                    for j in range(SBT):
                        tok0 = (sb * SBT + j) * 128
                        bctr[0] += 1
                        bc = bctr[0]
                        ps = psA[bc % 2]
                        proj_tok(ps, xT, j, w)
                        if kind == 0:
                            src_t = gh[bc % 2]
                            src_ap = src_t[:]
                            rope256(ps, src_ap, src_t, 1.0, tok0)
                            dstT = qTc[j]
                        else:
                            src_t, src_ap = kc[j], kc[j][:, cb * 512:(cb + 1) * 512]
                            rope256(ps, src_ap, kc[j], 1.0 / 16.0, tok0)
                            dstT = kTc[j]
                        while pend:
                            pend.pop(0)()

                        def fin(src_t=src_t, src_ap=src_ap, dstT=dstT, bc=bc, cb=cb):
                            pst = psT[bc % 2]

                            def f(e):
                                for dkc in range(2):
                                    for hl in range(2):
                                        i = dkc * 2 + hl
                                        ins = e.transpose(out=pst[:, i * 128:(i + 1) * 128], in_=src_ap[:, hl * 256 + dkc * 128:hl * 256 + (dkc + 1) * 128],
                                                          identity=identb[:])
                                return ins
                            op("pe", f, [src_t, identb], [pst])
                            evac(dstT[:, :, 2 * cb:2 * cb + 2, :], pst[:, 0:512].rearrange("p (a b t) -> p a b t", a=2, b=2), [pst], [dstT])
                        pend.append(fin)
            while pend:
                pend.pop(0)()
            for j in range(SBT):
                c = sb * SBT + j
                dma("pool", r_qT[c], qTc[j][:].rearrange("p a b t -> p (a b t)"), [qTc[j]], [dbuf(r_qT)])
                dma("pool", r_k[c * 128:(c + 1) * 128, :], kc[j][:], [kc[j]], [dbuf(r_k)])
                op("dve", lambda e, j=j: e.tensor_tensor(out=qTc[j][:], in0=qTc[j][:], in1=DqT[:].unsqueeze(1).to_broadcast([128, 2, 8, 128]), op=ALU.mult),
                   [qTc[j], DqT], [qTc[j]])
                op("dve", lambda e, j=j: e.tensor_tensor(out=kc[j][:].rearrange("p (h d) -> p h d", d=256), in0=kc[j][:].rearrange("p (h d) -> p h d", d=256),
                                                         in1=kdf[:].unsqueeze(2).to_broadcast([128, 8, 256]), op=ALU.mult), [kc[j], kdf], [kc[j]])
            for h in range(8):
                w = load_w(win, 4096 + h * 512, 512, 16, WB[('ri', jr)])
                for j in range(SBT):
                    ps = psA[j % 2]
                    proj_tok(ps, xT, j, w)
                    evac(vall[j][:, h * 512:(h + 1) * 512], ps[:], [ps], [vall[j]])
            for j in range(SBT):
                tok0 = (sb * SBT + j) * 128
                dma("pool", r_v[tok0:tok0 + 128, :], vall[j][:], [vall[j]], [dbuf(r_v)])
            steps = [(j, h) for j in range(SBT) for h in range(8)]

            def step_S(j, h, k_):
                psS = psB[k_ % 2]

                def f(e):
                    for dkc in range(2):
                        ins = e.matmul(psS[:, 0:128], lhsT=kTc[j][:, dkc, h, :], rhs=qTc[j][:, dkc, h, :], start=(dkc == 0), stop=(dkc == 1))
                    return ins
                op("pe", f, [kTc[j], qTc[j]], [psS])
                at = AT[k_ % 2]
                op("dve", lambda e: e.tensor_tensor(out=at[:], in0=psS[:, 0:128], in1=DcT[h][:], op=ALU.mult), [psS, DcT[h]], [at])

            def step_rest(j, h, k_):
                tok0 = (sb * SBT + j) * 128
                vht = vall[j][:, h * 512:(h + 1) * 512]
                vht_T[0] = vall[j]
                at = AT[k_ % 2]
                acc = psX[k_ % 2]
                sb_ = Stb[h]

                def f2(e):
                    e.matmul(acc[:], lhsT=at[:], rhs=vht, start=True, stop=False)
                    e.matmul(acc[:], lhsT=qTc[j][:, 0, h, :], rhs=sb_[:, 0, :], start=False, stop=False)
                    return e.matmul(acc[:], lhsT=qTc[j][:, 1, h, :], rhs=sb_[:, 1, :], start=False, stop=True)
                op("pe", f2, [at, vall[j], qTc[j], Stb[h]], [acc])
                ob = o1buf[k_ % 2]
                evac(ob[:], acc[:], [acc], [ob])
                dma("pool", r_o1[tok0:tok0 + 128, h * 512:(h + 1) * 512], ob[:], [ob], [dbuf(r_o1)])
                psU = psA if k_ % 2 else psTf
                state_update(St32, Stb, psU, h, kc[j], vht, h)

            k0 = sctr[0]
            step_S(steps[0][0], steps[0][1], k0)
            for i_, (j, h) in enumerate(steps):
                if i_ + 1 < len(steps):
                    step_S(steps[i_ + 1][0], steps[i_ + 1][1], k0 + i_ + 1)
                step_rest(j, h, k0 + i_)
            sctr[0] += len(steps)
            for h in range(8):
                w = load_w(win, 8192 + h * 512, 512, 16, WB[('ri', jr)])
                for j in range(SBT):
                    tok0 = (sb * SBT + j) * 128
                    ps = psA[j % 2]
                    proj_tok(ps, xT, j, w)
                    g_ = gh[j % 2]
                    op("act", lambda e, g_=g_, ps=ps: e.activation(out=g_[:], in_=ps[:], func=AF.Silu), [ps], [g_])
                    dma("pool", r_g[tok0:tok0 + 128, h * 512:(h + 1) * 512], g_[:], [g_], [dbuf(r_g)])

        new_phase()
        St32b = [T("St32_%d" % h, [128, 2, 512], F32) for h in range(8)]
        Stbb = [T("Stb_%d" % h, [128, 2, 512], BF16) for h in range(8)]
        DqbT = T("DqbT", [128, 8, 128], F32)
        qT1s = [T("qT1_%d" % i, [128, 2, 8, 128], BF16) for i in range(2)]
        kc1s = [T("kc1_%d" % i, [128, 2048], BF16) for i in range(2)]
        vh2 = [T("vh%d" % i, [128, 512], BF16) for i in range(2)]
        gh2 = [T("gh%d" % i, [128, 512], BF16) for i in range(3)]
        og = T("og", [128, 4096], BF16)
        ogT = T("ogT", [128, 32, SBT * 128], BF16)
        o1b = [T("o1b%d" % i, [128, 512], F32) for i in range(2)]
        ot = [T("ot%d" % i, [128, 512], F32) for i in range(2)]
        rbuf = T("rbuf", [128, SBT, D], F32)
        gst = [sm("gn_stats%d" % i, [128, 6]) for i in range(2)]
        gmv = [sm("gn_mv%d" % i, [128, 2]) for i in range(2)]
        grs = [sm("gn_rstd%d" % i, [128, 1]) for i in range(2)]
        gnm = [sm("gn_nmr%d" % i, [128, 1]) for i in range(2)]
        for h in range(8):
            op("act", lambda e, h=h: e.activation(out=DqbT[:, h, :], in_=prow[:, 128:256], func=AF.Exp, scale=lg[:, 8 + h:9 + h]), [prow, lg], [DqbT])
            op("pool", lambda e, h=h: e.memset(St32b[h][:], 0.0), [], [St32b[h]])
            op("pool", lambda e, h=h: e.memset(Stbb[h][:], 0.0), [], [Stbb[h]])

        chunks = [sb * SBT + j for sb in reversed(range(NSB)) for j in reversed(range(SBT))]

        def prefetch_chunk(ci):
            c = chunks[ci]
            q_, k_ = qT1s[ci % 2], kc1s[ci % 2]
            dma("sp", q_[:].rearrange("p a b t -> p (a b t)"), r_qT[c], [dbuf(r_qT)], [q_])
            dma("sp", k_[:], r_k[c * 128:(c + 1) * 128, :], [dbuf(r_k)], [k_])
            op("dve", lambda e: e.tensor_tensor(out=q_[:], in0=q_[:], in1=DqbT[:].unsqueeze(1).to_broadcast([128, 2, 8, 128]), op=ALU.mult),
               [q_, DqbT], [q_])
            op("dve", lambda e: e.tensor_tensor(out=k_[:].rearrange("p (h d) -> p h d", d=256), in0=k_[:].rearrange("p (h d) -> p h d", d=256),
                                                in1=kdb[:].unsqueeze(2).to_broadcast([128, 8, 256]), op=ALU.mult), [k_, kdb], [k_])

        def gn_part2(h, t_, g_, grs_, gmv_, gnm_):
            op("dve", lambda e: e.reciprocal(out=grs_[:], in_=grs_[:]), [grs_], [grs_])
            op("dve", lambda e: e.scalar_tensor_tensor(out=gnm_[:], in0=gmv_[:, 0:1], scalar=-1.0, in1=grs_[:], op0=ALU.mult, op1=ALU.mult),
               [gmv_, grs_], [gnm_])
            op("act", lambda e: e.activation(out=t_[:], in_=t_[:], func=AF.Identity, bias=gnm_[:, 0:1], scale=grs_[:, 0:1]), [t_, gnm_, grs_], [t_])
            op("pool", lambda e: e.tensor_tensor(out=og[:, h * 512:(h + 1) * 512], in0=t_[:], in1=g_[:], op=ALU.mult), [t_, g_], [og])

        k2 = 0
        pend2 = []
        prefetch_chunk(0)
        for ci, c in enumerate(chunks):
            sb, j = c // SBT, c % SBT
            if j == SBT - 1:
                trickle()
            if True:
                tok0 = c * 128
                qT1, kc1 = qT1s[ci % 2], kc1s[ci % 2]
                for h in range(8):
                    k2 += 1
                    v_ = vh2[k2 % 2]
                    g_ = gh2[k2 % 3]
                    o_ = o1b[k2 % 2]
                    t_ = ot[k2 % 2]
                    gst_, gmv_, grs_, gnm_ = gst[k2 % 2], gmv[k2 % 2], grs[k2 % 2], gnm[k2 % 2]
                    dma("sp", v_[:], r_v[tok0:tok0 + 128, h * 512:(h + 1) * 512], [dbuf(r_v)], [v_])
                    dma("sp", g_[:], r_g[tok0:tok0 + 128, h * 512:(h + 1) * 512], [dbuf(r_g)], [g_])
                    dma("sp", o_[:], r_o1[tok0:tok0 + 128, h * 512:(h + 1) * 512], [dbuf(r_o1)], [o_])
                    acc = psX[k2 % 2]

                    def f2(e, h=h, sb_=Stbb[h], acc=acc, qT1=qT1):
                        for dkc in range(2):
                            ins = e.matmul(acc[:], lhsT=qT1[:, dkc, h, :], rhs=sb_[:, dkc, :], start=(dkc == 0), stop=(dkc == 1))
                        return ins
                    op("pe", f2, [qT1, Stbb[h]], [acc])
                    vht_T[0] = v_
                    psU = psA if k2 % 2 else psB
                    state_update(St32b, Stbb, psU, h, kc1, v_[:], 8 + h)
                    op("dve", lambda e, t_=t_, o_=o_, acc=acc: e.tensor_tensor(out=t_[:], in0=acc[:], in1=o_[:], op=ALU.add), [acc, o_], [t_])
                    op("dve", lambda e, t_=t_, gst_=gst_: e.bn_stats(out=gst_[:], in_=t_[:]), [t_], [gst_])
                    op("dve", lambda e, gst_=gst_, gmv_=gmv_: e.bn_aggr(out=gmv_[:], in_=gst_[:]), [gst_], [gmv_])
                    op("dve", lambda e, gmv_=gmv_, grs_=grs_: e.tensor_scalar(out=grs_[:], in0=gmv_[:, 1:2], scalar1=LN_EPS, scalar2=None, op0=ALU.add), [gmv_], [grs_])
                    op("act", lambda e, grs_=grs_: e.activation(out=grs_[:], in_=grs_[:], func=AF.Sqrt), [grs_], [grs_])
                    while pend2:
                        pend2.pop(0)()
                    pend2.append(lambda h=h, t_=t_, g_=g_, grs_=grs_, gmv_=gmv_, gnm_=gnm_: gn_part2(h, t_, g_, grs_, gmv_, gnm_))
                    if h == 3 and ci + 1 < len(chunks):
                        prefetch_chunk(ci + 1)
                while pend2:
                    pend2.pop(0)()
                for q8 in range(4):
                    pst = psT[q8 % 2]

                    def f(e, q8=q8, pst=pst):
                        for i in range(8):
                            kk = q8 * 8 + i
                            ins = e.transpose(out=pst[:, i * 128:(i + 1) * 128], in_=og[:, kk * 128:(kk + 1) * 128], identity=identb[:])
                        return ins
                    op("pe", f, [og, identb], [pst])
                    evac(ogT[:, q8 * 8:(q8 + 1) * 8, j * 128:(j + 1) * 128], pst[:].rearrange("p (a t) -> p a t", t=128), [pst], [ogT])
            if j != 0:
                continue
            for cb in range(8):
                w = load_w(wout, cb * 256, 256, 32, WB[('ro', jr)])
                for j2 in range(SBT):
                    tok0 = (sb * SBT + j2) * 128
                    ps = psA[j2 % 2]

                    def f(e, ps=ps, w=w, j2=j2):
                        for kk in range(32):
                            ins = e.matmul(ps[:, 0:256], lhsT=ogT[:, kk, j2 * 128:(j2 + 1) * 128], rhs=w[:, kk, :], start=(kk == 0), stop=(kk == 31))
                        return ins
                    op("pe", f, [ogT, w], [ps])
                    st = xstg[xctr[0] % 2]
                    xctr[0] += 1
                    dma("sp", st[:, 0:256], xsrc[tok0:tok0 + 128, cb * 256:(cb + 1) * 256], [dbuf(xsrc)], [st])
                    op("dve", lambda e, ps=ps, st=st, j2=j2, cb=cb: e.scalar_tensor_tensor(out=rbuf[:, j2, cb * 256:(cb + 1) * 256], in0=st[:, 0:256], scalar=ALPHA,
                                                                                            in1=ps[:, 0:256], op0=ALU.mult, op1=ALU.add), [st, ps], [rbuf])
            for j2 in range(SBT):
                layer_norm_out(li, rbuf, j2, (sb * SBT + j2) * 128, xdst)

    ja = jr = 0
    for li, l in enumerate(layers):
        if l == "a":
            cast_w(li, ("ai", ja), a_win[ja], a_win_b[ja], D, A_IN, 512)
            cast_w(li, ("ao", ja), a_wout[ja], a_wout_b[ja], D, D, 512)
            ja += 1
        else:
            cast_w(li, ("ri", jr), r_win[jr], r_win_b[jr], D, R_IN, 512)
            cast_w(li, ("ro", jr), r_wout[jr], r_wout_b[jr], 2 * D, D, 256)
            jr += 1
    ja = jr = 0
    cur = x_in
    for li, l in enumerate(layers):
        dst = y_out if li == nL - 1 else xs[li % 2]
        cast_some(li, 10 ** 6)
        cur_layer[0] = li
        if l == "a":
            attention_layer(li, ja, cur, dst)
            ja += 1
        else:
            retention_layer(li, jr, cur, dst)
            jr += 1
        cur = dst

    sc.finish()
    sc.emit_all(nc, stack)
    stack.close()
    return nc, sc


def make_consts(S):
    cosa, sina = rope_tables(S, 128)
    cosr, sinr = rope_tables(S, 256)
    n = np.arange(128, dtype=np.float32)
    diff = n[None, :] - n[:, None]
    p1 = np.maximum(diff, 0).astype(np.float32)
    p2 = np.maximum(-diff, 0).astype(np.float32)
    posv = np.stack([n + 1, 128 - n, 127 - n, n], axis=1).astype(np.float32)
    prow = np.concatenate([np.tile((n + 1)[None, :], (128, 1)), np.tile((128 - n)[None, :], (128, 1))], axis=1).astype(np.float32)
    return dict(cosa=cosa, sina=sina, cosr=cosr, sinr=sinr, identf_d=np.eye(128, dtype=np.float32), p1_d=p1, p2_d=p2, posv_d=posv, prow_d=prow)


def run(x, attn_w_in, attn_q_gain, attn_k_gain, attn_w_out, ret_w_in, ret_decay_exp, ret_w_out, ln_gain, ln_bias, layers):
    B, S, _ = x.shape
    nc, sc = build_program(S, layers)
    consts = make_consts(S)
    n_a = max(1, sum(1 for l in layers if l == "a"))
    n_r = max(1, sum(1 for l in layers if l == "r"))
    nL = len(layers)
    shared = dict(a_win=np.ascontiguousarray(attn_w_in[:n_a]), a_wout=np.ascontiguousarray(attn_w_out[:n_a]),
                  a_qg=np.ascontiguousarray(attn_q_gain[:n_a]), a_kg=np.ascontiguousarray(attn_k_gain[:n_a]),
                  r_win=np.ascontiguousarray(ret_w_in[:n_r]), r_wout=np.ascontiguousarray(ret_w_out[:n_r]),
                  r_dec=np.ascontiguousarray(ret_decay_exp[:n_r].reshape(n_r, 16)),
                  ln_g=np.ascontiguousarray(ln_gain[:nL]), ln_b=np.ascontiguousarray(ln_bias[:nL]), **consts)
    in_maps = [dict(x=np.ascontiguousarray(x[b]), **shared) for b in range(B)]
    res = run_bass_kernel_spmd(nc, in_maps, core_ids=list(range(B)))
    return np.stack([res.results[b]["y"] for b in range(B)], axis=0)


def kernel(x, attn_w_in, attn_q_gain, attn_k_gain, attn_w_out, ret_w_in, ret_decay_exp, ret_w_out, ln_gain, ln_bias):
    out = run(np.asarray(x), np.asarray(attn_w_in), np.asarray(attn_q_gain), np.asarray(attn_k_gain), np.asarray(attn_w_out),
              np.asarray(ret_w_in), np.asarray(ret_decay_exp), np.asarray(ret_w_out), np.asarray(ln_gain), np.asarray(ln_bias),
              layers=["a", "r", "a", "r"])
    return out.astype(np.float32)
```

```python
import math
from contextlib import ExitStack
import numpy as np
import concourse.bass as bass
import concourse.mybir as mybir
from concourse.bass_utils import run_bass_kernel_spmd

F32 = mybir.dt.float32
BF16 = mybir.dt.bfloat16
AF = mybir.ActivationFunctionType
ALU = mybir.AluOpType
AX = mybir.AxisListType

D = 2048
GRID_W = 64
ROPE_THETA = 10000.0
RMS_EPS = 1e-6
LN_EPS = 1e-5
DEPTH = 4
ALPHA = (2.0 * DEPTH) ** 0.25
A_IN = 5120
R_IN = 12288
NCORES = 4


class Buf:
    __slots__ = ("name", "w", "r")

    def __init__(self, name):
        self.name = name
        self.w = None
        self.r = []


class Sched:
    ENG = ("pe", "act", "dve", "pool", "sp")
    NSLOT = 6

    def __init__(self):
        self.streams = {e: [] for e in self.ENG}
        self.cnt = {e: 0 for e in self.ENG}
        self.waited = {e: {} for e in self.ENG}
        self.dma_n = {e: 0 for e in self.ENG}
        self.dma_tok = {e: [None] * self.NSLOT for e in self.ENG}
        self.dma_cnt = {e: [0] * self.NSLOT for e in self.ENG}
        self.nops = 0

    def _need(self, eng, tok, waits):
        if tok is None:
            return
        key, val = tok
        if key == eng and eng in ("pe", "sp"):
            return
        if self.waited[eng].get(key, 0) >= val:
            return
        self.waited[eng][key] = val
        waits.append((key, val))

    def op(self, eng, emit, reads=(), writes=(), dma=False):
        waits = []
        for b in reads:
            self._need(eng, b.w, waits)
        for b in writes:
            self._need(eng, b.w, waits)
            for t in b.r:
                if t[0] == eng and not dma:
                    continue
                self._need(eng, t, waits)
        if dma:
            n = self.dma_n[eng]
            self.dma_n[eng] = n + 1
            slot = n % self.NSLOT
            self._need(eng, self.dma_tok[eng][slot], waits)
            self.dma_cnt[eng][slot] += 16
            tok = (("dma", eng, slot), self.dma_cnt[eng][slot])
            self.dma_tok[eng][slot] = tok
            sig = (tok[0], 16)
        else:
            self.cnt[eng] += 1
            tok = (eng, self.cnt[eng])
            sig = (eng, 1)
        for b in reads:
            b.r.append(tok)
        for b in writes:
            b.w = tok
            b.r = []
        self.streams[eng].append((waits, emit, sig))
        self.nops += 1
        return tok

    def fence(self):
        toks = [(e, self.cnt[e]) for e in ("pe", "act", "dve", "pool") if self.cnt[e] > 0]
        for e in self.ENG:
            for slot in range(self.NSLOT):
                if self.dma_tok[e][slot] is not None:
                    toks.append(self.dma_tok[e][slot])
        for e in self.ENG:
            waits = []
            for t in toks:
                self._need(e, t, waits)
            if waits:
                self.streams[e].append((waits, None, None))

    def finish(self):
        for eng in self.ENG:
            waits = []
            for slot in range(self.NSLOT):
                self._need(eng, self.dma_tok[eng][slot], waits)
            if waits:
                self.streams[eng].append((waits, None, None))

    def emit_all(self, nc, stack):
        sems = {}

        def sem(key):
            if key not in sems:
                nm = "s_" + ("_".join(str(k) for k in key) if isinstance(key, tuple) else key)
                sems[key] = stack.enter_context(nc.semaphore(nm))
            return sems[key]

        for e in self.ENG:
            sem(e)
            for s in range(self.NSLOT):
                if self.dma_cnt[e][s]:
                    sem(("dma", e, s))
        block = stack.enter_context(nc.Block())
        sect = {"pe": block.tensor, "act": block.scalar, "dve": block.vector, "pool": block.gpsimd, "sp": block.sync}

        def mk(stream):
            def body(engine):
                for waits, emit, sig in stream:
                    for key, val in waits:
                        engine.wait_ge(sems[key], val)
                    if emit is not None:
                        ins = emit(engine)
                        ins.then_inc(sems[sig[0]], sig[1])
            return body

        for e in self.ENG:
            if self.streams[e]:
                sect[e](mk(self.streams[e]))


def rope_tables(S, hd):
    rows_n = S // GRID_W
    row = np.repeat(np.arange(rows_n, dtype=np.float32), GRID_W)
    col = np.tile(np.arange(GRID_W, dtype=np.float32), rows_n)
    half = hd // 2
    inv_freq = (np.float32(ROPE_THETA) ** (-np.arange(0, half, 2, dtype=np.float32) / np.float32(half))).astype(np.float32)
    ang_r = row[:, None] * inv_freq
    ang_c = col[:, None] * inv_freq
    ang = np.concatenate([ang_r, ang_r, ang_c, ang_c], axis=-1).astype(np.float32)
    cos = np.cos(ang).astype(np.float32)
    sin = np.sin(ang).astype(np.float32)
    q4 = hd // 4
    sgn = np.concatenate([-np.ones(q4), np.ones(q4), -np.ones(q4), np.ones(q4)]).astype(np.float32)
    return cos, (sin * sgn[None, :]).astype(np.float32)


def build_program(S, layers):
    NT = S // 128
    n_a = sum(1 for l in layers if l == "a")
    n_r = sum(1 for l in layers if l == "r")
    nL = len(layers)
    nc = bass.Bass("TRN2", target_bir_lowering=False)
    sc = Sched()

    def din(name, shape, dt=F32):
        return nc.dram_tensor(name, list(shape), dt, kind="ExternalInput").ap()

    def dint(name, shape, dt):
        return nc.dram_tensor(name, list(shape), dt, kind="Internal").ap()

    x_in = din("x", [S, D])
    y_out = nc.dram_tensor("y", [S, D], F32, kind="ExternalOutput").ap()
    a_win = din("a_win", [max(n_a, 1), D, A_IN])
    a_wout = din("a_wout", [max(n_a, 1), D, D])
    a_qg = din("a_qg", [max(n_a, 1), 128])
    a_kg = din("a_kg", [max(n_a, 1), 128])
    r_win = din("r_win", [max(n_r, 1), D, R_IN])
    r_wout = din("r_wout", [max(n_r, 1), 2 * D, D])
    r_dec = din("r_dec", [max(n_r, 1), 16])
    ln_g = din("ln_g", [nL, D])
    ln_b = din("ln_b", [nL, D])
    cosa_d = din("cosa", [S, 128])
    sina_d = din("sina", [S, 128])
    cosr_d = din("cosr", [S, 256])
    sinr_d = din("sinr", [S, 256])
    identf_d = din("identf_d", [128, 128])
    p1_d = din("p1_d", [128, 128])
    p2_d = din("p2_d", [128, 128])
    posv_d = din("posv_d", [128, 4])
    prow_d = din("prow_d", [128, 256])

    a_win_b = dint("a_win_b", [max(n_a, 1), A_IN // 512, 128, 16, 512], BF16)
    a_wout_b = dint("a_wout_b", [max(n_a, 1), D // 512, 128, 16, 512], BF16)
    r_win_b = dint("r_win_b", [max(n_r, 1), R_IN // 512, 128, 16, 512], BF16)
    r_wout_b = dint("r_wout_b", [max(n_r, 1), D // 256, 128, 32, 256], BF16)
    xs = [dint("xs0", [S, D], F32), dint("xs1", [S, D], F32)]
    if n_r:
        r_qT = dint("r_qT", [NT, 128, 2 * 8 * 128], BF16)
        r_k = dint("r_k", [S, 2048], BF16)
        r_v = dint("r_v", [S, 4096], BF16)
        r_g = dint("r_g", [S, 4096], BF16)
        r_o1 = dint("r_o1", [S, 4096], F32)

    stack = ExitStack()
    DB = {}

    def dbuf(ap):
        k = ap.name if hasattr(ap, "name") else id(ap)
        if k not in DB:
            DB[k] = Buf(str(k))
        return DB[k]

    NA = 32768
    arena = stack.enter_context(nc.sbuf_tensor("arena", [128, NA], F32))
    aoff = [0]

    class T:
        def __init__(self, name, shape, dt, psum=False, persist=False, view=None, buf=None):
            if view is not None:
                self.t = view
                self.b = buf
                return
            shape = list(shape)
            if psum:
                self.t = stack.enter_context(nc.psum_tensor(name, shape, dt))
            elif persist:
                self.t = stack.enter_context(nc.sbuf_tensor(name, shape, dt))
            else:
                n = 1
                for d_ in shape[1:]:
                    n *= d_
                nw = n if dt == F32 else (n + 1) // 2
                nw = (nw + 7) // 8 * 8
                assert aoff[0] + nw <= NA, ("arena overflow", name, aoff[0], nw)
                v = arena[:, aoff[0]:aoff[0] + nw]
                aoff[0] += nw
                if dt != F32:
                    v = v.bitcast(dt)
                v = v[:, 0:n]
                if len(shape) == 3:
                    v = v.rearrange("p (a b) -> p a b", b=shape[2])
                elif len(shape) == 4:
                    v = v.rearrange("p (a b c) -> p a b c", b=shape[2], c=shape[3])
                self.t = v
            self.b = Buf(name)

        def __getitem__(self, k):
            return self.t[k]

    def new_phase():
        sc.fence()
        aoff[0] = 0

    def op(eng, fn, reads=(), writes=(), dma=False):
        rb = [x.b if isinstance(x, T) else x for x in reads]
        wb = [x.b if isinstance(x, T) else x for x in writes]
        return sc.op(eng, fn, rb, wb, dma)

    def dma(q, out, in_, reads, writes):
        op(q, lambda e: e.dma_start(out=out, in_=in_), reads, writes, dma=True)

    identf = T("identf", [128, 128], F32, persist=True)
    identb = T("identb", [128, 128], BF16, persist=True)
    onesb = T("onesb", [128, 128], BF16, persist=True)
    dma("sp", identf[:], identf_d[:, :], [], [identf])
    op("dve", lambda e: e.tensor_copy(out=identb[:], in_=identf[:]), [identf], [identb])
    op("dve", lambda e: e.memset(onesb[:], 1.0), [], [onesb])

    WB = {}
    cast_todo = {}

    def cast_w(li_, key, src, dst, rows, cols, cw):
        WB[key] = Buf(str(key))
        for kc in range(rows // 128):
            s_ = src[kc * 128:(kc + 1) * 128, :].rearrange("p (cb c) -> p cb c", c=cw)
            d_ = dst[:, :, kc, :].rearrange("cb p c -> p cb c")
            cast_todo.setdefault(li_, []).append(lambda d_=d_, s_=s_, key=key: dma("pool", d_, s_, [], [WB[key]]))

    def cast_some(li_, n):
        lst = cast_todo.get(li_, [])
        while lst and n > 0:
            lst.pop(0)()
            n -= 1

    psA2 = stack.enter_context(nc.psum_tensor("psA2", [128, 1024], F32))
    psB2 = stack.enter_context(nc.psum_tensor("psB2", [128, 1024], F32))
    psA = [T(None, None, None, view=psA2[:, i * 512:(i + 1) * 512], buf=Buf("psA%d" % i)) for i in range(2)]
    psB = [T(None, None, None, view=psB2[:, i * 512:(i + 1) * 512], buf=Buf("psB%d" % i)) for i in range(2)]
    psW = [(psA2, psA), (psB2, psB)]
    psT = [T("psT0", [128, 1024], BF16, True), T("psT1", [128, 1024], BF16, True)]
    psTf = [T(None, None, None, view=psT[i].t[:].bitcast(F32), buf=psT[i].b) for i in range(2)]
    psO = T("psO", [128, 512], F32, True)
    psL = T("psL", [128, 512], F32, True)

    wbuf = [T("wbuf0", [128, 16, 512], BF16, persist=True), T("wbuf1", [128, 16, 512], BF16, persist=True)]
    wctr = [0]
    xstg = [T("xstg0", [128, 512], F32, persist=True), T("xstg1", [128, 512], F32, persist=True)]
    xctr = [0]
    t1_g = T("t1", [128, 512], F32, persist=True)
    t2_g = T("t2", [128, 512], F32, persist=True)
    lng = [T("lng0", [128, 512], F32, persist=True), t1_g]
    lnb = [T("lnb0", [128, 512], F32, persist=True), t2_g]
    small = {"t1": t1_g, "t2": t2_g}

    def sm(name, shape, dt=F32):
        if name not in small:
            small[name] = T(name, shape, dt, persist=True)
        return small[name]

    evac_ctr = [0]
    cur_layer = [0]

    def trickle():
        cast_some(cur_layer[0] + 1, 2)

    def evac(out, in_, reads, writes):
        evac_ctr[0] += 1
        if evac_ctr[0] % 2:
            op("act", lambda e: e.activation(out=out, in_=in_, func=AF.Copy), reads, writes)
        else:
            op("dve", lambda e: e.tensor_copy(out=out, in_=in_), reads, writes)

    def load_w(src2d, c0, ncol, nk, wb=None):
        w = wbuf[wctr[0] % 2]
        wctr[0] += 1
        assert nk * ncol <= 16 * 512
        view = w.t[:].rearrange("p a b -> p (a b)")[:, 0:nk * ncol].rearrange("p (a b) -> p a b", b=ncol)
        dma("sp", view, src2d[c0 // ncol], [wb], [w])

        return T(None, None, None, view=view, buf=w.b)

    def load_xT(xsrc, t0, ntile, xT):
        for j in range(ntile):
            tok0 = (t0 + j) * 128
            for c4 in range(4):
                st = xstg[xctr[0] % 2]
                xctr[0] += 1
                dma("sp", st[:], xsrc[tok0:tok0 + 128, c4 * 512:(c4 + 1) * 512], [dbuf(xsrc)], [st])
                ps = psB[c4 % 2]

                def f(e, ps=ps, st=st):
                    for i in range(4):
                        ins = e.transpose(out=ps[:, i * 128:(i + 1) * 128], in_=st[:, i * 128:(i + 1) * 128], identity=identf[:])
                    return ins
                op("pe", f, [st, identf], [ps])
                evac(xT[:, c4 * 4:(c4 + 1) * 4, j * 128:(j + 1) * 128],
                     ps[:].rearrange("p (a b) -> p a b", b=128), [ps], [xT])

    def proj_tok(ps, xT, j, w, nk=16, ncol=512):
        def f(e):
            for kc in range(nk):
                ins = e.matmul(ps[:, 0:ncol], lhsT=xT[:, kc, j * 128:(j + 1) * 128], rhs=w[:, kc, 0:ncol],
                               start=(kc == 0), stop=(kc == nk - 1))
            return ins
        op("pe", f, [xT, w], [ps])

    def layer_norm_out(li, rbuf, j, tok0, xdst):
        stats = sm("ln_stats", [128, 4, 6])
        mv = sm("ln_mv", [128, 2])
        rstd = sm("ln_rstd", [128, 1])
        nmr = sm("ln_nmr", [128, 1])
        for c in range(4):
            op("dve", lambda e, c=c: e.bn_stats(out=stats[:, c, :], in_=rbuf[:, j, c * 512:(c + 1) * 512]), [rbuf], [stats])
        op("dve", lambda e: e.bn_aggr(out=mv[:], in_=stats[:].rearrange("p a b -> p (a b)")), [stats], [mv])
        op("dve", lambda e: e.tensor_scalar(out=rstd[:], in0=mv[:, 1:2], scalar1=LN_EPS, scalar2=None, op0=ALU.add), [mv], [rstd])
        op("act", lambda e: e.activation(out=rstd[:], in_=rstd[:], func=AF.Sqrt), [rstd], [rstd])
        op("dve", lambda e: e.reciprocal(out=rstd[:], in_=rstd[:]), [rstd], [rstd])
        op("dve", lambda e: e.scalar_tensor_tensor(out=nmr[:], in0=mv[:, 0:1], scalar=-1.0, in1=rstd[:], op0=ALU.mult, op1=ALU.mult),
           [mv, rstd], [nmr])
        op("act", lambda e: e.activation(out=rbuf[:, j, :], in_=rbuf[:, j, :], func=AF.Identity, bias=nmr[:, 0:1], scale=rstd[:, 0:1]),
           [rbuf, nmr, rstd], [rbuf])
        for c in range(4):
            g_ = lng[c % 2]
            b_ = lnb[c % 2]
            dma("pool", g_[:], ln_g[li:li + 1, c * 512:(c + 1) * 512].partition_broadcast(128), [], [g_])
            dma("pool", b_[:], ln_b[li:li + 1, c * 512:(c + 1) * 512].partition_broadcast(128), [], [b_])
            op("pool", lambda e, c=c, g_=g_: e.tensor_tensor(out=rbuf[:, j, c * 512:(c + 1) * 512], in0=rbuf[:, j, c * 512:(c + 1) * 512],
                                                             in1=g_[:], op=ALU.mult), [rbuf, g_], [rbuf])
            op("dve", lambda e, c=c, b_=b_: e.tensor_tensor(out=rbuf[:, j, c * 512:(c + 1) * 512], in0=rbuf[:, j, c * 512:(c + 1) * 512],
                                                            in1=b_[:], op=ALU.add), [rbuf, b_], [rbuf])
        dma("pool", xdst[tok0:tok0 + 128, :], rbuf[:, j, :], [rbuf], [dbuf(xdst)])

    def attention_layer(li, ja, xsrc, xdst):
        new_phase()
        SBT = 2
        NSB = NT // SBT
        win = a_win_b[ja]
        wout = a_wout_b[ja]
        KT = T("KT", [128, 4, S], BF16)
        V = T("V", [128, NT, 512], BF16)
        xTs = [T("xT0", [128, 16, SBT * 128], BF16), T("xT1", [128, 16, SBT * 128], BF16)]
        qT = T("qT", [128, 16, SBT * 128], BF16)
        gT = T("gT", [128, 16, SBT * 128], BF16)
        rbuf = T("rbuf", [128, SBT, D], F32)
        PT = [T("PT%d" % i, [128, 1024], BF16) for i in range(3)]
        cosb = [sm("cosb0", [128, 128]), sm("cosb1", [128, 128])]
        sinb = [sm("sinb0", [128, 128]), sm("sinb1", [128, 128])]
        gq = sm("gq_rep", [128, 128])
        gk = sm("gk_rep", [128, 128])
        negb = sm("negb", [128, 1])
        mq = sm("mq", [128, 1])
        mk_ = sm("mk", [128, 1])
        sq = T("sq", [128, 512], F32)
        ss = sm("ss", [128, 4])
        rs = sm("rs", [128, 4])
        qg = T("qg", [128, 512], F32)
        t1 = sm("t1", [128, 512])
        t2 = sm("t2", [128, 512])
        qbs = [sm("qb0", [128, 512], BF16), sm("qb1", [128, 512], BF16)]
        pend = []
        rl = T("rl", [128, 512], F32)
        tmpo = T("tmpo", [128, 512], F32)

        dma("sp", gq[:], a_qg[ja:ja + 1, :].partition_broadcast(128), [], [gq])
        dma("sp", gk[:], a_kg[ja:ja + 1, :].partition_broadcast(128), [], [gk])
        op("dve", lambda e: e.tensor_reduce(out=mq[:], in_=gq[:], axis=AX.X, op=ALU.max, apply_absolute_value=True), [gq], [mq])
        op("dve", lambda e: e.tensor_reduce(out=mk_[:], in_=gk[:], axis=AX.X, op=ALU.max, apply_absolute_value=True), [gk], [mk_])
        op("dve", lambda e: e.scalar_tensor_tensor(out=negb[:], in0=mq[:], scalar=-math.sqrt(128.0), in1=mk_[:], op0=ALU.mult, op1=ALU.mult),
           [mq, mk_], [negb])

        def qk_post(ps, gain, scale, tok0, cs, qb):
            cb_, sb_ = cosb[cs % 2], sinb[cs % 2]
            op("act", lambda e: e.activation(out=sq[:], in_=ps[:], func=AF.Square), [ps], [sq])
            op("dve", lambda e: e.tensor_reduce(out=ss[:], in_=sq[:].rearrange("p (h d) -> p h d", d=128), axis=AX.X, op=ALU.add), [sq], [ss])
            op("dve", lambda e: e.tensor_scalar(out=rs[:], in0=ss[:], scalar1=1.0 / (128.0 * scale * scale), scalar2=RMS_EPS / (scale * scale),
                                                op0=ALU.mult, op1=ALU.add), [ss], [rs])
            op("act", lambda e: e.activation(out=rs[:], in_=rs[:], func=AF.Sqrt), [rs], [rs])
            op("dve", lambda e: e.reciprocal(out=rs[:], in_=rs[:]), [rs], [rs])
            op("dve", lambda e: e.tensor_tensor(out=qg[:].rearrange("p (h d) -> p h d", d=128), in0=ps[:].rearrange("p (h d) -> p h d", d=128),
                                                in1=gain[:].unsqueeze(1).to_broadcast([128, 4, 128]), op=ALU.mult), [ps, gain], [qg])
            op("dve", lambda e: e.tensor_tensor(out=t1[:].rearrange("p (h d) -> p h d", d=128), in0=qg[:].rearrange("p (h d) -> p h d", d=128),
                                                in1=cb_[:].unsqueeze(1).to_broadcast([128, 4, 128]), op=ALU.mult), [qg, cb_], [t1])
            qv = qg[:].rearrange("p (g s d) -> p g s d", s=2, d=32)
            tv = t2[:].rearrange("p (g s d) -> p g s d", s=2, d=32)
            sv = sb_[:].rearrange("p (g s d) -> p g s d", s=2, d=32)
            qv5 = qg[:].rearrange("p (h g s d) -> p h g s d", h=4, g=2, s=2, d=32)
            tv5 = t2[:].rearrange("p (h g s d) -> p h g s d", h=4, g=2, s=2, d=32)
            for s_ in range(2):
                op("dve", lambda e, s_=s_: e.tensor_tensor(out=tv5[:, :, :, s_, :], in0=qv5[:, :, :, 1 - s_, :],
                                                            in1=sv[:, :, s_, :].unsqueeze(1).to_broadcast([128, 4, 2, 32]), op=ALU.mult), [qg, sb_], [t2])
            op("dve", lambda e: e.tensor_tensor(out=t1[:], in0=t1[:], in1=t2[:], op=ALU.add), [t1, t2], [t1])
            op("dve", lambda e: e.tensor_tensor(out=qb[:].rearrange("p (h d) -> p h d", d=128), in0=t1[:].rearrange("p (h d) -> p h d", d=128),
                                                in1=rs[:].unsqueeze(2).to_broadcast([128, 4, 128]), op=ALU.mult), [t1, rs], [qb])

        def load_tables(tok0, cs):
            dma("sp", cosb[cs % 2][:], cosa_d[tok0:tok0 + 128, :], [], [cosb[cs % 2]])
            dma("sp", sinb[cs % 2][:], sina_d[tok0:tok0 + 128, :], [], [sinb[cs % 2]])

        def transpose4(qb, pst):
            def f(e):
                for h in range(4):
                    ins = e.transpose(out=pst[:, h * 128:(h + 1) * 128], in_=qb[:, h * 128:(h + 1) * 128], identity=identb[:])
                return ins
            op("pe", f, [qb, identb], [pst])

        cs = 0
        load_xT(xsrc, 0, SBT, xTs[0])
        for sb in range(NSB):
            trickle()
            xT = xTs[sb % 2]
            wk = load_w(win, 2048, 512, 16, WB[('ai', ja)])
            wv = load_w(win, 2560, 512, 16, WB[('ai', ja)])
            for j in range(SBT):
                t = sb * SBT + j
                tok0 = t * 128
                load_tables(tok0, cs)
                ps = psA[t % 2]
                proj_tok(ps, xT, j, wk)
                qb = qbs[cs % 2]
                qk_post(ps, gk, 1.0, tok0, cs, qb)
                cs += 1
                ps = (psO, psL)[t % 2]
                proj_tok(ps, xT, j, wv)
                evac(V[:, t, :], ps[:], [ps], [V])
                while pend:
                    pend.pop(0)()

                def fin(qb=qb, t=t, tok0=tok0):
                    pst = psT[t % 2]
                    transpose4(qb, pst)
                    evac(KT[:, :, tok0:tok0 + 128], pst[:, 0:512].rearrange("p (h t) -> p h t", t=128), [pst], [KT])
                pend.append(fin)
            load_xT(xsrc, ((sb + 1) % NSB) * SBT, SBT, xTs[(sb + 1) % 2])
        while pend:
            pend.pop(0)()

        for sb in range(NSB):
            trickle()
            xT = xTs[(NSB + sb) % 2]
            for cb in range(4):
                w = load_w(win, cb * 512, 512, 16, WB[('ai', ja)])
                for j in range(SBT):
                    tok0 = (sb * SBT + j) * 128
                    load_tables(tok0, cs)
                    ps = psA[cs % 2]
                    proj_tok(ps, xT, j, w)
                    qb = qbs[cs % 2]
                    qk_post(ps, gq, 128.0 ** -0.5, tok0, cs, qb)
                    while pend:
                        pend.pop(0)()

                    def fin(qb=qb, cs_=cs, cb=cb, j=j):
                        pst = psT[cs_ % 2]
                        transpose4(qb, pst)
                        evac(qT[:, cb * 4:(cb + 1) * 4, j * 128:(j + 1) * 128], pst[:, 0:512].rearrange("p (h t) -> p h t", t=128), [pst], [qT])
                    pend.append(fin)
                    cs += 1
            for cb in range(4):
                w = load_w(win, 3072 + cb * 512, 512, 16, WB[('ai', ja)])
                for h in range(4):
                    ps = psA[h % 2]

                    def f(e, ps=ps, w=w, h=h, xT=xT):
                        for kc in range(16):
                            ins = e.matmul(ps[:, 0:SBT * 128], lhsT=w[:, kc, h * 128:(h + 1) * 128], rhs=xT[:, kc, :],
                                           start=(kc == 0), stop=(kc == 15))
                        return ins
                    op("pe", f, [xT, w], [ps])
                    op("act", lambda e, ps=ps, cb=cb, h=h: e.activation(out=gT[:, cb * 4 + h, :], in_=ps[:, 0:SBT * 128], func=AF.Silu), [ps], [gT])
            while pend:
                pend.pop(0)()
            if sb + 1 < NSB:
                load_xT(xsrc, (sb + 1) * SBT, SBT, xTs[(NSB + sb + 1) % 2])
            gctr = [0]
            for j in range(SBT):
                for g in range(4):
                    rhs_q = qT[:, g * 4:(g + 1) * 4, j * 128:(j + 1) * 128]

                    def s_mm(i, g=g, rhs_q=rhs_q):
                        ps2, halves = psW[i % 2]

                        def f(e):
                            for u_ in range(2):
                                kt = 2 * i + u_
                                ins = e.matmul(halves[u_][:], lhsT=KT[:, g, kt * 128:(kt + 1) * 128], rhs=rhs_q, start=True, stop=True)
                            return ins
                        op("pe", f, [KT, qT], [halves[0], halves[1]])
                        pt = PT[i % 3]
                        op("act", lambda e: e.activation(out=pt[:], in_=ps2[:, :], func=AF.Exp, bias=negb[:, 0:1], scale=1.0), [halves[0], halves[1], negb], [pt])

                    gctr[0] += 1
                    pO, pL = (psO, psL) if gctr[0] % 2 else (psTf[0], psTf[1])

                    def pv_mm(i, g=g, pO=pO, pL=pL):
                        pt = PT[i % 3]

                        def f(e):
                            for u_ in range(2):
                                kt = 2 * i + u_
                                e.matmul(pO[:], lhsT=V[:, kt, g * 128:(g + 1) * 128], rhs=pt[:, u_ * 512:(u_ + 1) * 512], start=(kt == 0), stop=(kt == NT - 1))
                                ins = e.matmul(pL[:], lhsT=onesb[:], rhs=pt[:, u_ * 512:(u_ + 1) * 512], start=(kt == 0), stop=(kt == NT - 1))
                            return ins
                        op("pe", f, [V, pt, onesb], [pO, pL])

                    NP = NT // 2
                    for i in range(NP):
                        s_mm(i)
                        if i >= 1:
                            pv_mm(i - 1)
                    pv_mm(NP - 1)
                    op("dve", lambda e, pL=pL: e.reciprocal(out=rl[:], in_=pL[:]), [pL], [rl])
                    op("dve", lambda e, pO=pO: e.tensor_tensor(out=tmpo[:], in0=pO[:], in1=rl[:], op=ALU.mult), [pO, rl], [tmpo])
                    gview = gT[:, g * 4:(g + 1) * 4, j * 128:(j + 1) * 128]
                    op("dve", lambda e, gview=gview: e.tensor_tensor(out=gview, in0=gview, in1=tmpo[:].rearrange("p (h t) -> p h t", t=128), op=ALU.mult),
                       [gT, tmpo], [gT])
            for cb in range(4):
                w = load_w(wout, cb * 512, 512, 16, WB[('ao', ja)])
                for j in range(SBT):
                    tok0 = (sb * SBT + j) * 128
                    ps = psA[j % 2]

                    def f(e, ps=ps, w=w, j=j):
                        for h in range(16):
                            ins = e.matmul(ps[:], lhsT=gT[:, h, j * 128:(j + 1) * 128], rhs=w[:, h, :], start=(h == 0), stop=(h == 15))
                        return ins
                    op("pe", f, [gT, w], [ps])
                    st = xstg[xctr[0] % 2]
                    xctr[0] += 1
                    dma("sp", st[:], xsrc[tok0:tok0 + 128, cb * 512:(cb + 1) * 512], [dbuf(xsrc)], [st])
                    op("dve", lambda e, ps=ps, st=st, j=j, cb=cb: e.scalar_tensor_tensor(out=rbuf[:, j, cb * 512:(cb + 1) * 512], in0=st[:], scalar=ALPHA,
                                                                                          in1=ps[:], op0=ALU.mult, op1=ALU.add), [st, ps], [rbuf])
            for j in range(SBT):
                layer_norm_out(li, rbuf, j, (sb * SBT + j) * 128, xdst)

    def retention_layer(li, jr, xsrc, xdst):
        SBT = 2
        NSB = NT // SBT
        win = r_win_b[jr]
        wout = r_wout_b[jr]
        LN2 = math.log(2.0)

        posv = sm("posv", [128, 4])
        p1 = sm("p1", [128, 128])
        p2 = sm("p2", [128, 128])
        dec = sm("dec_rep", [128, 16])
        u = sm("dec_u", [128, 16])
        tt = sm("dec_t", [128, 16])
        lg = sm("dec_lg", [128, 16])
        dqf = sm("dqf", [128, 8])
        dqb = sm("dqb", [128, 8])
        kdf = sm("kdf", [128, 8])
        kdb = sm("kdb", [128, 8])
        gc = sm("gc", [128, 16])
        arg = sm("dc_arg", [128, 128])
        dma("sp", posv[:], posv_d[:, :], [], [posv])
        dma("sp", p1[:], p1_d[:, :], [], [p1])
        dma("sp", p2[:], p2_d[:, :], [], [p2])
        dma("sp", dec[:], r_dec[jr:jr + 1, :].partition_broadcast(128), [], [dec])
        op("act", lambda e: e.activation(out=u[:], in_=dec[:], func=AF.Exp, scale=-LN2), [dec], [u])
        op("dve", lambda e: e.tensor_scalar(out=tt[:], in0=u[:], scalar1=0.25, scalar2=1.0 / 3.0, op0=ALU.mult, op1=ALU.add), [u], [tt])
        op("dve", lambda e: e.tensor_tensor(out=tt[:], in0=tt[:], in1=u[:], op=ALU.mult), [tt, u], [tt])
        op("dve", lambda e: e.tensor_scalar(out=tt[:], in0=tt[:], scalar1=0.5, scalar2=None, op0=ALU.add), [tt], [tt])
        op("dve", lambda e: e.tensor_tensor(out=tt[:], in0=tt[:], in1=u[:], op=ALU.mult), [tt, u], [tt])
        op("dve", lambda e: e.tensor_scalar(out=tt[:], in0=tt[:], scalar1=1.0, scalar2=None, op0=ALU.add), [tt], [tt])
        op("dve", lambda e: e.scalar_tensor_tensor(out=lg[:], in0=tt[:], scalar=-1.0, in1=u[:], op0=ALU.mult, op1=ALU.mult), [tt, u], [lg])
        op("act", lambda e: e.activation(out=dqf[:], in_=lg[:, 0:8], func=AF.Exp, scale=posv[:, 0:1]), [lg, posv], [dqf])
        op("act", lambda e: e.activation(out=dqb[:], in_=lg[:, 8:16], func=AF.Exp, scale=posv[:, 1:2]), [lg, posv], [dqb])
        op("act", lambda e: e.activation(out=kdf[:], in_=lg[:, 0:8], func=AF.Exp, scale=posv[:, 2:3]), [lg, posv], [kdf])
        op("act", lambda e: e.activation(out=kdb[:], in_=lg[:, 8:16], func=AF.Exp, scale=posv[:, 3:4]), [lg, posv], [kdb])
        op("act", lambda e: e.activation(out=gc[:], in_=lg[:], func=AF.Exp, scale=128.0), [lg], [gc])

        cosr = [sm("cosr0", [128, 256]), sm("cosr1", [128, 256])]
        sinr = [sm("sinr0", [128, 256]), sm("sinr1", [128, 256])]
        t1 = sm("t1", [128, 512])
        t2 = sm("t2", [128, 512])
        cs = [0]

        def rope256(ps, out_ap, outT, scale, tok0):
            c_ = cosr[cs[0] % 2]
            s_ = sinr[cs[0] % 2]
            cs[0] += 1
            dma("sp", c_[:], cosr_d[tok0:tok0 + 128, :], [], [c_])
            dma("sp", s_[:], sinr_d[tok0:tok0 + 128, :], [], [s_])
            op("dve", lambda e: e.tensor_tensor(out=t1[:].rearrange("p (h d) -> p h d", d=256), in0=ps[:].rearrange("p (h d) -> p h d", d=256),
                                                in1=c_[:].unsqueeze(1).to_broadcast([128, 2, 256]), op=ALU.mult), [ps, c_], [t1])
            pv5 = ps[:].rearrange("p (h g s d) -> p h g s d", h=2, g=2, s=2, d=64)
            tv5 = t2[:].rearrange("p (h g s d) -> p h g s d", h=2, g=2, s=2, d=64)
            sv = s_[:].rearrange("p (g s d) -> p g s d", s=2, d=64)
            for k_ in range(2):
                op("dve", lambda e, k_=k_: e.tensor_tensor(out=tv5[:, :, :, k_, :], in0=pv5[:, :, :, 1 - k_, :],
                                                           in1=sv[:, :, k_, :].unsqueeze(1).to_broadcast([128, 2, 2, 64]), op=ALU.mult), [ps, s_], [t2])
            if scale == 1.0:
                op("dve", lambda e: e.tensor_tensor(out=out_ap, in0=t1[:], in1=t2[:], op=ALU.add), [t1, t2], [outT])
            else:
                op("dve", lambda e: e.tensor_tensor(out=t1[:], in0=t1[:], in1=t2[:], op=ALU.add), [t1, t2], [t1])
                op("act", lambda e: e.activation(out=out_ap, in_=t1[:], func=AF.Copy, scale=scale), [t1], [outT])

        new_phase()
        prow = sm("prow", [128, 256])
        dma("sp", prow[:], prow_d[:, :], [], [prow])
        St32 = [T("St32_%d" % h, [128, 2, 512], F32) for h in range(8)]
        Stb = [T("Stb_%d" % h, [128, 2, 512], BF16) for h in range(8)]
        DcT = [T("DcT_%d" % h, [128, 128], F32) for h in range(8)]
        DqT = T("DqfT", [128, 8, 128], F32)
        xTs = [T("xT0", [128, 16, SBT * 128], BF16), T("xT1", [128, 16, SBT * 128], BF16)]
        qTc = [T("qTc%d" % j, [128, 2, 8, 128], BF16) for j in range(SBT)]
        kTc = [T("kTc%d" % j, [128, 2, 8, 128], BF16) for j in range(SBT)]
        kc = [T("kc%d" % j, [128, 2048], BF16) for j in range(SBT)]
        vall = [T("vall%d" % j, [128, 4096], BF16) for j in range(SBT)]
        AT = [T("AT%d" % i, [128, 128], BF16) for i in range(2)]
        gh = [T("gh%d" % i, [128, 512], BF16) for i in range(2)]
        o1buf = [T("o1buf%d" % i, [128, 512], F32) for i in range(2)]
        arg2 = sm("dc_arg2", [128, 128])

        for h in range(8):
            op("dve", lambda e, h=h: e.tensor_scalar(out=arg[:], in0=p1[:], scalar1=lg[:, h:h + 1], scalar2=None, op0=ALU.mult), [p1, lg], [arg])
            op("dve", lambda e, h=h: e.scalar_tensor_tensor(out=arg[:], in0=p2[:], scalar=lg[:, 8 + h:9 + h], in1=arg[:], op0=ALU.mult, op1=ALU.add),
               [p2, lg, arg], [arg])
            op("dve", lambda e, h=h: e.tensor_scalar(out=arg2[:], in0=prow[:, 0:128], scalar1=lg[:, h:h + 1], scalar2=None, op0=ALU.mult), [prow, lg], [arg2])
            op("dve", lambda e: e.tensor_tensor(out=arg[:], in0=arg[:], in1=arg2[:], op=ALU.subtract), [arg, arg2], [arg])
            op("act", lambda e, h=h: e.activation(out=DcT[h][:], in_=arg[:], func=AF.Exp), [arg], [DcT[h]])
            op("act", lambda e, h=h: e.activation(out=DqT[:, h, :], in_=prow[:, 0:128], func=AF.Exp, scale=lg[:, h:h + 1]), [prow, lg], [DqT])
            op("pool", lambda e, h=h: e.memset(St32[h][:], 0.0), [], [St32[h]])
            op("pool", lambda e, h=h: e.memset(Stb[h][:], 0.0), [], [Stb[h]])

        def state_update(St32, Stb, psU, h, kall, vht, gcol):
            S32h = St32[h]
            Sbh = Stb[h]
            for dkc in range(2):
                pu = psU[dkc]
                op("pe", lambda e, dkc=dkc, pu=pu: e.matmul(pu[:], lhsT=kall[:, h * 256 + dkc * 128:h * 256 + (dkc + 1) * 128], rhs=vht, start=True, stop=True),
                   [kall, vht_T[0]], [pu])
                op("dve", lambda e, dkc=dkc, pu=pu: e.scalar_tensor_tensor(out=S32h[:, dkc, :], in0=S32h[:, dkc, :], scalar=gc[:, gcol:gcol + 1],
                                                                          in1=pu[:], op0=ALU.mult, op1=ALU.add), [S32h, gc, pu], [S32h])
            op("act", lambda e: e.activation(out=Sbh[:], in_=S32h[:], func=AF.Copy), [S32h], [Sbh])

        vht_T = [None]
        psX = [psO, psL]
        pend = []
        bctr = [0]
        sctr = [0]
        load_xT(xsrc, 0, SBT, xTs[0])
        for sb in range(NSB):
            trickle()
            xT = xTs[sb % 2]
            for kind in range(2):
                for cb in range(4):
                    w = load_w(win, kind * 2048 + cb * 512, 512, 16, WB[('ri', jr)])
                    for j in range(SBT):
                        tok0 = (sb * SBT + j) * 128
                        bctr[0] += 1
                        bc = bctr[0]
                        ps = psA[bc % 2]
                        proj_tok(ps, xT, j, w)
                        if kind == 0:
                            src_t = gh[bc % 2]
                            src_ap = src_t[:]
                            rope256(ps, src_ap, src_t, 1.0, tok0)
                            dstT = qTc[j]
                        else:
                            src_t, src_ap = kc[j], kc[j][:, cb * 512:(cb + 1) * 512]
                            rope256(ps, src_ap, kc[j], 1.0 / 16.0, tok0)
                            dstT = kTc[j]
                        while pend:
                            pend.pop(0)()

                        def fin(src_t=src_t, src_ap=src_ap, dstT=dstT, bc=bc, cb=cb):
                            pst = psT[bc % 2]

                            def f(e):
                                for dkc in range(2):
                                    for hl in range(2):
                                        i = dkc * 2 + hl
                                        ins = e.transpose(out=pst[:, i * 128:(i + 1) * 128], in_=src_ap[:, hl * 256 + dkc * 128:hl * 256 + (dkc + 1) * 128],
                                                          identity=identb[:])
                                return ins
                            op("pe", f, [src_t, identb], [pst])
                            evac(dstT[:, :, 2 * cb:2 * cb + 2, :], pst[:, 0:512].rearrange("p (a b t) -> p a b t", a=2, b=2), [pst], [dstT])
                        pend.append(fin)
            while pend:
                pend.pop(0)()
            for j in range(SBT):
                c = sb * SBT + j
                dma("pool", r_qT[c], qTc[j][:].rearrange("p a b t -> p (a b t)"), [qTc[j]], [dbuf(r_qT)])
                dma("pool", r_k[c * 128:(c + 1) * 128, :], kc[j][:], [kc[j]], [dbuf(r_k)])
                op("dve", lambda e, j=j: e.tensor_tensor(out=qTc[j][:], in0=qTc[j][:], in1=DqT[:].unsqueeze(1).to_broadcast([128, 2, 8, 128]), op=ALU.mult),
                   [qTc[j], DqT], [qTc[j]])
                op("dve", lambda e, j=j: e.tensor_tensor(out=kc[j][:].rearrange("p (h d) -> p h d", d=256), in0=kc[j][:].rearrange("p (h d) -> p h d", d=256),
                                                         in1=kdf[:].unsqueeze(2).to_broadcast([128, 8, 256]), op=ALU.mult), [kc[j], kdf], [kc[j]])
            for h in range(8):
                w = load_w(win, 4096 + h * 512, 512, 16, WB[('ri', jr)])
                for j in range(SBT):
                    ps = psA[j % 2]
                    proj_tok(ps, xT, j, w)
                    evac(vall[j][:, h * 512:(h + 1) * 512], ps[:], [ps], [vall[j]])
            for j in range(SBT):
                tok0 = (sb * SBT + j) * 128
                dma("pool", r_v[tok0:tok0 + 128, :], vall[j][:], [vall[j]], [dbuf(r_v)])
            steps = [(j, h) for j in range(SBT) for h in range(8)]

            def step_S(j, h, k_):
                psS = psB[k_ % 2]

                def f(e):
                    for dkc in range(2):
                        ins = e.matmul(psS[:, 0:128], lhsT=kTc[j][:, dkc, h, :], rhs=qTc[j][:, dkc, h, :], start=(dkc == 0), stop=(dkc == 1))
                    return ins
                op("pe", f, [kTc[j], qTc[j]], [psS])
                at = AT[k_ % 2]
                op("dve", lambda e: e.tensor_tensor(out=at[:], in0=psS[:, 0:128], in1=DcT[h][:], op=ALU.mult), [psS, DcT[h]], [at])

            def step_rest(j, h, k_):
                tok0 = (sb * SBT + j) * 128
                vht = vall[j][:, h * 512:(h + 1) * 512]
                vht_T[0] = vall[j]
                at = AT[k_ % 2]
                acc = psX[k_ % 2]
                sb_ = Stb[h]

                def f2(e):
                    e.matmul(acc[:], lhsT=at[:], rhs=vht, start=True, stop=False)
                    e.matmul(acc[:], lhsT=qTc[j][:, 0, h, :], rhs=sb_[:, 0, :], start=False, stop=False)
                    return e.matmul(acc[:], lhsT=qTc[j][:, 1, h, :], rhs=sb_[:, 1, :], start=False, stop=True)
                op("pe", f2, [at, vall[j], qTc[j], Stb[h]], [acc])
                ob = o1buf[k_ % 2]
                evac(ob[:], acc[:], [acc], [ob])
                dma("pool", r_o1[tok0:tok0 + 128, h * 512:(h + 1) * 512], ob[:], [ob], [dbuf(r_o1)])
                psU = psA if k_ % 2 else psTf
                state_update(St32, Stb, psU, h, kc[j], vht, h)

            k0 = sctr[0]
            step_S(steps[0][0], steps[0][1], k0)
            for i_, (j, h) in enumerate(steps):
                if i_ + 1 < len(steps):
                    step_S(steps[i_ + 1][0], steps[i_ + 1][1], k0 + i_ + 1)
                step_rest(j, h, k0 + i_)
            sctr[0] += len(steps)
            if sb + 1 < NSB:
                load_xT(xsrc, (sb + 1) * SBT, SBT, xTs[(sb + 1) % 2])
            for h in range(8):
                w = load_w(win, 8192 + h * 512, 512, 16, WB[('ri', jr)])
                for j in range(SBT):
                    tok0 = (sb * SBT + j) * 128
                    ps = psA[j % 2]
                    proj_tok(ps, xT, j, w)
                    g_ = gh[j % 2]
                    op("act", lambda e, g_=g_, ps=ps: e.activation(out=g_[:], in_=ps[:], func=AF.Silu), [ps], [g_])
                    dma("pool", r_g[tok0:tok0 + 128, h * 512:(h + 1) * 512], g_[:], [g_], [dbuf(r_g)])

        new_phase()
        St32b = [T("St32_%d" % h, [128, 2, 512], F32) for h in range(8)]
        Stbb = [T("Stb_%d" % h, [128, 2, 512], BF16) for h in range(8)]
        DqbT = T("DqbT", [128, 8, 128], F32)
        qT1s = [T("qT1_%d" % i, [128, 2, 8, 128], BF16) for i in range(2)]
        kc1s = [T("kc1_%d" % i, [128, 2048], BF16) for i in range(2)]
        vh2 = [T("vh%d" % i, [128, 512], BF16) for i in range(2)]
        gh2 = [T("gh%d" % i, [128, 512], BF16) for i in range(3)]
        og = T("og", [128, 4096], BF16)
        ogT = T("ogT", [128, 32, SBT * 128], BF16)
        o1b = [T("o1b%d" % i, [128, 512], F32) for i in range(2)]
        ot = [T("ot%d" % i, [128, 512], F32) for i in range(2)]
        rbuf = T("rbuf", [128, SBT, D], F32)
        gst = [sm("gn_stats%d" % i, [128, 6]) for i in range(2)]
        gmv = [sm("gn_mv%d" % i, [128, 2]) for i in range(2)]
        grs = [sm("gn_rstd%d" % i, [128, 1]) for i in range(2)]
        gnm = [sm("gn_nmr%d" % i, [128, 1]) for i in range(2)]
        for h in range(8):
            op("act", lambda e, h=h: e.activation(out=DqbT[:, h, :], in_=prow[:, 128:256], func=AF.Exp, scale=lg[:, 8 + h:9 + h]), [prow, lg], [DqbT])
            op("pool", lambda e, h=h: e.memset(St32b[h][:], 0.0), [], [St32b[h]])
            op("pool", lambda e, h=h: e.memset(Stbb[h][:], 0.0), [], [Stbb[h]])

        chunks = [sb * SBT + j for sb in reversed(range(NSB)) for j in reversed(range(SBT))]

        def prefetch_chunk(ci):
            c = chunks[ci]
            q_, k_ = qT1s[ci % 2], kc1s[ci % 2]
            dma("sp", q_[:].rearrange("p a b t -> p (a b t)"), r_qT[c], [dbuf(r_qT)], [q_])
            dma("sp", k_[:], r_k[c * 128:(c + 1) * 128, :], [dbuf(r_k)], [k_])
            op("dve", lambda e: e.tensor_tensor(out=q_[:], in0=q_[:], in1=DqbT[:].unsqueeze(1).to_broadcast([128, 2, 8, 128]), op=ALU.mult),
               [q_, DqbT], [q_])
            op("dve", lambda e: e.tensor_tensor(out=k_[:].rearrange("p (h d) -> p h d", d=256), in0=k_[:].rearrange("p (h d) -> p h d", d=256),
                                                in1=kdb[:].unsqueeze(2).to_broadcast([128, 8, 256]), op=ALU.mult), [k_, kdb], [k_])

        def gn_part2(h, t_, g_, grs_, gmv_, gnm_):
            op("dve", lambda e: e.reciprocal(out=grs_[:], in_=grs_[:]), [grs_], [grs_])
            op("dve", lambda e: e.scalar_tensor_tensor(out=gnm_[:], in0=gmv_[:, 0:1], scalar=-1.0, in1=grs_[:], op0=ALU.mult, op1=ALU.mult),
               [gmv_, grs_], [gnm_])
            op("act", lambda e: e.activation(out=t_[:], in_=t_[:], func=AF.Identity, bias=gnm_[:, 0:1], scale=grs_[:, 0:1]), [t_, gnm_, grs_], [t_])
            op("pool", lambda e: e.tensor_tensor(out=og[:, h * 512:(h + 1) * 512], in0=t_[:], in1=g_[:], op=ALU.mult), [t_, g_], [og])

        k2 = 0
        pend2 = []
        prefetch_chunk(0)
        for ci, c in enumerate(chunks):
            sb, j = c // SBT, c % SBT
            if j == SBT - 1:
                trickle()
            if True:
                tok0 = c * 128
                qT1, kc1 = qT1s[ci % 2], kc1s[ci % 2]
                for h in range(8):
                    k2 += 1
                    v_ = vh2[k2 % 2]
                    g_ = gh2[k2 % 3]
                    o_ = o1b[k2 % 2]
                    t_ = ot[k2 % 2]
                    gst_, gmv_, grs_, gnm_ = gst[k2 % 2], gmv[k2 % 2], grs[k2 % 2], gnm[k2 % 2]
                    dma("sp", v_[:], r_v[tok0:tok0 + 128, h * 512:(h + 1) * 512], [dbuf(r_v)], [v_])
                    dma("sp", g_[:], r_g[tok0:tok0 + 128, h * 512:(h + 1) * 512], [dbuf(r_g)], [g_])
                    dma("sp", o_[:], r_o1[tok0:tok0 + 128, h * 512:(h + 1) * 512], [dbuf(r_o1)], [o_])
                    acc = psX[k2 % 2]

                    def f2(e, h=h, sb_=Stbb[h], acc=acc, qT1=qT1):
                        for dkc in range(2):
                            ins = e.matmul(acc[:], lhsT=qT1[:, dkc, h, :], rhs=sb_[:, dkc, :], start=(dkc == 0), stop=(dkc == 1))
                        return ins
                    op("pe", f2, [qT1, Stbb[h]], [acc])
                    vht_T[0] = v_
                    psU = psA if k2 % 2 else psB
                    state_update(St32b, Stbb, psU, h, kc1, v_[:], 8 + h)
                    op("dve", lambda e, t_=t_, o_=o_, acc=acc: e.tensor_tensor(out=t_[:], in0=acc[:], in1=o_[:], op=ALU.add), [acc, o_], [t_])
                    op("dve", lambda e, t_=t_, gst_=gst_: e.bn_stats(out=gst_[:], in_=t_[:]), [t_], [gst_])
                    op("dve", lambda e, gst_=gst_, gmv_=gmv_: e.bn_aggr(out=gmv_[:], in_=gst_[:]), [gst_], [gmv_])
                    op("dve", lambda e, gmv_=gmv_, grs_=grs_: e.tensor_scalar(out=grs_[:], in0=gmv_[:, 1:2], scalar1=LN_EPS, scalar2=None, op0=ALU.add), [gmv_], [grs_])
                    op("act", lambda e, grs_=grs_: e.activation(out=grs_[:], in_=grs_[:], func=AF.Sqrt), [grs_], [grs_])
                    while pend2:
                        pend2.pop(0)()
                    pend2.append(lambda h=h, t_=t_, g_=g_, grs_=grs_, gmv_=gmv_, gnm_=gnm_: gn_part2(h, t_, g_, grs_, gmv_, gnm_))
                    if h == 3 and ci + 1 < len(chunks):
                        prefetch_chunk(ci + 1)
                while pend2:
                    pend2.pop(0)()
                for q8 in range(4):
                    pst = psT[q8 % 2]

                    def f(e, q8=q8, pst=pst):
                        for i in range(8):
                            kk = q8 * 8 + i
                            ins = e.transpose(out=pst[:, i * 128:(i + 1) * 128], in_=og[:, kk * 128:(kk + 1) * 128], identity=identb[:])
                        return ins
                    op("pe", f, [og, identb], [pst])
                    evac(ogT[:, q8 * 8:(q8 + 1) * 8, j * 128:(j + 1) * 128], pst[:].rearrange("p (a t) -> p a t", t=128), [pst], [ogT])
            if j != 0:
                continue
            for cb in range(8):
                w = load_w(wout, cb * 256, 256, 32, WB[('ro', jr)])
                for j2 in range(SBT):
                    tok0 = (sb * SBT + j2) * 128
                    ps = psA[j2 % 2]

                    def f(e, ps=ps, w=w, j2=j2):
                        for kk in range(32):
                            ins = e.matmul(ps[:, 0:256], lhsT=ogT[:, kk, j2 * 128:(j2 + 1) * 128], rhs=w[:, kk, :], start=(kk == 0), stop=(kk == 31))
                        return ins
                    op("pe", f, [ogT, w], [ps])
                    st = xstg[xctr[0] % 2]
                    xctr[0] += 1
                    dma("sp", st[:, 0:256], xsrc[tok0:tok0 + 128, cb * 256:(cb + 1) * 256], [dbuf(xsrc)], [st])
                    op("dve", lambda e, ps=ps, st=st, j2=j2, cb=cb: e.scalar_tensor_tensor(out=rbuf[:, j2, cb * 256:(cb + 1) * 256], in0=st[:, 0:256], scalar=ALPHA,
                                                                                            in1=ps[:, 0:256], op0=ALU.mult, op1=ALU.add), [st, ps], [rbuf])
            for j2 in range(SBT):
                layer_norm_out(li, rbuf, j2, (sb * SBT + j2) * 128, xdst)

    ja = jr = 0
    for li, l in enumerate(layers):
        if l == "a":
            cast_w(li, ("ai", ja), a_win[ja], a_win_b[ja], D, A_IN, 512)
            cast_w(li, ("ao", ja), a_wout[ja], a_wout_b[ja], D, D, 512)
            ja += 1
        else:
            cast_w(li, ("ri", jr), r_win[jr], r_win_b[jr], D, R_IN, 512)
            cast_w(li, ("ro", jr), r_wout[jr], r_wout_b[jr], 2 * D, D, 256)
            jr += 1
    ja = jr = 0
    cur = x_in
    for li, l in enumerate(layers):
        dst = y_out if li == nL - 1 else xs[li % 2]
        cast_some(li, 10 ** 6)
        cur_layer[0] = li
        if l == "a":
            attention_layer(li, ja, cur, dst)
            ja += 1
        else:
            retention_layer(li, jr, cur, dst)
            jr += 1
        cur = dst

    sc.finish()
    sc.emit_all(nc, stack)
    stack.close()
    return nc, sc


def make_consts(S):
    cosa, sina = rope_tables(S, 128)
    cosr, sinr = rope_tables(S, 256)
    n = np.arange(128, dtype=np.float32)
    diff = n[None, :] - n[:, None]
    p1 = np.maximum(diff, 0).astype(np.float32)
    p2 = np.maximum(-diff, 0).astype(np.float32)
    posv = np.stack([n + 1, 128 - n, 127 - n, n], axis=1).astype(np.float32)
    prow = np.concatenate([np.tile((n + 1)[None, :], (128, 1)), np.tile((128 - n)[None, :], (128, 1))], axis=1).astype(np.float32)
    return dict(cosa=cosa, sina=sina, cosr=cosr, sinr=sinr, identf_d=np.eye(128, dtype=np.float32), p1_d=p1, p2_d=p2, posv_d=posv, prow_d=prow)


def run(x, attn_w_in, attn_q_gain, attn_k_gain, attn_w_out, ret_w_in, ret_decay_exp, ret_w_out, ln_gain, ln_bias, layers):
    B, S, _ = x.shape
    nc, sc = build_program(S, layers)
    consts = make_consts(S)
    n_a = max(1, sum(1 for l in layers if l == "a"))
    n_r = max(1, sum(1 for l in layers if l == "r"))
    nL = len(layers)
    shared = dict(a_win=np.ascontiguousarray(attn_w_in[:n_a]), a_wout=np.ascontiguousarray(attn_w_out[:n_a]),
                  a_qg=np.ascontiguousarray(attn_q_gain[:n_a]), a_kg=np.ascontiguousarray(attn_k_gain[:n_a]),
                  r_win=np.ascontiguousarray(ret_w_in[:n_r]), r_wout=np.ascontiguousarray(ret_w_out[:n_r]),
                  r_dec=np.ascontiguousarray(ret_decay_exp[:n_r].reshape(n_r, 16)),
                  ln_g=np.ascontiguousarray(ln_gain[:nL]), ln_b=np.ascontiguousarray(ln_bias[:nL]), **consts)
    in_maps = [dict(x=np.ascontiguousarray(x[b]), **shared) for b in range(B)]
    res = run_bass_kernel_spmd(nc, in_maps, core_ids=list(range(B)))
    return np.stack([res.results[b]["y"] for b in range(B)], axis=0)


def kernel(x, attn_w_in, attn_q_gain, attn_k_gain, attn_w_out, ret_w_in, ret_decay_exp, ret_w_out, ln_gain, ln_bias):
    out = run(np.asarray(x), np.asarray(attn_w_in), np.asarray(attn_q_gain), np.asarray(attn_k_gain), np.asarray(attn_w_out),
              np.asarray(ret_w_in), np.asarray(ret_decay_exp), np.asarray(ret_w_out), np.asarray(ln_gain), np.asarray(ln_bias),
              layers=["a", "r", "a", "r"])
    return out.astype(np.float32)
```

```python
import math
from contextlib import ExitStack
import numpy as np
import concourse.bass as bass
import concourse.mybir as mybir
from concourse.bass_utils import run_bass_kernel_spmd

F32 = mybir.dt.float32
BF16 = mybir.dt.bfloat16
AF = mybir.ActivationFunctionType
ALU = mybir.AluOpType
AX = mybir.AxisListType

D = 2048
GRID_W = 64
ROPE_THETA = 10000.0
RMS_EPS = 1e-6
LN_EPS = 1e-5
DEPTH = 4
ALPHA = (2.0 * DEPTH) ** 0.25
A_IN = 5120
R_IN = 12288
NCORES = 4


class Buf:
    __slots__ = ("name", "w", "r")

    def __init__(self, name):
        self.name = name
        self.w = None
        self.r = []


class Sched:
    ENG = ("pe", "act", "dve", "pool", "sp")
    NSLOT = 6

    def __init__(self):
        self.streams = {e: [] for e in self.ENG}
        self.cnt = {e: 0 for e in self.ENG}
        self.waited = {e: {} for e in self.ENG}
        self.dma_n = {e: 0 for e in self.ENG}
        self.dma_tok = {e: [None] * self.NSLOT for e in self.ENG}
        self.dma_cnt = {e: [0] * self.NSLOT for e in self.ENG}
        self.nops = 0

    def _need(self, eng, tok, waits):
        if tok is None:
            return
        key, val = tok
        if key == eng and eng in ("pe", "sp"):
            return
        if self.waited[eng].get(key, 0) >= val:
            return
        self.waited[eng][key] = val
        waits.append((key, val))

    def op(self, eng, emit, reads=(), writes=(), dma=False):
        waits = []
        for b in reads:
            self._need(eng, b.w, waits)
        for b in writes:
            self._need(eng, b.w, waits)
            for t in b.r:
                if t[0] == eng and not dma:
                    continue
                self._need(eng, t, waits)
        if dma:
            n = self.dma_n[eng]
            self.dma_n[eng] = n + 1
            slot = n % self.NSLOT
            self._need(eng, self.dma_tok[eng][slot], waits)
            self.dma_cnt[eng][slot] += 16
            tok = (("dma", eng, slot), self.dma_cnt[eng][slot])
            self.dma_tok[eng][slot] = tok
            sig = (tok[0], 16)
        else:
            self.cnt[eng] += 1
            tok = (eng, self.cnt[eng])
            sig = (eng, 1)
        for b in reads:
            b.r.append(tok)
        for b in writes:
            b.w = tok
            b.r = []
        self.streams[eng].append((waits, emit, sig))
        self.nops += 1
        return tok

    def fence(self):
        toks = [(e, self.cnt[e]) for e in ("pe", "act", "dve", "pool") if self.cnt[e] > 0]
        for e in self.ENG:
            for slot in range(self.NSLOT):
                if self.dma_tok[e][slot] is not None:
                    toks.append(self.dma_tok[e][slot])
        for e in self.ENG:
            waits = []
            for t in toks:
                self._need(e, t, waits)
            if waits:
                self.streams[e].append((waits, None, None))

    def finish(self):
        for eng in self.ENG:
            waits = []
            for slot in range(self.NSLOT):
                self._need(eng, self.dma_tok[eng][slot], waits)
            if waits:
                self.streams[eng].append((waits, None, None))

    def emit_all(self, nc, stack):
        sems = {}

        def sem(key):
            if key not in sems:
                nm = "s_" + ("_".join(str(k) for k in key) if isinstance(key, tuple) else key)
                sems[key] = stack.enter_context(nc.semaphore(nm))
            return sems[key]

        for e in self.ENG:
            sem(e)
            for s in range(self.NSLOT):
                if self.dma_cnt[e][s]:
                    sem(("dma", e, s))
        block = stack.enter_context(nc.Block())
        sect = {"pe": block.tensor, "act": block.scalar, "dve": block.vector, "pool": block.gpsimd, "sp": block.sync}

        def mk(stream):
            def body(engine):
                for waits, emit, sig in stream:
                    for key, val in waits:
                        engine.wait_ge(sems[key], val)
                    if emit is not None:
                        ins = emit(engine)
                        ins.then_inc(sems[sig[0]], sig[1])
            return body

        for e in self.ENG:
            if self.streams[e]:
                sect[e](mk(self.streams[e]))


def rope_tables(S, hd):
    rows_n = S // GRID_W
    row = np.repeat(np.arange(rows_n, dtype=np.float32), GRID_W)
    col = np.tile(np.arange(GRID_W, dtype=np.float32), rows_n)
    half = hd // 2
    inv_freq = (np.float32(ROPE_THETA) ** (-np.arange(0, half, 2, dtype=np.float32) / np.float32(half))).astype(np.float32)
    ang_r = row[:, None] * inv_freq
    ang_c = col[:, None] * inv_freq
    ang = np.concatenate([ang_r, ang_r, ang_c, ang_c], axis=-1).astype(np.float32)
    cos = np.cos(ang).astype(np.float32)
    sin = np.sin(ang).astype(np.float32)
    q4 = hd // 4
    sgn = np.concatenate([-np.ones(q4), np.ones(q4), -np.ones(q4), np.ones(q4)]).astype(np.float32)
    return cos, (sin * sgn[None, :]).astype(np.float32)


def build_program(S, layers):
    NT = S // 128
    n_a = sum(1 for l in layers if l == "a")
    n_r = sum(1 for l in layers if l == "r")
    nL = len(layers)
    nc = bass.Bass("TRN2", target_bir_lowering=False)
    sc = Sched()

    def din(name, shape, dt=F32):
        return nc.dram_tensor(name, list(shape), dt, kind="ExternalInput").ap()

    def dint(name, shape, dt):
        return nc.dram_tensor(name, list(shape), dt, kind="Internal").ap()

    x_in = din("x", [S, D])
    y_out = nc.dram_tensor("y", [S, D], F32, kind="ExternalOutput").ap()
    a_win = din("a_win", [max(n_a, 1), D, A_IN])
    a_wout = din("a_wout", [max(n_a, 1), D, D])
    a_qg = din("a_qg", [max(n_a, 1), 128])
    a_kg = din("a_kg", [max(n_a, 1), 128])
    r_win = din("r_win", [max(n_r, 1), D, R_IN])
    r_wout = din("r_wout", [max(n_r, 1), 2 * D, D])
    r_dec = din("r_dec", [max(n_r, 1), 16])
    ln_g = din("ln_g", [nL, D])
    ln_b = din("ln_b", [nL, D])
    cosa_d = din("cosa", [S, 128])
    sina_d = din("sina", [S, 128])
    cosr_d = din("cosr", [S, 256])
    sinr_d = din("sinr", [S, 256])
    identf_d = din("identf_d", [128, 128])
    p1_d = din("p1_d", [128, 128])
    p2_d = din("p2_d", [128, 128])
    posv_d = din("posv_d", [128, 4])
    prow_d = din("prow_d", [128, 256])

    a_win_b = dint("a_win_b", [max(n_a, 1), A_IN // 512, 128, 16, 512], BF16)
    a_wout_b = dint("a_wout_b", [max(n_a, 1), D // 512, 128, 16, 512], BF16)
    r_win_b = dint("r_win_b", [max(n_r, 1), R_IN // 512, 128, 16, 512], BF16)
    r_wout_b = dint("r_wout_b", [max(n_r, 1), D // 256, 128, 32, 256], BF16)
    xs = [dint("xs0", [S, D], F32), dint("xs1", [S, D], F32)]
    if n_r:
        r_qT = dint("r_qT", [NT, 128, 2 * 8 * 128], BF16)
        r_k = dint("r_k", [S, 2048], BF16)
        r_v = dint("r_v", [S, 4096], BF16)
        r_g = dint("r_g", [S, 4096], BF16)
        r_o1 = dint("r_o1", [S, 4096], F32)

    stack = ExitStack()
    DB = {}

    def dbuf(ap):
        k = ap.name if hasattr(ap, "name") else id(ap)
        if k not in DB:
            DB[k] = Buf(str(k))
        return DB[k]

    NA = 32768
    arena = stack.enter_context(nc.sbuf_tensor("arena", [128, NA], F32))
    aoff = [0]

    class T:
        def __init__(self, name, shape, dt, psum=False, persist=False, view=None, buf=None):
            if view is not None:
                self.t = view
                self.b = buf
                return
            shape = list(shape)
            if psum:
                self.t = stack.enter_context(nc.psum_tensor(name, shape, dt))
            elif persist:
                self.t = stack.enter_context(nc.sbuf_tensor(name, shape, dt))
            else:
                n = 1
                for d_ in shape[1:]:
                    n *= d_
                nw = n if dt == F32 else (n + 1) // 2
                nw = (nw + 7) // 8 * 8
                assert aoff[0] + nw <= NA, ("arena overflow", name, aoff[0], nw)
                v = arena[:, aoff[0]:aoff[0] + nw]
                aoff[0] += nw
                if dt != F32:
                    v = v.bitcast(dt)
                v = v[:, 0:n]
                if len(shape) == 3:
                    v = v.rearrange("p (a b) -> p a b", b=shape[2])
                elif len(shape) == 4:
                    v = v.rearrange("p (a b c) -> p a b c", b=shape[2], c=shape[3])
                self.t = v
            self.b = Buf(name)

        def __getitem__(self, k):
            return self.t[k]

    def new_phase():
        sc.fence()
        aoff[0] = 0

    def op(eng, fn, reads=(), writes=(), dma=False):
        rb = [x.b if isinstance(x, T) else x for x in reads]
        wb = [x.b if isinstance(x, T) else x for x in writes]
        return sc.op(eng, fn, rb, wb, dma)

    def dma(q, out, in_, reads, writes):
        op(q, lambda e: e.dma_start(out=out, in_=in_), reads, writes, dma=True)

    identf = T("identf", [128, 128], F32, persist=True)
    identb = T("identb", [128, 128], BF16, persist=True)
    onesb = T("onesb", [128, 128], BF16, persist=True)
    dma("sp", identf[:], identf_d[:, :], [], [identf])
    op("dve", lambda e: e.tensor_copy(out=identb[:], in_=identf[:]), [identf], [identb])
    op("dve", lambda e: e.memset(onesb[:], 1.0), [], [onesb])

    WB = {}
    cast_todo = {}

    def cast_w(li_, key, src, dst, rows, cols, cw):
        WB[key] = Buf(str(key))
        for kc in range(rows // 128):
            s_ = src[kc * 128:(kc + 1) * 128, :].rearrange("p (cb c) -> p cb c", c=cw)
            d_ = dst[:, :, kc, :].rearrange("cb p c -> p cb c")
            cast_todo.setdefault(li_, []).append(lambda d_=d_, s_=s_, key=key: dma("pool", d_, s_, [], [WB[key]]))

    def cast_some(li_, n):
        lst = cast_todo.get(li_, [])
        while lst and n > 0:
            lst.pop(0)()
            n -= 1

    psA2 = stack.enter_context(nc.psum_tensor("psA2", [128, 1024], F32))
    psB2 = stack.enter_context(nc.psum_tensor("psB2", [128, 1024], F32))
    psA = [T(None, None, None, view=psA2[:, i * 512:(i + 1) * 512], buf=Buf("psA%d" % i)) for i in range(2)]
    psB = [T(None, None, None, view=psB2[:, i * 512:(i + 1) * 512], buf=Buf("psB%d" % i)) for i in range(2)]
    psW = [(psA2, psA), (psB2, psB)]
    psT = [T("psT0", [128, 1024], BF16, True), T("psT1", [128, 1024], BF16, True)]
    psTf = [T(None, None, None, view=psT[i].t[:].bitcast(F32), buf=psT[i].b) for i in range(2)]
    psO = T("psO", [128, 512], F32, True)
    psL = T("psL", [128, 512], F32, True)

    wbuf = [T("wbuf0", [128, 16, 512], BF16, persist=True), T("wbuf1", [128, 16, 512], BF16, persist=True)]
    wctr = [0]
    xstg = [T("xstg0", [128, 512], F32, persist=True), T("xstg1", [128, 512], F32, persist=True)]
    xctr = [0]
    t1_g = T("t1", [128, 512], F32, persist=True)
    t2_g = T("t2", [128, 512], F32, persist=True)
    lng = [T("lng0", [128, 512], F32, persist=True), t1_g]
    lnb = [T("lnb0", [128, 512], F32, persist=True), t2_g]
    small = {"t1": t1_g, "t2": t2_g}

    def sm(name, shape, dt=F32):
        if name not in small:
            small[name] = T(name, shape, dt, persist=True)
        return small[name]

    evac_ctr = [0]
    cur_layer = [0]

    def trickle():
        cast_some(cur_layer[0] + 1, 2)

    def evac(out, in_, reads, writes):
        evac_ctr[0] += 1
        if evac_ctr[0] % 2:
            op("act", lambda e: e.activation(out=out, in_=in_, func=AF.Copy), reads, writes)
        else:
            op("dve", lambda e: e.tensor_copy(out=out, in_=in_), reads, writes)

    def load_w(src2d, c0, ncol, nk, wb=None):
        w = wbuf[wctr[0] % 2]
        wctr[0] += 1
        assert nk * ncol <= 16 * 512
        view = w.t[:].rearrange("p a b -> p (a b)")[:, 0:nk * ncol].rearrange("p (a b) -> p a b", b=ncol)
        dma("sp", view, src2d[c0 // ncol], [wb], [w])

        return T(None, None, None, view=view, buf=w.b)

    def load_xT(xsrc, t0, ntile, xT):
        for j in range(ntile):
            tok0 = (t0 + j) * 128
            for c4 in range(4):
                st = xstg[xctr[0] % 2]
                xctr[0] += 1
                dma("sp", st[:], xsrc[tok0:tok0 + 128, c4 * 512:(c4 + 1) * 512], [dbuf(xsrc)], [st])
                ps = psB[c4 % 2]

                def f(e, ps=ps, st=st):
                    for i in range(4):
                        ins = e.transpose(out=ps[:, i * 128:(i + 1) * 128], in_=st[:, i * 128:(i + 1) * 128], identity=identf[:])
                    return ins
                op("pe", f, [st, identf], [ps])
                evac(xT[:, c4 * 4:(c4 + 1) * 4, j * 128:(j + 1) * 128],
                     ps[:].rearrange("p (a b) -> p a b", b=128), [ps], [xT])

    def proj_tok(ps, xT, j, w, nk=16, ncol=512):
        def f(e):
            for kc in range(nk):
                ins = e.matmul(ps[:, 0:ncol], lhsT=xT[:, kc, j * 128:(j + 1) * 128], rhs=w[:, kc, 0:ncol],
                               start=(kc == 0), stop=(kc == nk - 1))
            return ins
        op("pe", f, [xT, w], [ps])

    def layer_norm_out(li, rbuf, j, tok0, xdst):
        stats = sm("ln_stats", [128, 4, 6])
        mv = sm("ln_mv", [128, 2])
        rstd = sm("ln_rstd", [128, 1])
        nmr = sm("ln_nmr", [128, 1])
        for c in range(4):
            op("dve", lambda e, c=c: e.bn_stats(out=stats[:, c, :], in_=rbuf[:, j, c * 512:(c + 1) * 512]), [rbuf], [stats])
        op("dve", lambda e: e.bn_aggr(out=mv[:], in_=stats[:].rearrange("p a b -> p (a b)")), [stats], [mv])
        op("dve", lambda e: e.tensor_scalar(out=rstd[:], in0=mv[:, 1:2], scalar1=LN_EPS, scalar2=None, op0=ALU.add), [mv], [rstd])
        op("act", lambda e: e.activation(out=rstd[:], in_=rstd[:], func=AF.Sqrt), [rstd], [rstd])
        op("dve", lambda e: e.reciprocal(out=rstd[:], in_=rstd[:]), [rstd], [rstd])
        op("dve", lambda e: e.scalar_tensor_tensor(out=nmr[:], in0=mv[:, 0:1], scalar=-1.0, in1=rstd[:], op0=ALU.mult, op1=ALU.mult),
           [mv, rstd], [nmr])
        op("act", lambda e: e.activation(out=rbuf[:, j, :], in_=rbuf[:, j, :], func=AF.Identity, bias=nmr[:, 0:1], scale=rstd[:, 0:1]),
           [rbuf, nmr, rstd], [rbuf])
        for c in range(4):
            g_ = lng[c % 2]
            b_ = lnb[c % 2]
            dma("pool", g_[:], ln_g[li:li + 1, c * 512:(c + 1) * 512].partition_broadcast(128), [], [g_])
            dma("pool", b_[:], ln_b[li:li + 1, c * 512:(c + 1) * 512].partition_broadcast(128), [], [b_])
            op("pool", lambda e, c=c, g_=g_: e.tensor_tensor(out=rbuf[:, j, c * 512:(c + 1) * 512], in0=rbuf[:, j, c * 512:(c + 1) * 512],
                                                             in1=g_[:], op=ALU.mult), [rbuf, g_], [rbuf])
            op("dve", lambda e, c=c, b_=b_: e.tensor_tensor(out=rbuf[:, j, c * 512:(c + 1) * 512], in0=rbuf[:, j, c * 512:(c + 1) * 512],
                                                            in1=b_[:], op=ALU.add), [rbuf, b_], [rbuf])
        dma("pool", xdst[tok0:tok0 + 128, :], rbuf[:, j, :], [rbuf], [dbuf(xdst)])

    def attention_layer(li, ja, xsrc, xdst):
        new_phase()
        SBT = 2
        NSB = NT // SBT
        win = a_win_b[ja]
        wout = a_wout_b[ja]
        KT = T("KT", [128, 4, S], BF16)
        V = T("V", [128, NT, 512], BF16)
        xTs = [T("xT0", [128, 16, SBT * 128], BF16), T("xT1", [128, 16, SBT * 128], BF16)]
        qT = T("qT", [128, 16, SBT * 128], BF16)
        gT = T("gT", [128, 16, SBT * 128], BF16)
        rbuf = T("rbuf", [128, SBT, D], F32)
        PT = [T("PT%d" % i, [128, 1024], BF16) for i in range(3)]
        cosb = [sm("cosb0", [128, 128]), sm("cosb1", [128, 128])]
        sinb = [sm("sinb0", [128, 128]), sm("sinb1", [128, 128])]
        gq = sm("gq_rep", [128, 128])
        gk = sm("gk_rep", [128, 128])
        negb = sm("negb", [128, 1])
        mq = sm("mq", [128, 1])
        mk_ = sm("mk", [128, 1])
        sq = T("sq", [128, 512], F32)
        ss = sm("ss", [128, 4])
        rs = sm("rs", [128, 4])
        qg = T("qg", [128, 512], F32)
        t1 = sm("t1", [128, 512])
        t2 = sm("t2", [128, 512])
        qbs = [sm("qb0", [128, 512], BF16), sm("qb1", [128, 512], BF16)]
        pend = []
        rl = T("rl", [128, 512], F32)
        tmpo = T("tmpo", [128, 512], F32)

        dma("sp", gq[:], a_qg[ja:ja + 1, :].partition_broadcast(128), [], [gq])
        dma("sp", gk[:], a_kg[ja:ja + 1, :].partition_broadcast(128), [], [gk])
        op("dve", lambda e: e.tensor_reduce(out=mq[:], in_=gq[:], axis=AX.X, op=ALU.max, apply_absolute_value=True), [gq], [mq])
        op("dve", lambda e: e.tensor_reduce(out=mk_[:], in_=gk[:], axis=AX.X, op=ALU.max, apply_absolute_value=True), [gk], [mk_])
        op("dve", lambda e: e.scalar_tensor_tensor(out=negb[:], in0=mq[:], scalar=-math.sqrt(128.0), in1=mk_[:], op0=ALU.mult, op1=ALU.mult),
           [mq, mk_], [negb])

        def qk_post(ps, gain, scale, tok0, cs, qb):
            cb_, sb_ = cosb[cs % 2], sinb[cs % 2]
            op("act", lambda e: e.activation(out=sq[:], in_=ps[:], func=AF.Square), [ps], [sq])
            op("dve", lambda e: e.tensor_reduce(out=ss[:], in_=sq[:].rearrange("p (h d) -> p h d", d=128), axis=AX.X, op=ALU.add), [sq], [ss])
            op("dve", lambda e: e.tensor_scalar(out=rs[:], in0=ss[:], scalar1=1.0 / (128.0 * scale * scale), scalar2=RMS_EPS / (scale * scale),
                                                op0=ALU.mult, op1=ALU.add), [ss], [rs])
            op("act", lambda e: e.activation(out=rs[:], in_=rs[:], func=AF.Sqrt), [rs], [rs])
            op("dve", lambda e: e.reciprocal(out=rs[:], in_=rs[:]), [rs], [rs])
            op("dve", lambda e: e.tensor_tensor(out=qg[:].rearrange("p (h d) -> p h d", d=128), in0=ps[:].rearrange("p (h d) -> p h d", d=128),
                                                in1=gain[:].unsqueeze(1).to_broadcast([128, 4, 128]), op=ALU.mult), [ps, gain], [qg])
            op("dve", lambda e: e.tensor_tensor(out=t1[:].rearrange("p (h d) -> p h d", d=128), in0=qg[:].rearrange("p (h d) -> p h d", d=128),
                                                in1=cb_[:].unsqueeze(1).to_broadcast([128, 4, 128]), op=ALU.mult), [qg, cb_], [t1])
            qv = qg[:].rearrange("p (g s d) -> p g s d", s=2, d=32)
            tv = t2[:].rearrange("p (g s d) -> p g s d", s=2, d=32)
            sv = sb_[:].rearrange("p (g s d) -> p g s d", s=2, d=32)
            qv5 = qg[:].rearrange("p (h g s d) -> p h g s d", h=4, g=2, s=2, d=32)
            tv5 = t2[:].rearrange("p (h g s d) -> p h g s d", h=4, g=2, s=2, d=32)
            for s_ in range(2):
                op("dve", lambda e, s_=s_: e.tensor_tensor(out=tv5[:, :, :, s_, :], in0=qv5[:, :, :, 1 - s_, :],
                                                            in1=sv[:, :, s_, :].unsqueeze(1).to_broadcast([128, 4, 2, 32]), op=ALU.mult), [qg, sb_], [t2])
            op("dve", lambda e: e.tensor_tensor(out=t1[:], in0=t1[:], in1=t2[:], op=ALU.add), [t1, t2], [t1])
            op("dve", lambda e: e.tensor_tensor(out=qb[:].rearrange("p (h d) -> p h d", d=128), in0=t1[:].rearrange("p (h d) -> p h d", d=128),
                                                in1=rs[:].unsqueeze(2).to_broadcast([128, 4, 128]), op=ALU.mult), [t1, rs], [qb])

        def load_tables(tok0, cs):
            dma("sp", cosb[cs % 2][:], cosa_d[tok0:tok0 + 128, :], [], [cosb[cs % 2]])
            dma("sp", sinb[cs % 2][:], sina_d[tok0:tok0 + 128, :], [], [sinb[cs % 2]])

        def transpose4(qb, pst):
            def f(e):
                for h in range(4):
                    ins = e.transpose(out=pst[:, h * 128:(h + 1) * 128], in_=qb[:, h * 128:(h + 1) * 128], identity=identb[:])
                return ins
            op("pe", f, [qb, identb], [pst])

        cs = 0
        load_xT(xsrc, 0, SBT, xTs[0])
        for sb in range(NSB):
            trickle()
            xT = xTs[sb % 2]
            wk = load_w(win, 2048, 512, 16, WB[('ai', ja)])
            wv = load_w(win, 2560, 512, 16, WB[('ai', ja)])
            for j in range(SBT):
                t = sb * SBT + j
                tok0 = t * 128
                load_tables(tok0, cs)
                ps = psA[t % 2]
                proj_tok(ps, xT, j, wk)
                qb = qbs[cs % 2]
                qk_post(ps, gk, 1.0, tok0, cs, qb)
                cs += 1
                ps = (psO, psL)[t % 2]
                proj_tok(ps, xT, j, wv)
                evac(V[:, t, :], ps[:], [ps], [V])
                while pend:
                    pend.pop(0)()

                def fin(qb=qb, t=t, tok0=tok0):
                    pst = psT[t % 2]
                    transpose4(qb, pst)
                    evac(KT[:, :, tok0:tok0 + 128], pst[:, 0:512].rearrange("p (h t) -> p h t", t=128), [pst], [KT])
                pend.append(fin)
            load_xT(xsrc, ((sb + 1) % NSB) * SBT, SBT, xTs[(sb + 1) % 2])
        while pend:
            pend.pop(0)()

        pend_ln = []
        for sb in range(NSB):
            trickle()
            xT = xTs[(NSB + sb) % 2]
            for cb in range(4):
                w = load_w(win, cb * 512, 512, 16, WB[('ai', ja)])
                for j in range(SBT):
                    tok0 = (sb * SBT + j) * 128
                    load_tables(tok0, cs)
                    ps = (psA[0], psA[1], psB[0], psB[1])[cs % 4]
                    proj_tok(ps, xT, j, w)
                    qb = qbs[cs % 2]
                    qk_post(ps, gq, 128.0 ** -0.5, tok0, cs, qb)
                    while pend:
                        pend.pop(0)()

                    def fin(qb=qb, cs_=cs, cb=cb, j=j):
                        pst = psT[cs_ % 2]
                        transpose4(qb, pst)
                        evac(qT[:, cb * 4:(cb + 1) * 4, j * 128:(j + 1) * 128], pst[:, 0:512].rearrange("p (h t) -> p h t", t=128), [pst], [qT])
                    pend.append(fin)
                    cs += 1
            for cb in range(4):
                w = load_w(win, 3072 + cb * 512, 512, 16, WB[('ai', ja)])
                for h in range(4):
                    ps = psA[h % 2]

                    def f(e, ps=ps, w=w, h=h, xT=xT):
                        for kc in range(16):
                            ins = e.matmul(ps[:, 0:SBT * 128], lhsT=w[:, kc, h * 128:(h + 1) * 128], rhs=xT[:, kc, :],
                                           start=(kc == 0), stop=(kc == 15))
                        return ins
                    op("pe", f, [xT, w], [ps])
                    op("act", lambda e, ps=ps, cb=cb, h=h: e.activation(out=gT[:, cb * 4 + h, :], in_=ps[:, 0:SBT * 128], func=AF.Silu), [ps], [gT])
            while pend:
                pend.pop(0)()
            if sb + 1 < NSB:
                load_xT(xsrc, (sb + 1) * SBT, SBT, xTs[(NSB + sb + 1) % 2])
            while pend_ln:
                pend_ln.pop(0)()
            gctr = [0]
            for j in range(SBT):
                for g in range(4):
                    rhs_q = qT[:, g * 4:(g + 1) * 4, j * 128:(j + 1) * 128]

                    def s_mm(i, g=g, rhs_q=rhs_q):
                        ps2, halves = psW[i % 2]

                        def f(e):
                            for u_ in range(2):
                                kt = 2 * i + u_
                                ins = e.matmul(halves[u_][:], lhsT=KT[:, g, kt * 128:(kt + 1) * 128], rhs=rhs_q, start=True, stop=True)
                            return ins
                        op("pe", f, [KT, qT], [halves[0], halves[1]])
                        pt = PT[i % 3]
                        op("act", lambda e: e.activation(out=pt[:], in_=ps2[:, :], func=AF.Exp, bias=negb[:, 0:1], scale=1.0), [halves[0], halves[1], negb], [pt])

                    gctr[0] += 1
                    pO, pL = (psO, psL) if gctr[0] % 2 else (psTf[0], psTf[1])

                    def pv_mm(i, g=g, pO=pO, pL=pL):
                        pt = PT[i % 3]

                        def f(e):
                            for u_ in range(2):
                                kt = 2 * i + u_
                                e.matmul(pO[:], lhsT=V[:, kt, g * 128:(g + 1) * 128], rhs=pt[:, u_ * 512:(u_ + 1) * 512], start=(kt == 0), stop=(kt == NT - 1))
                                ins = e.matmul(pL[:], lhsT=onesb[:], rhs=pt[:, u_ * 512:(u_ + 1) * 512], start=(kt == 0), stop=(kt == NT - 1))
                            return ins
                        op("pe", f, [V, pt, onesb], [pO, pL])

                    NP = NT // 2
                    for i in range(NP):
                        s_mm(i)
                        if i >= 1:
                            pv_mm(i - 1)
                    pv_mm(NP - 1)
                    op("dve", lambda e, pL=pL: e.reciprocal(out=rl[:], in_=pL[:]), [pL], [rl])
                    op("dve", lambda e, pO=pO: e.tensor_tensor(out=tmpo[:], in0=pO[:], in1=rl[:], op=ALU.mult), [pO, rl], [tmpo])
                    gview = gT[:, g * 4:(g + 1) * 4, j * 128:(j + 1) * 128]
                    op("dve", lambda e, gview=gview: e.tensor_tensor(out=gview, in0=gview, in1=tmpo[:].rearrange("p (h t) -> p h t", t=128), op=ALU.mult),
                       [gT, tmpo], [gT])
            for cb in range(4):
                w = load_w(wout, cb * 512, 512, 16, WB[('ao', ja)])
                for j in range(SBT):
                    tok0 = (sb * SBT + j) * 128
                    ps = psA[j % 2]

                    def f(e, ps=ps, w=w, j=j):
                        for h in range(16):
                            ins = e.matmul(ps[:], lhsT=gT[:, h, j * 128:(j + 1) * 128], rhs=w[:, h, :], start=(h == 0), stop=(h == 15))
                        return ins
                    op("pe", f, [gT, w], [ps])
                    st = xstg[xctr[0] % 2]
                    xctr[0] += 1
                    dma("sp", st[:], xsrc[tok0:tok0 + 128, cb * 512:(cb + 1) * 512], [dbuf(xsrc)], [st])
                    op("dve", lambda e, ps=ps, st=st, j=j, cb=cb: e.scalar_tensor_tensor(out=rbuf[:, j, cb * 512:(cb + 1) * 512], in0=st[:], scalar=ALPHA,
                                                                                          in1=ps[:], op0=ALU.mult, op1=ALU.add), [st, ps], [rbuf])
            for j in range(SBT):
                pend_ln.append(lambda j=j, sb=sb: layer_norm_out(li, rbuf, j, (sb * SBT + j) * 128, xdst))
        while pend_ln:
            pend_ln.pop(0)()

    def retention_layer(li, jr, xsrc, xdst):
        SBT = 2
        NSB = NT // SBT
        win = r_win_b[jr]
        wout = r_wout_b[jr]
        LN2 = math.log(2.0)

        posv = sm("posv", [128, 4])
        p1 = sm("p1", [128, 128])
        p2 = sm("p2", [128, 128])
        dec = sm("dec_rep", [128, 16])
        u = sm("dec_u", [128, 16])
        tt = sm("dec_t", [128, 16])
        lg = sm("dec_lg", [128, 16])
        dqf = sm("dqf", [128, 8])
        dqb = sm("dqb", [128, 8])
        kdf = sm("kdf", [128, 8])
        kdb = sm("kdb", [128, 8])
        gc = sm("gc", [128, 16])
        arg = sm("dc_arg", [128, 128])
        dma("sp", posv[:], posv_d[:, :], [], [posv])
        dma("sp", p1[:], p1_d[:, :], [], [p1])
        dma("sp", p2[:], p2_d[:, :], [], [p2])
        dma("sp", dec[:], r_dec[jr:jr + 1, :].partition_broadcast(128), [], [dec])
        op("act", lambda e: e.activation(out=u[:], in_=dec[:], func=AF.Exp, scale=-LN2), [dec], [u])
        op("dve", lambda e: e.tensor_scalar(out=tt[:], in0=u[:], scalar1=0.25, scalar2=1.0 / 3.0, op0=ALU.mult, op1=ALU.add), [u], [tt])
        op("dve", lambda e: e.tensor_tensor(out=tt[:], in0=tt[:], in1=u[:], op=ALU.mult), [tt, u], [tt])
        op("dve", lambda e: e.tensor_scalar(out=tt[:], in0=tt[:], scalar1=0.5, scalar2=None, op0=ALU.add), [tt], [tt])
        op("dve", lambda e: e.tensor_tensor(out=tt[:], in0=tt[:], in1=u[:], op=ALU.mult), [tt, u], [tt])
        op("dve", lambda e: e.tensor_scalar(out=tt[:], in0=tt[:], scalar1=1.0, scalar2=None, op0=ALU.add), [tt], [tt])
        op("dve", lambda e: e.scalar_tensor_tensor(out=lg[:], in0=tt[:], scalar=-1.0, in1=u[:], op0=ALU.mult, op1=ALU.mult), [tt, u], [lg])
        op("act", lambda e: e.activation(out=dqf[:], in_=lg[:, 0:8], func=AF.Exp, scale=posv[:, 0:1]), [lg, posv], [dqf])
        op("act", lambda e: e.activation(out=dqb[:], in_=lg[:, 8:16], func=AF.Exp, scale=posv[:, 1:2]), [lg, posv], [dqb])
        op("act", lambda e: e.activation(out=kdf[:], in_=lg[:, 0:8], func=AF.Exp, scale=posv[:, 2:3]), [lg, posv], [kdf])
        op("act", lambda e: e.activation(out=kdb[:], in_=lg[:, 8:16], func=AF.Exp, scale=posv[:, 3:4]), [lg, posv], [kdb])
        op("act", lambda e: e.activation(out=gc[:], in_=lg[:], func=AF.Exp, scale=128.0), [lg], [gc])

        cosr = [sm("cosr0", [128, 256]), sm("cosr1", [128, 256])]
        sinr = [sm("sinr0", [128, 256]), sm("sinr1", [128, 256])]
        t1 = sm("t1", [128, 512])
        t2 = sm("t2", [128, 512])
        cs = [0]

        def rope256(ps, out_ap, outT, scale, tok0):
            c_ = cosr[cs[0] % 2]
            s_ = sinr[cs[0] % 2]
            cs[0] += 1
            dma("sp", c_[:], cosr_d[tok0:tok0 + 128, :], [], [c_])
            dma("sp", s_[:], sinr_d[tok0:tok0 + 128, :], [], [s_])
            op("dve", lambda e: e.tensor_tensor(out=t1[:].rearrange("p (h d) -> p h d", d=256), in0=ps[:].rearrange("p (h d) -> p h d", d=256),
                                                in1=c_[:].unsqueeze(1).to_broadcast([128, 2, 256]), op=ALU.mult), [ps, c_], [t1])
            pv5 = ps[:].rearrange("p (h g s d) -> p h g s d", h=2, g=2, s=2, d=64)
            tv5 = t2[:].rearrange("p (h g s d) -> p h g s d", h=2, g=2, s=2, d=64)
            sv = s_[:].rearrange("p (g s d) -> p g s d", s=2, d=64)
            for k_ in range(2):
                op("dve", lambda e, k_=k_: e.tensor_tensor(out=tv5[:, :, :, k_, :], in0=pv5[:, :, :, 1 - k_, :],
                                                           in1=sv[:, :, k_, :].unsqueeze(1).to_broadcast([128, 2, 2, 64]), op=ALU.mult), [ps, s_], [t2])
            if scale == 1.0:
                op("dve", lambda e: e.tensor_tensor(out=out_ap, in0=t1[:], in1=t2[:], op=ALU.add), [t1, t2], [outT])
            else:
                op("dve", lambda e: e.tensor_tensor(out=t1[:], in0=t1[:], in1=t2[:], op=ALU.add), [t1, t2], [t1])
                op("act", lambda e: e.activation(out=out_ap, in_=t1[:], func=AF.Copy, scale=scale), [t1], [outT])

        new_phase()
        prow = sm("prow", [128, 256])
        dma("sp", prow[:], prow_d[:, :], [], [prow])
        St32 = [T("St32_%d" % h, [128, 2, 512], F32) for h in range(8)]
        Stb = [T("Stb_%d" % h, [128, 2, 512], BF16) for h in range(8)]
        DcT = [T("DcT_%d" % h, [128, 128], F32) for h in range(8)]
        DqT = T("DqfT", [128, 8, 128], F32)
        xTs = [T("xT0", [128, 16, SBT * 128], BF16), T("xT1", [128, 16, SBT * 128], BF16)]
        qTc = [T("qTc%d" % j, [128, 2, 8, 128], BF16) for j in range(SBT)]
        kTc = [T("kTc%d" % j, [128, 2, 8, 128], BF16) for j in range(SBT)]
        kc = [T("kc%d" % j, [128, 2048], BF16) for j in range(SBT)]
        vall = [T("vall%d" % j, [128, 4096], BF16) for j in range(SBT)]
        AT = [T("AT%d" % i, [128, 128], BF16) for i in range(2)]
        gh = [T("gh%d" % i, [128, 512], BF16) for i in range(2)]
        o1buf = [T("o1buf%d" % i, [128, 512], F32) for i in range(2)]
        arg2 = sm("dc_arg2", [128, 128])

        for h in range(8):
            op("dve", lambda e, h=h: e.tensor_scalar(out=arg[:], in0=p1[:], scalar1=lg[:, h:h + 1], scalar2=None, op0=ALU.mult), [p1, lg], [arg])
            op("dve", lambda e, h=h: e.scalar_tensor_tensor(out=arg[:], in0=p2[:], scalar=lg[:, 8 + h:9 + h], in1=arg[:], op0=ALU.mult, op1=ALU.add),
               [p2, lg, arg], [arg])
            op("dve", lambda e, h=h: e.tensor_scalar(out=arg2[:], in0=prow[:, 0:128], scalar1=lg[:, h:h + 1], scalar2=None, op0=ALU.mult), [prow, lg], [arg2])
            op("dve", lambda e: e.tensor_tensor(out=arg[:], in0=arg[:], in1=arg2[:], op=ALU.subtract), [arg, arg2], [arg])
            op("act", lambda e, h=h: e.activation(out=DcT[h][:], in_=arg[:], func=AF.Exp), [arg], [DcT[h]])
            op("act", lambda e, h=h: e.activation(out=DqT[:, h, :], in_=prow[:, 0:128], func=AF.Exp, scale=lg[:, h:h + 1]), [prow, lg], [DqT])
            op("pool", lambda e, h=h: e.memset(St32[h][:], 0.0), [], [St32[h]])
            op("pool", lambda e, h=h: e.memset(Stb[h][:], 0.0), [], [Stb[h]])

        def state_update(St32, Stb, psU, h, kall, vht, gcol):
            S32h = St32[h]
            Sbh = Stb[h]
            for dkc in range(2):
                pu = psU[dkc]
                op("pe", lambda e, dkc=dkc, pu=pu: e.matmul(pu[:], lhsT=kall[:, h * 256 + dkc * 128:h * 256 + (dkc + 1) * 128], rhs=vht, start=True, stop=True),
                   [kall, vht_T[0]], [pu])
                op("dve", lambda e, dkc=dkc, pu=pu: e.scalar_tensor_tensor(out=S32h[:, dkc, :], in0=S32h[:, dkc, :], scalar=gc[:, gcol:gcol + 1],
                                                                          in1=pu[:], op0=ALU.mult, op1=ALU.add), [S32h, gc, pu], [S32h])
            op("act", lambda e: e.activation(out=Sbh[:], in_=S32h[:], func=AF.Copy), [S32h], [Sbh])

        vht_T = [None]
        psX = [psO, psL]
        pend = []
        bctr = [0]
        sctr = [0]
        load_xT(xsrc, 0, SBT, xTs[0])
        for sb in range(NSB):
            trickle()
            xT = xTs[sb % 2]
            for kind in range(2):
                for cb in range(4):
                    w = load_w(win, kind * 2048 + cb * 512, 512, 16, WB[('ri', jr)])
                    for j in range(SBT):
                        tok0 = (sb * SBT + j) * 128
                        bctr[0] += 1
                        bc = bctr[0]
                        ps = (psA[0], psA[1], psB[0], psB[1])[bc % 4]
                        proj_tok(ps, xT, j, w)
                        if kind == 0:
                            src_t = gh[bc % 2]
                            src_ap = src_t[:]
                            rope256(ps, src_ap, src_t, 1.0, tok0)
                            dstT = qTc[j]
                        else:
                            src_t, src_ap = kc[j], kc[j][:, cb * 512:(cb + 1) * 512]
                            rope256(ps, src_ap, kc[j], 1.0 / 16.0, tok0)
                            dstT = kTc[j]
                        while pend:
                            pend.pop(0)()

                        def fin(src_t=src_t, src_ap=src_ap, dstT=dstT, bc=bc, cb=cb):
                            pst = psT[bc % 2]

                            def f(e):
                                for dkc in range(2):
                                    for hl in range(2):
                                        i = dkc * 2 + hl
                                        ins = e.transpose(out=pst[:, i * 128:(i + 1) * 128], in_=src_ap[:, hl * 256 + dkc * 128:hl * 256 + (dkc + 1) * 128],
                                                          identity=identb[:])
                                return ins
                            op("pe", f, [src_t, identb], [pst])
                            evac(dstT[:, :, 2 * cb:2 * cb + 2, :], pst[:, 0:512].rearrange("p (a b t) -> p a b t", a=2, b=2), [pst], [dstT])
                        pend.append(fin)
            while pend:
                pend.pop(0)()
            for j in range(SBT):
                c = sb * SBT + j
                dma("pool", r_qT[c], qTc[j][:].rearrange("p a b t -> p (a b t)"), [qTc[j]], [dbuf(r_qT)])
                dma("pool", r_k[c * 128:(c + 1) * 128, :], kc[j][:], [kc[j]], [dbuf(r_k)])
                op("dve", lambda e, j=j: e.tensor_tensor(out=qTc[j][:], in0=qTc[j][:], in1=DqT[:].unsqueeze(1).to_broadcast([128, 2, 8, 128]), op=ALU.mult),
                   [qTc[j], DqT], [qTc[j]])
                op("dve", lambda e, j=j: e.tensor_tensor(out=kc[j][:].rearrange("p (h d) -> p h d", d=256), in0=kc[j][:].rearrange("p (h d) -> p h d", d=256),
                                                         in1=kdf[:].unsqueeze(2).to_broadcast([128, 8, 256]), op=ALU.mult), [kc[j], kdf], [kc[j]])
            for h in range(8):
                w = load_w(win, 4096 + h * 512, 512, 16, WB[('ri', jr)])
                for j in range(SBT):
                    ps = psA[j % 2]
                    proj_tok(ps, xT, j, w)
                    evac(vall[j][:, h * 512:(h + 1) * 512], ps[:], [ps], [vall[j]])
            for j in range(SBT):
                tok0 = (sb * SBT + j) * 128
                dma("pool", r_v[tok0:tok0 + 128, :], vall[j][:], [vall[j]], [dbuf(r_v)])
            steps = [(j, h) for j in range(SBT) for h in range(8)]

            def step_S(j, h, k_):
                psS = psB[k_ % 2]

                def f(e):
                    for dkc in range(2):
                        ins = e.matmul(psS[:, 0:128], lhsT=kTc[j][:, dkc, h, :], rhs=qTc[j][:, dkc, h, :], start=(dkc == 0), stop=(dkc == 1))
                    return ins
                op("pe", f, [kTc[j], qTc[j]], [psS])
                at = AT[k_ % 2]
                op("dve", lambda e: e.tensor_tensor(out=at[:], in0=psS[:, 0:128], in1=DcT[h][:], op=ALU.mult), [psS, DcT[h]], [at])

            def step_rest(j, h, k_):
                tok0 = (sb * SBT + j) * 128
                vht = vall[j][:, h * 512:(h + 1) * 512]
                vht_T[0] = vall[j]
                at = AT[k_ % 2]
                acc = psX[k_ % 2]
                sb_ = Stb[h]

                def f2(e):
                    e.matmul(acc[:], lhsT=at[:], rhs=vht, start=True, stop=False)
                    e.matmul(acc[:], lhsT=qTc[j][:, 0, h, :], rhs=sb_[:, 0, :], start=False, stop=False)
                    return e.matmul(acc[:], lhsT=qTc[j][:, 1, h, :], rhs=sb_[:, 1, :], start=False, stop=True)
                op("pe", f2, [at, vall[j], qTc[j], Stb[h]], [acc])
                ob = o1buf[k_ % 2]
                evac(ob[:], acc[:], [acc], [ob])
                dma("pool", r_o1[tok0:tok0 + 128, h * 512:(h + 1) * 512], ob[:], [ob], [dbuf(r_o1)])
                psU = psA if k_ % 2 else psTf
                state_update(St32, Stb, psU, h, kc[j], vht, h)

            k0 = sctr[0]
            step_S(steps[0][0], steps[0][1], k0)
            for i_, (j, h) in enumerate(steps):
                if i_ + 1 < len(steps):
                    step_S(steps[i_ + 1][0], steps[i_ + 1][1], k0 + i_ + 1)
                step_rest(j, h, k0 + i_)
            sctr[0] += len(steps)
            if sb + 1 < NSB:
                load_xT(xsrc, (sb + 1) * SBT, SBT, xTs[(sb + 1) % 2])
            for h in range(8):
                w = load_w(win, 8192 + h * 512, 512, 16, WB[('ri', jr)])
                for j in range(SBT):
                    tok0 = (sb * SBT + j) * 128
                    ps = psA[j % 2]
                    proj_tok(ps, xT, j, w)
                    g_ = gh[j % 2]
                    op("act", lambda e, g_=g_, ps=ps: e.activation(out=g_[:], in_=ps[:], func=AF.Silu), [ps], [g_])
                    dma("pool", r_g[tok0:tok0 + 128, h * 512:(h + 1) * 512], g_[:], [g_], [dbuf(r_g)])

        new_phase()
        St32b = [T("St32_%d" % h, [128, 2, 512], F32) for h in range(8)]
        Stbb = [T("Stb_%d" % h, [128, 2, 512], BF16) for h in range(8)]
        DqbT = T("DqbT", [128, 8, 128], F32)
        qT1s = [T("qT1_%d" % i, [128, 2, 8, 128], BF16) for i in range(2)]
        kc1s = [T("kc1_%d" % i, [128, 2048], BF16) for i in range(2)]
        vh2 = [T("vh%d" % i, [128, 512], BF16) for i in range(2)]
        gh2 = [T("gh%d" % i, [128, 512], BF16) for i in range(3)]
        og = T("og", [128, 4096], BF16)
        ogT = T("ogT", [128, 32, SBT * 128], BF16)
        o1b = [T("o1b%d" % i, [128, 512], F32) for i in range(2)]
        ot = [T("ot%d" % i, [128, 512], F32) for i in range(2)]
        rbuf = T("rbuf", [128, SBT, D], F32)
        gst = [sm("gn_stats%d" % i, [128, 6]) for i in range(2)]
        gmv = [sm("gn_mv%d" % i, [128, 2]) for i in range(2)]
        grs = [sm("gn_rstd%d" % i, [128, 1]) for i in range(2)]
        gnm = [sm("gn_nmr%d" % i, [128, 1]) for i in range(2)]
        for h in range(8):
            op("act", lambda e, h=h: e.activation(out=DqbT[:, h, :], in_=prow[:, 128:256], func=AF.Exp, scale=lg[:, 8 + h:9 + h]), [prow, lg], [DqbT])
            op("pool", lambda e, h=h: e.memset(St32b[h][:], 0.0), [], [St32b[h]])
            op("pool", lambda e, h=h: e.memset(Stbb[h][:], 0.0), [], [Stbb[h]])

        chunks = [sb * SBT + j for sb in reversed(range(NSB)) for j in reversed(range(SBT))]

        def prefetch_chunk(ci):
            c = chunks[ci]
            q_, k_ = qT1s[ci % 2], kc1s[ci % 2]
            dma("sp", q_[:].rearrange("p a b t -> p (a b t)"), r_qT[c], [dbuf(r_qT)], [q_])
            dma("sp", k_[:], r_k[c * 128:(c + 1) * 128, :], [dbuf(r_k)], [k_])
            op("dve", lambda e: e.tensor_tensor(out=q_[:], in0=q_[:], in1=DqbT[:].unsqueeze(1).to_broadcast([128, 2, 8, 128]), op=ALU.mult),
               [q_, DqbT], [q_])
            op("dve", lambda e: e.tensor_tensor(out=k_[:].rearrange("p (h d) -> p h d", d=256), in0=k_[:].rearrange("p (h d) -> p h d", d=256),
                                                in1=kdb[:].unsqueeze(2).to_broadcast([128, 8, 256]), op=ALU.mult), [k_, kdb], [k_])

        def gn_part2(h, t_, g_, grs_, gmv_, gnm_):
            op("dve", lambda e: e.reciprocal(out=grs_[:], in_=grs_[:]), [grs_], [grs_])
            op("dve", lambda e: e.scalar_tensor_tensor(out=gnm_[:], in0=gmv_[:, 0:1], scalar=-1.0, in1=grs_[:], op0=ALU.mult, op1=ALU.mult),
               [gmv_, grs_], [gnm_])
            op("act", lambda e: e.activation(out=t_[:], in_=t_[:], func=AF.Identity, bias=gnm_[:, 0:1], scale=grs_[:, 0:1]), [t_, gnm_, grs_], [t_])
            op("pool", lambda e: e.tensor_tensor(out=og[:, h * 512:(h + 1) * 512], in0=t_[:], in1=g_[:], op=ALU.mult), [t_, g_], [og])

        k2 = 0
        pend2 = []
        prefetch_chunk(0)
        for ci, c in enumerate(chunks):
            sb, j = c // SBT, c % SBT
            if j == SBT - 1:
                trickle()
            if True:
                tok0 = c * 128
                qT1, kc1 = qT1s[ci % 2], kc1s[ci % 2]
                for h in range(8):
                    k2 += 1
                    v_ = vh2[k2 % 2]
                    g_ = gh2[k2 % 3]
                    o_ = o1b[k2 % 2]
                    t_ = ot[k2 % 2]
                    gst_, gmv_, grs_, gnm_ = gst[k2 % 2], gmv[k2 % 2], grs[k2 % 2], gnm[k2 % 2]
                    dma("sp", v_[:], r_v[tok0:tok0 + 128, h * 512:(h + 1) * 512], [dbuf(r_v)], [v_])
                    dma("sp", g_[:], r_g[tok0:tok0 + 128, h * 512:(h + 1) * 512], [dbuf(r_g)], [g_])
                    dma("sp", o_[:], r_o1[tok0:tok0 + 128, h * 512:(h + 1) * 512], [dbuf(r_o1)], [o_])
                    acc = psX[k2 % 2]

                    def f2(e, h=h, sb_=Stbb[h], acc=acc, qT1=qT1):
                        for dkc in range(2):
                            ins = e.matmul(acc[:], lhsT=qT1[:, dkc, h, :], rhs=sb_[:, dkc, :], start=(dkc == 0), stop=(dkc == 1))
                        return ins
                    op("pe", f2, [qT1, Stbb[h]], [acc])
                    vht_T[0] = v_
                    psU = psA if k2 % 2 else psB
                    state_update(St32b, Stbb, psU, h, kc1, v_[:], 8 + h)
                    op("dve", lambda e, t_=t_, o_=o_, acc=acc: e.tensor_tensor(out=t_[:], in0=acc[:], in1=o_[:], op=ALU.add), [acc, o_], [t_])
                    op("dve", lambda e, t_=t_, gst_=gst_: e.bn_stats(out=gst_[:], in_=t_[:]), [t_], [gst_])
                    op("dve", lambda e, gst_=gst_, gmv_=gmv_: e.bn_aggr(out=gmv_[:], in_=gst_[:]), [gst_], [gmv_])
                    op("dve", lambda e, gmv_=gmv_, grs_=grs_: e.tensor_scalar(out=grs_[:], in0=gmv_[:, 1:2], scalar1=LN_EPS, scalar2=None, op0=ALU.add), [gmv_], [grs_])
                    op("act", lambda e, grs_=grs_: e.activation(out=grs_[:], in_=grs_[:], func=AF.Sqrt), [grs_], [grs_])
                    while pend2:
                        pend2.pop(0)()
                    pend2.append(lambda h=h, t_=t_, g_=g_, grs_=grs_, gmv_=gmv_, gnm_=gnm_: gn_part2(h, t_, g_, grs_, gmv_, gnm_))
                    if h == 3 and ci + 1 < len(chunks):
                        prefetch_chunk(ci + 1)
                while pend2:
                    pend2.pop(0)()
                for q8 in range(4):
                    pst = psT[q8 % 2]

                    def f(e, q8=q8, pst=pst):
                        for i in range(8):
                            kk = q8 * 8 + i
                            ins = e.transpose(out=pst[:, i * 128:(i + 1) * 128], in_=og[:, kk * 128:(kk + 1) * 128], identity=identb[:])
                        return ins
                    op("pe", f, [og, identb], [pst])
                    evac(ogT[:, q8 * 8:(q8 + 1) * 8, j * 128:(j + 1) * 128], pst[:].rearrange("p (a t) -> p a t", t=128), [pst], [ogT])
            if j != 0:
                continue
            for cb in range(8):
                w = load_w(wout, cb * 256, 256, 32, WB[('ro', jr)])
                for j2 in range(SBT):
                    tok0 = (sb * SBT + j2) * 128
                    ps = psA[j2 % 2]

                    def f(e, ps=ps, w=w, j2=j2):
                        for kk in range(32):
                            ins = e.matmul(ps[:, 0:256], lhsT=ogT[:, kk, j2 * 128:(j2 + 1) * 128], rhs=w[:, kk, :], start=(kk == 0), stop=(kk == 31))
                        return ins
                    op("pe", f, [ogT, w], [ps])
                    st = xstg[xctr[0] % 2]
                    xctr[0] += 1
                    dma("sp", st[:, 0:256], xsrc[tok0:tok0 + 128, cb * 256:(cb + 1) * 256], [dbuf(xsrc)], [st])
                    op("dve", lambda e, ps=ps, st=st, j2=j2, cb=cb: e.scalar_tensor_tensor(out=rbuf[:, j2, cb * 256:(cb + 1) * 256], in0=st[:, 0:256], scalar=ALPHA,
                                                                                            in1=ps[:, 0:256], op0=ALU.mult, op1=ALU.add), [st, ps], [rbuf])
            for j2 in range(SBT):
                layer_norm_out(li, rbuf, j2, (sb * SBT + j2) * 128, xdst)

    ja = jr = 0
    for li, l in enumerate(layers):
        if l == "a":
            cast_w(li, ("ai", ja), a_win[ja], a_win_b[ja], D, A_IN, 512)
            cast_w(li, ("ao", ja), a_wout[ja], a_wout_b[ja], D, D, 512)
            ja += 1
        else:
            cast_w(li, ("ri", jr), r_win[jr], r_win_b[jr], D, R_IN, 512)
            cast_w(li, ("ro", jr), r_wout[jr], r_wout_b[jr], 2 * D, D, 256)
            jr += 1
    ja = jr = 0
    cur = x_in
    for li, l in enumerate(layers):
        dst = y_out if li == nL - 1 else xs[li % 2]
        cast_some(li, 10 ** 6)
        cur_layer[0] = li
        if l == "a":
            attention_layer(li, ja, cur, dst)
            ja += 1
        else:
            retention_layer(li, jr, cur, dst)
            jr += 1
        cur = dst

    sc.finish()
    sc.emit_all(nc, stack)
    stack.close()
    return nc, sc


def make_consts(S):
    cosa, sina = rope_tables(S, 128)
    cosr, sinr = rope_tables(S, 256)
    n = np.arange(128, dtype=np.float32)
    diff = n[None, :] - n[:, None]
    p1 = np.maximum(diff, 0).astype(np.float32)
    p2 = np.maximum(-diff, 0).astype(np.float32)
    posv = np.stack([n + 1, 128 - n, 127 - n, n], axis=1).astype(np.float32)
    prow = np.concatenate([np.tile((n + 1)[None, :], (128, 1)), np.tile((128 - n)[None, :], (128, 1))], axis=1).astype(np.float32)
    return dict(cosa=cosa, sina=sina, cosr=cosr, sinr=sinr, identf_d=np.eye(128, dtype=np.float32), p1_d=p1, p2_d=p2, posv_d=posv, prow_d=prow)


def run(x, attn_w_in, attn_q_gain, attn_k_gain, attn_w_out, ret_w_in, ret_decay_exp, ret_w_out, ln_gain, ln_bias, layers):
    B, S, _ = x.shape
    nc, sc = build_program(S, layers)
    consts = make_consts(S)
    n_a = max(1, sum(1 for l in layers if l == "a"))
    n_r = max(1, sum(1 for l in layers if l == "r"))
    nL = len(layers)
    shared = dict(a_win=np.ascontiguousarray(attn_w_in[:n_a]), a_wout=np.ascontiguousarray(attn_w_out[:n_a]),
                  a_qg=np.ascontiguousarray(attn_q_gain[:n_a]), a_kg=np.ascontiguousarray(attn_k_gain[:n_a]),
                  r_win=np.ascontiguousarray(ret_w_in[:n_r]), r_wout=np.ascontiguousarray(ret_w_out[:n_r]),
                  r_dec=np.ascontiguousarray(ret_decay_exp[:n_r].reshape(n_r, 16)),
                  ln_g=np.ascontiguousarray(ln_gain[:nL]), ln_b=np.ascontiguousarray(ln_bias[:nL]), **consts)
    in_maps = [dict(x=np.ascontiguousarray(x[b]), **shared) for b in range(B)]
    res = run_bass_kernel_spmd(nc, in_maps, core_ids=list(range(B)))
    return np.stack([res.results[b]["y"] for b in range(B)], axis=0)


def kernel(x, attn_w_in, attn_q_gain, attn_k_gain, attn_w_out, ret_w_in, ret_decay_exp, ret_w_out, ln_gain, ln_bias):
    out = run(np.asarray(x), np.asarray(attn_w_in), np.asarray(attn_q_gain), np.asarray(attn_k_gain), np.asarray(attn_w_out),
              np.asarray(ret_w_in), np.asarray(ret_decay_exp), np.asarray(ret_w_out), np.asarray(ln_gain), np.asarray(ln_bias),
              layers=["a", "r", "a", "r"])
    return out.astype(np.float32)
```
